# Optimizing a Trainium2 kernel written in Bass

```python
import jax, jax.numpy as jnp
from jax import lax
import numpy as np

D_MODEL = 1024
BATCH = 16
SEQ = 256
DEPTH = 4
DEC_BATCH = 4
DEC_SEQ = 2048
PAST_LEN = 512

GRID_W = 64
N_BRANCH = 4
BRANCH_W = D_MODEL // 2
N_MOD = 9
D_FF = ((8 * D_MODEL // 3 + 127) // 128) * 128
EPS = 1e-6
CONV_W = 4
CONV_PAD_LEFT = CONV_W // 2
LRU_W = BRANCH_W
LRU_BLOCKS = 8
LRU_BW = LRU_W // LRU_BLOCKS
LRU_C = 8.0
HEAD_DIM = 64
GQA_HEADS = BRANCH_W // HEAD_DIM
GQA_KV = GQA_HEADS // 4
ROPE_BASE = 10000.0
Q_BLOCK = 128
NA_HEADS = BRANCH_W // HEAD_DIM
NA_WIN_R = 8
NA_WIN_C = 16
DN_DK = 128
DN_DV = 128
DN_HEADS = BRANCH_W // DN_DV
DN_CHUNK = 64
IN_WIDTHS = (
    LRU_W, LRU_W,
    GQA_HEADS * HEAD_DIM, GQA_KV * HEAD_DIM, GQA_KV * HEAD_DIM,
    NA_HEADS * HEAD_DIM, NA_HEADS * HEAD_DIM, NA_HEADS * HEAD_DIM,
    DN_HEADS * DN_DK, DN_HEADS * DN_DK, DN_HEADS * DN_DV, DN_HEADS * DN_DV,
    2 * DN_HEADS, 2 * DN_HEADS,
    N_BRANCH * D_MODEL,
)
N_IN = sum(IN_WIDTHS)
DN_CONV_CH = 2 * DN_HEADS * DN_DK + DN_HEADS * DN_DV

kernel_name = 'hybrid_prefix_diffusion_trunk_step'


def rmsnorm(x, g):
    xf = x.astype(jnp.float32)
    y = xf * lax.rsqrt(jnp.mean(xf * xf, axis=-1, keepdims=True) + EPS)
    return (y * g.astype(jnp.float32)).astype(x.dtype)


def l2norm(x):
    return x * lax.rsqrt(jnp.sum(x * x, axis=-1, keepdims=True) + EPS)


def swiglu(h, w_gate, w_up, w_down):
    return (jax.nn.silu(h @ w_gate) * (h @ w_up)) @ w_down


def centred_dwconv(x, w):
    ch = x.shape[-1]
    return lax.conv_general_dilated(
        x, w[:, None, :].astype(x.dtype), window_strides=(1,),
        padding=[(CONV_PAD_LEFT, CONV_W - 1 - CONV_PAD_LEFT)],
        dimension_numbers=('NWC', 'WIO', 'NWC'), feature_group_count=ch)


def _lin_combine(left, right):
    return (left[0] * right[0], right[0] * left[1] + right[1])


def rglru_scan(x, w_r, b_r, w_i, b_i, lam, h0):
    f32 = jnp.float32
    b, t, w = x.shape
    xb = x.reshape(b, t, LRU_BLOCKS, LRU_BW)
    r = jax.nn.sigmoid(jnp.einsum('btnk,nkj->btnj', xb, w_r.astype(f32)).reshape(b, t, w) + b_r.astype(f32))
    i = jax.nn.sigmoid(jnp.einsum('btnk,nkj->btnj', xb, w_i.astype(f32)).reshape(b, t, w) + b_i.astype(f32))
    log_a = LRU_C * r * jax.nn.log_sigmoid(lam.astype(f32))
    a = jnp.exp(log_a)
    u = jnp.sqrt(-jnp.expm1(2.0 * log_a)) * (i * x)
    a_cum, u_cum = lax.associative_scan(_lin_combine, (a, u), axis=1)
    return a_cum * h0[:, None, :] + u_cum


def axial_angles(t):
    pos = jnp.arange(t)
    half = HEAD_DIM // 2
    inv = jnp.power(ROPE_BASE, -jnp.arange(0, half, 2, dtype=jnp.float32) / half)
    ang_r = (pos // GRID_W).astype(jnp.float32)[:, None] * inv[None, :]
    ang_c = (pos % GRID_W).astype(jnp.float32)[:, None] * inv[None, :]
    return ang_r, ang_c


def _rotate(x, ang):
    x1, x2 = jnp.split(x, 2, axis=-1)
    cos = jnp.cos(ang)[None, :, None, :]
    sin = jnp.sin(ang)[None, :, None, :]
    return jnp.concatenate([x1 * cos - x2 * sin, x2 * cos + x1 * sin], axis=-1)


def axial_rope(x, ang_r, ang_c):
    xf = x.astype(jnp.float32)
    half = x.shape[-1] // 2
    return jnp.concatenate([_rotate(xf[..., :half], ang_r), _rotate(xf[..., half:], ang_c)], axis=-1).astype(x.dtype)


def blocked_attention(q, k, v):
    b, t, h, hd = q.shape
    g = k.shape[2]
    rep = h // g
    nb = t // Q_BLOCK
    scale = hd ** -0.5
    qb = jnp.moveaxis(q.reshape(b, nb, Q_BLOCK, g, rep, hd), 1, 0)

    def one_block(qi):
        s = jnp.einsum('bqgrd,bkgd->bgrqk', qi, k, preferred_element_type=jnp.float32) * scale
        pr = jax.nn.softmax(s, axis=-1).astype(v.dtype)
        return jnp.einsum('bgrqk,bkgd->bqgrd', pr, v)

    o = lax.map(one_block, qb)
    return jnp.moveaxis(o, 0, 1).reshape(b, t, h * hd)


def neighbourhood_attention(q, k, v, k_ctx, v_ctx, rpb):
    b, t, h, hd = q.shape
    rows = t // GRID_W
    kr = min(NA_WIN_R, rows)
    kc = NA_WIN_C
    scale = hd ** -0.5
    qg = q.reshape(b, rows, GRID_W, h, hd)
    kg = k.reshape(b, rows, GRID_W, h, hd)
    vg = v.reshape(b, rows, GRID_W, h, hd)
    k_ctx = k_ctx.astype(q.dtype)
    v_ctx = v_ctx.astype(v.dtype)
    col = jnp.arange(GRID_W)
    col_idx = jnp.clip(col - kc // 2, 0, GRID_W - kc)[:, None] + jnp.arange(kc)[None, :]
    col_off = col_idx - col[:, None] + (NA_WIN_C - 1)
    rpb = rpb.astype(jnp.float32)

    def one_row(r):
        rs = jnp.clip(r - kr // 2, 0, rows - kr)
        k_rows = lax.dynamic_slice_in_dim(kg, rs, kr, axis=1)
        v_rows = lax.dynamic_slice_in_dim(vg, rs, kr, axis=1)
        k_nb = k_rows[:, :, col_idx]
        v_nb = v_rows[:, :, col_idx]
        q_r = lax.dynamic_index_in_dim(qg, r, axis=1, keepdims=False)
        row_off = rs + jnp.arange(kr) - r + (NA_WIN_R - 1)
        bias = rpb[:, row_off[None, :, None], col_off[:, None, :]]
        s_loc = jnp.einsum('bqhd,bjqchd->bhqjc', q_r, k_nb, preferred_element_type=jnp.float32) * scale + bias[None]
        s_ctx = jnp.einsum('bqhd,bkhd->bhqk', q_r, k_ctx, preferred_element_type=jnp.float32) * scale
        s = jnp.concatenate([s_loc.reshape(b, h, GRID_W, kr * kc), s_ctx], axis=-1)
        pr = jax.nn.softmax(s, axis=-1).astype(v.dtype)
        p_loc = pr[..., :kr * kc].reshape(b, h, GRID_W, kr, kc)
        p_ctx = pr[..., kr * kc:]
        return (jnp.einsum('bhqjc,bjqchd->bqhd', p_loc, v_nb)
                + jnp.einsum('bhqk,bkhd->bqhd', p_ctx, v_ctx))

    o = lax.map(one_row, jnp.arange(rows))
    return jnp.moveaxis(o, 0, 1).reshape(b, t, h * hd)


def gated_delta_chunked(q, k, v, beta, g, s0):
    b, t, h, dk = q.shape
    dv = v.shape[-1]
    n = t // DN_CHUNK

    def to_chunks(x):
        x = x.reshape((b, n, DN_CHUNK, h) + x.shape[3:])
        return jnp.moveaxis(x, (1, 3), (0, 2))

    qc, kc, vc = to_chunks(q), to_chunks(k), to_chunks(v)
    bc = to_chunks(beta)
    gc = jnp.cumsum(to_chunks(g), axis=-1)
    causal = jnp.tril(jnp.ones((DN_CHUNK, DN_CHUNK), dtype=bool))
    strict = jnp.tril(jnp.ones((DN_CHUNK, DN_CHUNK), dtype=bool), -1)
    decay = jnp.exp(jnp.where(causal, gc[..., :, None] - gc[..., None, :], -jnp.inf))
    kb = kc * bc[..., None]
    lower = jnp.where(strict, jnp.einsum('...id,...jd->...ij', kb, kc) * decay, 0.0)
    eye = jnp.eye(DN_CHUNK, dtype=jnp.float32)
    tmat = lax.linalg.triangular_solve(eye + lower, jnp.broadcast_to(eye, lower.shape),
                                       left_side=True, lower=True, unit_diagonal=True)
    u = tmat @ (vc * bc[..., None])
    w = tmat @ (kb * jnp.exp(gc)[..., None])
    intra = jnp.einsum('...id,...jd->...ij', qc, kc) * decay

    def step(s, xs):
        q_i, k_i, u_i, w_i, g_i, a_i = xs
        v_new = u_i - w_i @ s
        o_i = (q_i * jnp.exp(g_i)[..., None]) @ s + a_i @ v_new
        g_last = g_i[..., -1:]
        s = s * jnp.exp(g_last)[..., None] + jnp.einsum(
            'bhcd,bhce->bhde', k_i * jnp.exp(g_last - g_i)[..., None], v_new)
        return s, o_i

    s_fin, o = lax.scan(step, s0, (qc, kc, u, w, gc, intra))
    o = jnp.moveaxis(o, (0, 2), (1, 3)).reshape(b, t, h, dv)
    return o, s_fin


def mixer(hn, p, ctx):
    b, t, _ = hn.shape
    f32 = jnp.float32
    is_ctx = ctx is None
    split_at = np.cumsum(IN_WIDTHS)[:-1].tolist()
    (a_x, a_y, b_q, b_k, b_v, c_q, c_k, c_v,
     d_q, d_k, d_v, d_z, d_b, d_a, g_lin) = jnp.split(hn @ p['w_in'], split_at, axis=-1)

    xa = (centred_dwconv(a_x, p['lru_conv_w']) + p['lru_conv_b']).astype(f32)
    h0 = jnp.zeros((b, 2, LRU_W), f32) if is_ctx else ctx[4].astype(f32)
    lru_out, lru_fin = [], []
    for dr in range(2):
        xs = xa if dr == 0 else xa[:, ::-1]
        hs = rglru_scan(xs, p['lru_w_r'][dr], p['lru_b_r'][dr], p['lru_w_i'][dr], p['lru_b_i'][dr],
                        p['lru_lambda'][dr], h0[:, dr])
        lru_fin.append(hs[:, -1])
        lru_out.append(hs if dr == 0 else hs[:, ::-1])
    o_a = (lru_out[0] + lru_out[1]).astype(hn.dtype) * jax.nn.gelu(a_y)

    q_b = rmsnorm(b_q.reshape(b, t, GQA_HEADS, HEAD_DIM), p['gqa_q_norm'])
    k_b = rmsnorm(b_k.reshape(b, t, GQA_KV, HEAD_DIM), p['gqa_k_norm'])
    v_b = b_v.reshape(b, t, GQA_KV, HEAD_DIM)
    if is_ctx:
        o_b = blocked_attention(q_b, k_b, v_b)
    else:
        ang_r, ang_c = axial_angles(t)
        o_b = blocked_attention(axial_rope(q_b, ang_r, ang_c),
                                jnp.concatenate([axial_rope(k_b, ang_r, ang_c), ctx[0].astype(k_b.dtype)], axis=1),
                                jnp.concatenate([v_b, ctx[1].astype(v_b.dtype)], axis=1))

    q_c = c_q.reshape(b, t, NA_HEADS, HEAD_DIM)
    k_c = c_k.reshape(b, t, NA_HEADS, HEAD_DIM)
    v_c = c_v.reshape(b, t, NA_HEADS, HEAD_DIM)
    if is_ctx:
        o_c = blocked_attention(q_c, k_c, v_c)
    else:
        o_c = neighbourhood_attention(q_c, k_c, v_c, ctx[2], ctx[3], p['na_rpb'])

    qkv = jax.nn.silu(centred_dwconv(jnp.concatenate([d_q, d_k, d_v], axis=-1), p['dn_conv_w'])).astype(f32)
    q_d, k_d, v_d = jnp.split(qkv, [DN_HEADS * DN_DK, 2 * DN_HEADS * DN_DK], axis=-1)
    q_d = l2norm(q_d.reshape(b, t, DN_HEADS, DN_DK)) * (DN_DK ** -0.5)
    k_d = l2norm(k_d.reshape(b, t, DN_HEADS, DN_DK))
    v_d = v_d.reshape(b, t, DN_HEADS, DN_DV)
    beta_in = d_b.astype(f32).reshape(b, t, 2, DN_HEADS)
    dec_in = d_a.astype(f32).reshape(b, t, 2, DN_HEADS)
    s0 = jnp.zeros((b, 2, DN_HEADS, DN_DK, DN_DV), f32) if is_ctx else ctx[5].astype(f32)
    dn_out, dn_fin = [], []
    for dr in range(2):
        beta = jax.nn.sigmoid(beta_in[:, :, dr])
        g = -jnp.exp(p['dn_a_log'][dr].astype(f32)) * jax.nn.softplus(dec_in[:, :, dr] + p['dn_dt_bias'][dr].astype(f32))
        seq = (q_d, k_d, v_d, beta, g)
        if dr == 1:
            seq = tuple(z[:, ::-1] for z in seq)
        o_dr, s_dr = gated_delta_chunked(seq[0], seq[1], seq[2], seq[3], seq[4], s0[:, dr])
        dn_out.append(o_dr if dr == 0 else o_dr[:, ::-1])
        dn_fin.append(s_dr)
    o_dn = rmsnorm(dn_out[0] + dn_out[1], p['dn_norm_g']) * jax.nn.silu(d_z.astype(f32).reshape(b, t, DN_HEADS, DN_DV))
    o_d = o_dn.reshape(b, t, BRANCH_W).astype(hn.dtype)

    o = jnp.stack([o_a, o_b, o_c, o_d], axis=2)
    br = jnp.einsum('btnw,nwd->btnd', o, p['w_branch'])
    gate = jax.nn.sigmoid(g_lin.reshape(b, t, N_BRANCH, D_MODEL))
    y = jnp.sum(gate * br, axis=2) @ p['w_out']
    if is_ctx:
        new_ctx = (k_b, v_b, k_c, v_c, jnp.stack(lru_fin, axis=1), jnp.stack(dn_fin, axis=1))
    else:
        new_ctx = None
    return y, new_ctx


def trunk_layer(x, mod, p, ctx):
    m = [mod[:, i][:, None, :] for i in range(N_MOD)]
    h = rmsnorm(x, p['norm_g'][0]) * (1 + m[1]) + m[0]
    x = x + 0.5 * m[2] * swiglu(h, p['w_ffn_gate'][0], p['w_ffn_up'][0], p['w_ffn_down'][0])
    h = rmsnorm(x, p['norm_g'][1]) * (1 + m[4]) + m[3]
    y, new_ctx = mixer(h, p, ctx)
    x = x + m[5] * y
    h = rmsnorm(x, p['norm_g'][2]) * (1 + m[7]) + m[6]
    x = x + 0.5 * m[8] * swiglu(h, p['w_ffn_gate'][1], p['w_ffn_up'][1], p['w_ffn_down'][1])
    return x, new_ctx


def setup_inputs(seed: int = 0) -> dict:
    key = jax.random.key(seed)
    ks = list(jax.random.split(key, 40))
    f32 = jnp.float32

    def nrm(i, shape, s):
        return jax.random.normal(ks[i], shape, f32) * s

    u_lam = jax.random.uniform(ks[30], (DEPTH, 2, LRU_W), f32, 0.9, 0.999)
    dt = jnp.exp(jax.random.uniform(ks[31], (DEPTH, 2, DN_HEADS), f32, float(np.log(1e-3)), float(np.log(1e-1))))
    return {
        'x_prompt': nrm(0, (BATCH, SEQ, D_MODEL), 1.0),
        'x_sample': nrm(1, (DEC_BATCH, DEC_SEQ, D_MODEL), 1.0),
        'c': nrm(2, (DEC_BATCH, D_MODEL), 1.0),
        'cache_attn_k': nrm(3, (DEC_BATCH, DEPTH, PAST_LEN, GQA_KV, HEAD_DIM), 1.0),
        'cache_attn_v': nrm(4, (DEC_BATCH, DEPTH, PAST_LEN, GQA_KV, HEAD_DIM), 1.0),
        'cache_na_k': nrm(5, (DEC_BATCH, DEPTH, PAST_LEN, NA_HEADS, HEAD_DIM), 1.0),
        'cache_na_v': nrm(6, (DEC_BATCH, DEPTH, PAST_LEN, NA_HEADS, HEAD_DIM), 1.0),
        'state_lru': nrm(7, (DEC_BATCH, DEPTH, 2, LRU_W), 0.5),
        'state_delta': nrm(8, (DEC_BATCH, DEPTH, 2, DN_HEADS, DN_DK, DN_DV), DN_DK ** -0.5),
        'c_ctx': nrm(9, (D_MODEL,), 1.0),
        'w_mod': nrm(10, (DEPTH, D_MODEL, N_MOD * D_MODEL), 0.5 * D_MODEL ** -0.5),
        'b_mod': nrm(11, (DEPTH, N_MOD * D_MODEL), 0.02),
        'norm_g': 1.0 + nrm(12, (DEPTH, 3, D_MODEL), 0.02),
        'w_ffn_gate': nrm(13, (DEPTH, 2, D_MODEL, D_FF), D_MODEL ** -0.5),
        'w_ffn_up': nrm(14, (DEPTH, 2, D_MODEL, D_FF), D_MODEL ** -0.5),
        'w_ffn_down': nrm(15, (DEPTH, 2, D_FF, D_MODEL), D_FF ** -0.5),
        'w_in': nrm(16, (DEPTH, D_MODEL, N_IN), D_MODEL ** -0.5),
        'lru_conv_w': nrm(17, (DEPTH, CONV_W, LRU_W), CONV_W ** -0.5),
        'lru_conv_b': nrm(18, (DEPTH, LRU_W), 0.02),
        'lru_w_r': nrm(19, (DEPTH, 2, LRU_BLOCKS, LRU_BW, LRU_BW), LRU_BW ** -0.5),
        'lru_b_r': nrm(20, (DEPTH, 2, LRU_W), 0.02),
        'lru_w_i': nrm(21, (DEPTH, 2, LRU_BLOCKS, LRU_BW, LRU_BW), LRU_BW ** -0.5),
        'lru_b_i': nrm(22, (DEPTH, 2, LRU_W), 0.02),
        'lru_lambda': jnp.log(u_lam) - jnp.log1p(-u_lam),
        'gqa_q_norm': 1.0 + nrm(23, (DEPTH, HEAD_DIM), 0.02),
        'gqa_k_norm': 1.0 + nrm(24, (DEPTH, HEAD_DIM), 0.02),
        'na_rpb': nrm(25, (DEPTH, NA_HEADS, 2 * NA_WIN_R - 1, 2 * NA_WIN_C - 1), 0.1),
        'dn_conv_w': nrm(26, (DEPTH, CONV_W, DN_CONV_CH), CONV_W ** -0.5),
        'dn_a_log': jnp.log(jax.random.uniform(ks[27], (DEPTH, 2, DN_HEADS), f32, 1.0, 16.0)),
        'dn_dt_bias': dt + jnp.log(-jnp.expm1(-dt)),
        'dn_norm_g': 1.0 + nrm(28, (DEPTH, DN_DV), 0.02),
        'w_branch': nrm(29, (DEPTH, N_BRANCH, BRANCH_W, D_MODEL), BRANCH_W ** -0.5),
        'w_out': nrm(32, (DEPTH, D_MODEL, D_MODEL), D_MODEL ** -0.5),
        'final_norm_g': 1.0 + nrm(33, (D_MODEL,), 0.02),
    }


def reference(x_prompt, x_sample, c, cache_attn_k, cache_attn_v, cache_na_k, cache_na_v,
              state_lru, state_delta, c_ctx, w_mod, b_mod, norm_g, w_ffn_gate, w_ffn_up,
              w_ffn_down, w_in, lru_conv_w, lru_conv_b, lru_w_r, lru_b_r, lru_w_i, lru_b_i,
              lru_lambda, gqa_q_norm, gqa_k_norm, na_rpb, dn_conv_w, dn_a_log, dn_dt_bias,
              dn_norm_g, w_branch, w_out, final_norm_g):
    xp, xs = x_prompt, x_sample
    silu_c = jax.nn.silu(c)
    silu_ctx = jax.nn.silu(c_ctx)[None, :]
    new_ak, new_av, new_nk, new_nv, new_lru, new_dn = [], [], [], [], [], []
    for l in range(DEPTH):
        p = {
            'norm_g': norm_g[l], 'w_ffn_gate': w_ffn_gate[l], 'w_ffn_up': w_ffn_up[l],
            'w_ffn_down': w_ffn_down[l], 'w_in': w_in[l], 'lru_conv_w': lru_conv_w[l],
            'lru_conv_b': lru_conv_b[l], 'lru_w_r': lru_w_r[l], 'lru_b_r': lru_b_r[l],
            'lru_w_i': lru_w_i[l], 'lru_b_i': lru_b_i[l], 'lru_lambda': lru_lambda[l],
            'gqa_q_norm': gqa_q_norm[l], 'gqa_k_norm': gqa_k_norm[l], 'na_rpb': na_rpb[l],
            'dn_conv_w': dn_conv_w[l], 'dn_a_log': dn_a_log[l], 'dn_dt_bias': dn_dt_bias[l],
            'dn_norm_g': dn_norm_g[l], 'w_branch': w_branch[l], 'w_out': w_out[l],
        }
        mod_ctx = (silu_ctx @ w_mod[l] + b_mod[l]).reshape(1, N_MOD, D_MODEL)
        mod_lat = (silu_c @ w_mod[l] + b_mod[l]).reshape(-1, N_MOD, D_MODEL)
        xp, ctx_new = trunk_layer(xp, mod_ctx, p, None)
        new_ak.append(ctx_new[0])
        new_av.append(ctx_new[1])
        new_nk.append(ctx_new[2])
        new_nv.append(ctx_new[3])
        new_lru.append(ctx_new[4])
        new_dn.append(ctx_new[5])
        ctx_cached = (cache_attn_k[:, l], cache_attn_v[:, l], cache_na_k[:, l], cache_na_v[:, l],
                      state_lru[:, l], state_delta[:, l])
        xs, _ = trunk_layer(xs, mod_lat, p, ctx_cached)
    y_prompt = rmsnorm(xp, final_norm_g)
    y_sample = rmsnorm(xs, final_norm_g)
    return (y_prompt, y_sample, jnp.stack(new_ak, axis=1), jnp.stack(new_av, axis=1),
            jnp.stack(new_nk, axis=1), jnp.stack(new_nv, axis=1),
            jnp.stack(new_lru, axis=1), jnp.stack(new_dn, axis=1))
```

```python
import contextlib
import math
import numpy as np
import concourse.bass as bass
import concourse.mybir as mybir
from concourse.bass_utils import run_bass_kernel_spmd

F32 = mybir.dt.float32
BF16 = mybir.dt.bfloat16
AF = mybir.ActivationFunctionType
ALU = mybir.AluOpType
AX = mybir.AxisListType

SAME_ENGINE_SYNC = True

DEPTH = 4
D = 1024
KC = 8
DFF = 2816
JC = 22
NTOK = 2560
TT = 512
NTT = 5
NIN = 9488
EPS = 1e-6
SEQS = [(0, 256, True, 0), (256, 256, True, 1), (512, 2048, False, 0)]
OFF = {}
_o = 0
for _n, _w in [("ax", 512), ("ay", 512), ("bq", 512), ("bk", 128), ("bv", 128), ("cq", 512), ("ck", 512),
               ("cv", 512), ("dq", 512), ("dk", 512), ("dv", 512), ("dz", 512), ("db", 8), ("da", 8), ("g", 4096)]:
    OFF[_n] = _o
    _o += _w
NEG = -30000.0


class Tr:
    __slots__ = ("w", "r", "x")

    def __init__(self, fence=()):
        self.w = None
        self.r = list(fence)
        self.x = False


class Tile:
    def __init__(self, t, name, fence=()):
        self.t = t
        self.name = name
        self.fence = list(fence)
        self.tr = Tr(self.fence)
        self.sub = {}

    def __getitem__(self, k):
        return self.t[k]

    def s(self, key):
        tr = self.sub.get(key)
        if tr is None:
            tr = self.sub[key] = Tr(self.fence)
        return tr

    def all_ops(self):
        out = set()
        for tr in [self.tr] + list(self.sub.values()):
            if tr.w is not None:
                out.add(tr.w)
            out.update(tr.r)
        return out


class Op:
    __slots__ = ("eng", "fn", "deps", "dma", "sig", "sigidx", "dsem", "dval", "id")


def _trs(lst):
    out = []
    for x in lst:
        if x is None:
            continue
        if isinstance(x, Tile):
            out.append(x.tr)
        elif isinstance(x, Tr):
            out.append(x)
        else:
            out.extend(_trs(x))
    return out


class Prog:
    ENGS = ("pe", "act", "dve", "pool", "sp")

    def __init__(self, ndma_sems=8):
        self.nc = bass.Bass("TRN2", target_bir_lowering=False)
        self.es = contextlib.ExitStack()
        self.ops = []
        self.by_eng = {e: [] for e in self.ENGS}
        self.ndma = ndma_sems
        self.dma_count = {e: 0 for e in self.ENGS}

    def sb(self, name, shape, dtype):
        t = self.es.enter_context(self.nc.sbuf_tensor(name, list(shape), dtype))
        return Tile(t, name)

    def ps(self, name, shape, dtype):
        t = self.es.enter_context(self.nc.psum_tensor(name, list(shape), dtype))
        tl = Tile(t, name)
        tl.tr.x = True
        return tl

    def dram(self, name, shape, dtype, kind="Internal"):
        t = self.nc.dram_tensor(name, list(shape), dtype, kind=kind)
        return Tile(t, name)

    def _add(self, eng, fn, r, w, dma):
        op = Op()
        op.eng = eng
        op.fn = fn
        op.dma = dma
        op.sig = False
        op.sigidx = None
        op.dsem = None
        op.dval = None
        op.id = len(self.ops)
        deps = set()
        rt = _trs(r)
        wt = _trs(w)
        xr = [t for t in rt if t.x]
        if xr:
            rt = [t for t in rt if not t.x]
            wt = wt + [t for t in xr if t not in wt]
        for tr in rt:
            if tr.w is not None:
                deps.add(tr.w)
        for tr in wt:
            if tr.w is not None:
                deps.add(tr.w)
            deps.update(tr.r)
        deps.discard(op.id)
        last = {}
        red = set()
        for d in deps:
            p = self.ops[d]
            if p.dma:
                red.add(d)
            elif last.get(p.eng, -1) < d:
                last[p.eng] = d
        red.update(last.values())
        op.deps = red
        for tr in rt:
            tr.r.append(op.id)
        for tr in wt:
            tr.w = op.id
            tr.r = []
        self.ops.append(op)
        self.by_eng[eng].append(op)
        if dma:
            j = self.dma_count[eng]
            self.dma_count[eng] = j + 1
            op.dsem = (eng, j % self.ndma)
            op.dval = 16 * (j // self.ndma + 1)
        return op

    def pe(self, fn, r=(), w=()):
        return self._add("pe", fn, r, w, False)

    def act(self, fn, r=(), w=()):
        return self._add("act", fn, r, w, False)

    def dve(self, fn, r=(), w=()):
        return self._add("dve", fn, r, w, False)

    def pool(self, fn, r=(), w=()):
        return self._add("pool", fn, r, w, False)

    def dma(self, q, out, in_, r=(), w=(), **kw):
        return self._add(q, lambda e: e.dma_start(out=out, in_=in_, **kw), r, w, True)

    def finish(self):
        nc = self.nc
        ops = self.ops
        for op in ops:
            for d in op.deps:
                p = ops[d]
                if p.dma:
                    continue
                if p.eng == op.eng and (op.eng == "pe" or not SAME_ENGINE_SYNC) and not op.dma:
                    continue
                p.sig = True
        for e in self.ENGS:
            c = 0
            for op in self.by_eng[e]:
                if op.sig:
                    c += 1
                    op.sigidx = c
        es = self.es
        csem = {e: es.enter_context(nc.semaphore(f"c_{e}")) for e in self.ENGS}
        dsem = {}
        for e in self.ENGS:
            if self.dma_count[e]:
                for i in range(self.ndma):
                    dsem[(e, i)] = es.enter_context(nc.semaphore(f"d_{e}{i}"))
        block = es.enter_context(nc.Block())
        ndma = self.ndma
        nwaits = [0]

        def gen(ename):
            def body(eng):
                waited = {}

                def wait(sem, val):
                    if waited.get(sem, 0) >= val:
                        return
                    waited[sem] = val
                    eng.wait_ge(sem, val)
                    nwaits[0] += 1

                for op in self.by_eng[ename]:
                    for d in sorted(op.deps):
                        p = ops[d]
                        if p.dma:
                            wait(dsem[p.dsem], p.dval)
                        else:
                            if p.eng == ename and (ename == "pe" or not SAME_ENGINE_SYNC) and not op.dma:
                                continue
                            wait(csem[p.eng], p.sigidx)
                    if op.dma and op.dval > 16:
                        wait(dsem[op.dsem], op.dval - 16)
                    ins = op.fn(eng)
                    if op.dma:
                        ins.then_inc(dsem[op.dsem], 16)
                    elif op.sig:
                        ins.then_inc(csem[ename], 1)
                if self.dma_count[ename]:
                    n = self.dma_count[ename]
                    for i in range(min(ndma, n)):
                        cnt = (n - 1 - i) // ndma + 1
                        wait(dsem[(ename, i)], 16 * cnt)
            return body

        block.tensor(gen("pe"))
        block.scalar(gen("act"))
        block.vector(gen("dve"))
        block.gpsimd(gen("pool"))
        block.sync(gen("sp"))
        es.close()
        self.nwaits = nwaits[0]
        return nc


class Ring:
    def __init__(self, tiles):
        self.tiles = tiles
        self.i = 0

    def next(self):
        t = self.tiles[self.i % len(self.tiles)]
        self.i += 1
        return t


class Arena:
    def __init__(self, P, nwords):
        self.P = P
        self.t = P.es.enter_context(P.nc.sbuf_tensor("arena", [128, nwords], F32))
        self.nwords = nwords
        self.top = 0
        self.allocs = []
        self.peak = 0

    def mark(self):
        return self.top

    def release(self, m):
        self.top = m

    def alloc(self, name, fshape, dtype, parts=128):
        fshape = list(fshape)
        n = int(np.prod(fshape))
        nw = n if dtype == F32 else (n + 1) // 2
        nw = (nw + 7) // 8 * 8
        lo = self.top
        hi = lo + nw
        assert hi <= self.nwords, f"arena overflow {name} {hi}>{self.nwords}"
        self.top = hi
        self.peak = max(self.peak, hi)
        ap = self.t[0:parts, lo:hi]
        if dtype != F32:
            ap = ap.bitcast(dtype)
        ap = ap[:, 0:n]
        if len(fshape) == 2:
            ap = ap.rearrange("p (a b) -> p a b", a=fshape[0], b=fshape[1])
        elif len(fshape) == 3:
            ap = ap.rearrange("p (a b c) -> p a b c", a=fshape[0], b=fshape[1], c=fshape[2])
        elif len(fshape) == 4:
            ap = ap.rearrange("p (a b c d) -> p a b c d", a=fshape[0], b=fshape[1], c=fshape[2], d=fshape[3])
        fence = set()
        keep = []
        for (l2, h2, tl) in self.allocs:
            if l2 < hi and lo < h2:
                fence |= tl.all_ops()
                if lo <= l2 and h2 <= hi:
                    continue
            keep.append((l2, h2, tl))
        self.allocs = keep
        tile = Tile(ap, name, fence)
        self.allocs.append((lo, hi, tile))
        return tile

    def ring(self, name, fshape, dtype, n, parts=128):
        return Ring([self.alloc(f"{name}{i}", fshape, dtype, parts) for i in range(n)])


def build(cfg):
    nl = cfg.get("nl", DEPTH)
    LD = cfg.get("ld", DEPTH)
    dbg = cfg.get("dbg", ())
    do_mixer = cfg.get("mixer", True)
    do_ffn = cfg.get("ffn", True)
    branches = cfg.get("branches", "ABCD")
    P = Prog()
    nc = P.nc
    A = Arena(P, cfg.get("arena_words", 50000))

    def din(name, shape):
        return P.dram(name, shape, F32, kind="ExternalInput")

    def dout(name, shape):
        return P.dram(name, shape, F32, kind="ExternalOutput")

    xp_d = din("xp", [512, D])
    xs_d = din("xs", [2048, D])
    c2_d = din("c2", [2, D])
    cak_d = din("cak", [LD, 512, 2, 64])
    cav_d = din("cav", [LD, 512, 2, 64])
    cnk_d = din("cnk", [LD, 512, 8, 64])
    cnv_d = din("cnv", [LD, 512, 8, 64])
    slru_d = din("slru", [LD, 2, 512])
    sdn_d = din("sdn", [LD, 2, 4, 128, 128])
    w_mod = din("w_mod", [LD, D, 9 * D])
    b_mod = din("b_mod", [LD, 72, 128])
    norm_g = din("norm_g", [LD, 24, 128])
    w_g = din("w_ffn_gate", [LD, 2, D, DFF])
    w_u = din("w_ffn_up", [LD, 2, D, DFF])
    w_dn = din("w_ffn_down", [LD, 2, DFF, D])
    w_in = din("w_in", [LD, D, NIN])
    lru_cw = din("lru_conv_w", [LD, 16, 128])
    lru_cb = din("lru_conv_b", [LD, 4, 128])
    lru_wr = din("lru_w_r", [LD, 2, 8, 64, 64])
    lru_br = din("lru_b_r", [LD, 8, 128])
    lru_wi = din("lru_w_i", [LD, 2, 8, 64, 64])
    lru_bi = din("lru_b_i", [LD, 8, 128])
    lru_lam = din("lru_lambda", [LD, 8, 128])
    gqa_qn = din("gqa_q_norm", [LD, 64])
    gqa_kn = din("gqa_k_norm", [LD, 64])
    rpbT = din("rpbT", [LD, 14, 128, 8, 64])
    namask = din("namask", [128, 64])
    dn_cw = din("dn_conv_w", [LD, 48, 128])
    dn_alog = din("dn_a_log", [LD, 8])
    dn_dtb = din("dn_dt_bias", [LD, 8])
    dn_ng = din("dn_norm_g", [LD, 128])
    w_br = din("w_branch", [LD, 4, 512, D])
    w_out = din("w_out", [LD, D, D])
    fin_g = din("final_norm_g", [8, 128])
    rope_cs = din("rope_cs", [2048, 2, 64])
    dnm = din("dnmasks", [6, 128, 128])

    yp_d = dout("y_p", [512, D])
    ys_d = dout("y_s", [2048, D])
    nak_d = dout("new_ak", [2, LD, 256, 2, 64])
    nav_d = dout("new_av", [2, LD, 256, 2, 64])
    nnk_d = dout("new_nk", [2, LD, 256, 8, 64])
    nnv_d = dout("new_nv", [2, LD, 256, 8, 64])
    nlru_d = dout("new_lru", [2, LD, 2, 512])
    ndn_d = dout("new_dn", [2, LD, 2, 4, 128, 128])

    def scr(name, shape, dtype):
        return P.dram(name, shape, dtype, kind=("ExternalOutput" if name in dbg else "Internal"))

    xT = scr("xT", [D, NTOK], F32)
    actd = scr("actd", [DFF, NTOK], BF16)
    oT = scr("oT", [4, 512, NTOK], BF16)

    banks = [P.ps(f"bank{i}", [128, 512], F32) for i in range(8)]
    ident = P.sb("ident", [128, 128], F32)
    ones_bf = P.sb("ones_bf", [128, 128], BF16)
    ones_f = P.sb("ones_f", [128, 128], F32)
    mods = P.sb("mods", [128, DEPTH * 9 * 8 * 2], F32)
    ng_sb = P.sb("ng_sb", [128, DEPTH * 24], F32)
    fing_sb = P.sb("fing_sb", [128, 8], F32)
    silc = P.sb("silc", [128, 16], BF16)
    eff = P.sb("eff", [128, 2 * 8 * 2], F32)
    eff_ring = Ring([P.sb(f"effr{i}", [128, 8 * 2 * 2], F32) for i in range(4)])

    def modv(l, i, k, g):
        o = ((l * 9 + i) * 8 + k) * 2 + g
        return mods[:, o:o + 1]

    def mm(out, lhsT, rhs, start, stop, r, w, **kw):
        P.pe(lambda e: e.matmul(out, lhsT=lhsT, rhs=rhs, start=start, stop=stop, **kw), r=r, w=w)

    def tp(out, in_, idn, r, w):
        P.pe(lambda e: e.transpose(out=out, in_=in_, identity=idn), r=r, w=w)

    def actf(out, in_, func, r, w, bias=None, scale=None):
        kw = {}
        if bias is not None:
            kw["bias"] = bias
        if scale is not None:
            kw["scale"] = scale
        P.act(lambda e: e.activation(out=out, in_=in_, func=func, **kw), r=r, w=w)

    def tt(eng, out, in0, in1, op, r, w):
        getattr(P, eng)(lambda e: e.tensor_tensor(out=out, in0=in0, in1=in1, op=op), r=r, w=w)

    def ts(eng, out, in0, s1, s2, op0, op1, r, w):
        if s2 is None:
            getattr(P, eng)(lambda e: e.tensor_scalar(out=out, in0=in0, scalar1=s1, scalar2=None, op0=op0), r=r, w=w)
        else:
            getattr(P, eng)(lambda e: e.tensor_scalar(out=out, in0=in0, scalar1=s1, scalar2=s2, op0=op0, op1=op1), r=r, w=w)

    def stt(eng, out, in0, scalar, in1, op0, op1, r, w):
        getattr(P, eng)(lambda e: e.scalar_tensor_tensor(out=out, in0=in0, scalar=scalar, in1=in1, op0=op0, op1=op1), r=r, w=w)

    def cp(eng, out, in_, r, w):
        if eng == "act":
            P.act(lambda e: e.copy(out=out, in_=in_), r=r, w=w)
        else:
            getattr(P, eng)(lambda e: e.tensor_copy(out=out, in_=in_), r=r, w=w)

    def memset(eng, ap, val, w):
        getattr(P, eng)(lambda e: e.memset(ap, val), w=w)

    def pe_warm(n):
        for wi_ in range(n):
            bk_ = banks[wi_ % 8]
            mm(bk_[:, 0:512], ones_bf[:], h_all[:, 0, 0:512], True, True, [ones_bf], [bk_])

    def run_rr(gens):
        alive = list(gens)
        while alive:
            for g_ in list(alive):
                try:
                    next(g_)
                except StopIteration:
                    alive.remove(g_)

    memset("pool", ident[:], 0.0, [ident])
    P.pool(lambda e: e.affine_select(out=ident[:], in_=ident[:], pattern=[[-1, 128]], compare_op=ALU.not_equal,
                                     fill=1.0, base=0, channel_multiplier=1), r=[ident], w=[ident])
    memset("pool", ones_bf[:], 1.0, [ones_bf])
    memset("pool", ones_f[:], 1.0, [ones_f])

    stage_r = Ring([P.sb(f"vstage{i}", [128, 128], F32) for i in range(2)])
    m0 = A.mark()

    def load_rowsT(dst_ap, dst_tile, src_ap, src_tile, R, bank, C=128):
        st = stage_r.next()
        P.dma("sp", st[0:R, 0:C], src_ap, r=[src_tile], w=[st])
        tp(bank[0:C, 0:R], st[0:R, 0:C], ident[0:R, 0:R], [st, ident], [bank])
        cp("dve", dst_ap, bank[0:C, 0:R], [bank], [dst_tile])

    xin_r = A.ring("xin", [D], F32, 2)
    xst_r = A.ring("xst", [8, 128], F32, 2)
    for tc in range(20):
        src = xp_d[tc * 128:(tc + 1) * 128, :] if tc < 4 else xs_d[(tc - 4) * 128:(tc - 3) * 128, :]
        srct = xp_d if tc < 4 else xs_d
        xi = xin_r.next()
        P.dma("sp", xi[:], src, r=[srct], w=[xi])
        xs_ = xst_r.next()
        for half in range(2):
            bk = banks[(tc * 2 + half) % 4]
            for q in range(4):
                k = half * 4 + q
                tp(bk[:, q * 128:(q + 1) * 128], xi[:, k * 128:(k + 1) * 128], ident[:], [xi, ident], [bk])
            cp("dve" if half == 0 else "act", xs_[:, half * 4:(half + 1) * 4, :],
               bk[:].rearrange("p (a b) -> p a b", a=4, b=128), [bk], [xs_])
        P.dma("sp", xT.t.rearrange("(k p) t -> p k t", p=128)[:, :, tc * 128:(tc + 1) * 128], xs_[:],
              r=[xs_], w=[xT.s(tc // 4)])

    csb = A.alloc("csb", [16], F32)
    load_rowsT(csb[:], csb, c2_d.t.rearrange("g (k p) -> (g k) p", p=128), c2_d, 16, banks[4])
    actf(silc[:], csb[:], AF.Silu, [csb], [silc])
    for l in range(nl):
        load_rowsT(ng_sb[:, l * 24:(l + 1) * 24], ng_sb, norm_g[l], norm_g, 24, banks[5])
    load_rowsT(fing_sb[:], fing_sb, fin_g[:], fin_g, 8, banks[5])
    bm_sb = A.alloc("bm_sb", [72], F32)
    wm_r = A.ring("wm", [8, 1024], BF16, 2)
    for l in range(nl):
        load_rowsT(bm_sb[:], bm_sb, b_mod[l], b_mod, 72, banks[5])
        for i in range(9):
            wt = wm_r.next()
            P.dma("pool", wt[:], w_mod[l].rearrange("(kc p) n -> p kc n", p=128)[:, :, i * 1024:(i + 1) * 1024],
                  r=[w_mod], w=[wt])
            bk = banks[6 + (i % 2)]
            for m in range(8):
                for k in range(8):
                    mm(bk[:, m * 2:m * 2 + 2], wt[:, k, m * 128:(m + 1) * 128],
                       silc[:].rearrange("p (g k) -> p k g", g=2)[:, k, :], k == 0, k == 7, [wt, silc], [bk])
            o = (l * 9 + i) * 16
            for g in range(2):
                tt("dve", mods[:, o:o + 16].rearrange("p (k g) -> p k g", g=2)[:, :, g],
                   bk[:, 0:16].rearrange("p (k g) -> p k g", g=2)[:, :, g], bm_sb[:, i * 8:(i + 1) * 8], ALU.add,
                   [bk, bm_sb], [mods])
    A.release(m0)

    h_all = A.alloc("h_all", [KC, NTOK], BF16)
    m_base = A.mark()

    def norm_prep(l, i):
        e_t = eff_ring.next()
        o_s = (l * 9 + 3 * i + 1) * 16
        ts("dve", e_t[:, 0:16], mods[:, o_s:o_s + 16], 1.0, None, ALU.add, None, [mods], [e_t])
        for g in range(2):
            v = e_t[:, 0:16].rearrange("p (k g) -> p k g", g=2)[:, :, g]
            tt("dve", v, v, ng_sb[:, l * 24 + i * 8:l * 24 + i * 8 + 8], ALU.mult, [e_t, ng_sb], [e_t])
        return e_t

    def norm_rings():
        return (A.ring("n_sq", [KC, TT], BF16, 2), A.ring("n_rs", [TT], F32, 2), A.ring("n_tm", [TT], F32, 3))

    def norm_tile(l, i, e_t, t_, xt, rings):
        sq_r, rs_r, tm_r = rings
        g = 0 if t_ == 0 else 1
        cs = slice(t_ * TT, (t_ + 1) * TT)
        sq = sq_r.next()
        actf(sq[:], xt[:], AF.Square, [xt], [sq])
        bk = banks[t_ % 2]
        for k in range(KC):
            mm(bk[:], ones_bf[:], sq[:, k, :], k == 0, k == KC - 1, [ones_bf, sq], [bk])
        rs = rs_r.next()
        ts("dve", rs[:], bk[:], 1.0 / D, EPS, ALU.mult, ALU.add, [bk], [rs])
        actf(rs[:], rs[:], AF.Sqrt, [rs], [rs])
        P.dve(lambda e, rs=rs: e.reciprocal(out=rs[:], in_=rs[:]), r=[rs], w=[rs])
        for k in range(KC):
            tm = tm_r.next()
            tt("dve", tm[:], xt[:, k, :], rs[:], ALU.mult, [xt, rs], [tm])
            actf(h_all[:, k, cs], tm[:], AF.Identity, [tm, e_t, mods], [h_all.s(t_)],
                 bias=modv(l, 3 * i, k, g), scale=e_t[:, k * 2 + g:k * 2 + g + 1])

    def norm_stage(l, i):
        m = A.mark()
        e_t = norm_prep(l, i)
        xt_r = A.ring("n_xt", [KC, TT], F32, 2)
        rings = norm_rings()
        for t_ in range(NTT):
            cs = slice(t_ * TT, (t_ + 1) * TT)
            xt = xt_r.next()
            P.dma("sp", xt[:], xT.t.rearrange("(k p) t -> p k t", p=128)[:, :, cs], r=[xT.s(t_)], w=[xt])
            norm_tile(l, i, e_t, t_, xt, rings)
        A.release(m)

    def ffn_stage(l, f, gi, next_norm=None):
        m0_ = A.mark()
        wd = A.alloc("f_wd", [JC, D], BF16)
        for j0 in range(0, JC, 6):
            nj = min(6, JC - j0)
            P.dma("pool", wd[:, j0:j0 + nj, :], w_dn[l, f].rearrange("(j p) n -> p j n", p=128)[:, j0:j0 + nj, :],
                  r=[w_dn], w=[wd.s(j0)])
        wd_trs = [wd.s(j0) for j0 in range(0, JC, 6)]
        m = A.mark()
        e_t = eff_ring.next()
        o_g = (l * 9 + gi) * 16
        ts("dve", e_t[:, 16:32], mods[:, o_g:o_g + 16], 0.5, None, ALU.mult, None, [mods], [e_t])
        wg_r = A.ring("f_wg", [KC, 512], BF16, 2)
        wu_r = A.ring("f_wu", [KC, 512], BF16, 2)
        sg_r = A.ring("f_sg", [TT], F32, 2)
        ao_r = A.ring("f_ao", [TT], BF16, 4)
        bi = 0
        for j0 in range(0, JC, 4):
            nj = min(4, JC - j0)
            wg = wg_r.next()
            wu = wu_r.next()
            P.dma("pool", wg[:, :, 0:nj * 128], w_g[l, f].rearrange("(kc p) n -> p kc n", p=128)[:, :, j0 * 128:(j0 + nj) * 128],
                  r=[w_g], w=[wg])
            P.dma("pool", wu[:, :, 0:nj * 128], w_u[l, f].rearrange("(kc p) n -> p kc n", p=128)[:, :, j0 * 128:(j0 + nj) * 128],
                  r=[w_u], w=[wu])
            for t_ in range(NTT):
                cs = slice(t_ * TT, (t_ + 1) * TT)
                for jj in range(nj):
                    j = j0 + jj
                    pg = banks[(bi * 2) % 8]
                    pu = banks[(bi * 2 + 1) % 8]
                    bi += 1
                    for k in range(KC):
                        mm(pg[:], wg[:, k, jj * 128:(jj + 1) * 128], h_all[:, k, cs], k == 0, k == KC - 1,
                           [wg, h_all.s(t_)], [pg])
                    for k in range(KC):
                        mm(pu[:], wu[:, k, jj * 128:(jj + 1) * 128], h_all[:, k, cs], k == 0, k == KC - 1,
                           [wu, h_all.s(t_)], [pu])
                    sg = sg_r.next()
                    actf(sg[:], pg[:], AF.Silu, [pg], [sg])
                    ao = ao_r.next()
                    tt("dve", ao[:], sg[:], pu[:], ALU.mult, [sg, pu], [ao])
                    P.dma("sp", actd[j * 128:(j + 1) * 128, cs], ao[:], r=[ao], w=[actd.s((j, t_))])
        A.release(m)
        m = A.mark()
        ai_r = A.ring("f_ai", [JC, TT], BF16, 2)
        xt_r = A.ring("f_xt", [KC, TT], F32, 2)
        if next_norm is not None:
            ne_t = norm_prep(*next_norm)
            nrings = norm_rings()
        for t_ in range(NTT):
            g = 0 if t_ == 0 else 1
            cs = slice(t_ * TT, (t_ + 1) * TT)
            ai = ai_r.next()
            P.dma("sp", ai[:], actd.t.rearrange("(j p) t -> p j t", p=128)[:, :, cs],
                  r=[actd.s((j, t_)) for j in range(JC)], w=[ai])
            xt = xt_r.next()
            P.dma("sp", xt[:], xT.t.rearrange("(k p) t -> p k t", p=128)[:, :, cs], r=[xT.s(t_)], w=[xt])
            for mo in range(KC):
                bk = banks[2 + (t_ * KC + mo) % 6]
                for j in range(JC):
                    mm(bk[:], wd[:, j, mo * 128:(mo + 1) * 128], ai[:, j, :], j == 0, j == JC - 1,
                       [wd_trs, ai], [bk])
                stt("dve", xt[:, mo, :], bk[:], e_t[:, 16 + mo * 2 + g:16 + mo * 2 + g + 1], xt[:, mo, :],
                    ALU.mult, ALU.add, [bk, e_t, xt], [xt])
            P.dma("sp", xT.t.rearrange("(k p) t -> p k t", p=128)[:, :, cs], xt[:], r=[xt], w=[xT.s(t_)])
            if next_norm is not None:
                norm_tile(next_norm[0], next_norm[1], ne_t, t_, xt, nrings)
        A.release(m0_)

    def final_stage():
        m = A.mark()
        xt_r = A.ring("fn_xt", [KC, TT], F32, 2)
        sq_r = A.ring("fn_sq", [KC, TT], BF16, 2)
        rs_r = A.ring("fn_rs", [TT], F32, 2)
        yo_r = A.ring("fn_yo", [D], F32, 2)
        for t_ in range(NTT):
            cs = slice(t_ * TT, (t_ + 1) * TT)
            xt = xt_r.next()
            P.dma("sp", xt[:], xT.t.rearrange("(k p) t -> p k t", p=128)[:, :, cs], r=[xT.s(t_)], w=[xt])
            sq = sq_r.next()
            actf(sq[:], xt[:], AF.Square, [xt], [sq])
            bk = banks[t_ % 2]
            for k in range(KC):
                mm(bk[:], ones_bf[:], sq[:, k, :], k == 0, k == KC - 1, [ones_bf, sq], [bk])
            rs = rs_r.next()
            ts("dve", rs[:], bk[:], 1.0 / D, EPS, ALU.mult, ALU.add, [bk], [rs])
            actf(rs[:], rs[:], AF.Sqrt, [rs], [rs])
            P.dve(lambda e, rs=rs: e.reciprocal(out=rs[:], in_=rs[:]), r=[rs], w=[rs])
            for k in range(KC):
                stt("dve", xt[:, k, :], xt[:, k, :], fing_sb[:, k:k + 1], rs[:], ALU.mult, ALU.mult,
                    [xt, fing_sb, rs], [xt])
            for c4 in range(4):
                tc = t_ * 4 + c4
                yo = yo_r.next()
                for half in range(2):
                    bk2 = banks[2 + (tc * 2 + half) % 4]
                    for q in range(4):
                        k = half * 4 + q
                        tp(bk2[:, q * 128:(q + 1) * 128], xt[:, k, c4 * 128:(c4 + 1) * 128], ident[:], [xt, ident], [bk2])
                    cp("act" if half == 0 else "dve", yo[:, half * 512:(half + 1) * 512], bk2[:], [bk2], [yo])
                if tc < 4:
                    P.dma("sp", yp_d[tc * 128:(tc + 1) * 128, :], yo[:], r=[yo], w=[yp_d.s(tc)])
                else:
                    P.dma("sp", ys_d[(tc - 4) * 128:(tc - 3) * 128, :], yo[:], r=[yo], w=[ys_d.s(tc)])
        A.release(m)

    def w_in_ap(l, c0, c1):
        return w_in[l].rearrange("(kc p) n -> p kc n", p=128)[:, :, c0:c1]

    def zero_branch(n):
        m = A.mark()
        z = A.alloc("zb", [NTOK], BF16)
        memset("dve", z[:], 0.0, [z])
        for c in range(4):
            P.dma("sp", oT[n, c * 128:(c + 1) * 128, :], z[:], r=[z], w=[oT.s((n, t_)) for t_ in range(NTT)])
        A.release(m)

    def otr(n, lo, hi):
        return [oT.s((n, t_)) for t_ in range(lo // TT, (hi - 1) // TT + 1)]

    def merge_stage(l, next_norm=None):
        m = A.mark()
        mgall = A.alloc("mg_all", [KC, NTOK], BF16)
        m2 = A.mark()
        wg_r = A.ring("mg_wg", [KC, 4, 128], BF16, 2)
        wb_r = A.ring("mg_wb", [16, 128], BF16, 2)
        ot_r = A.ring("mg_ot", [16, TT], BF16, 2)
        sg_r = A.ring("mg_sg", [TT], F32, 3)
        ac_r = A.ring("mg_ac", [TT], F32, 2)
        bi = 0
        for mo in range(KC):
            wg = wg_r.next()
            for n in range(4):
                c0 = OFF["g"] + n * 1024 + mo * 128
                P.dma("pool", wg[:, :, n, :], w_in_ap(l, c0, c0 + 128), r=[w_in], w=[wg])
            wb = wb_r.next()
            P.dma("pool", wb[:], w_br[l].rearrange("n (kc p) d -> p (n kc) d", p=128)[:, :, mo * 128:(mo + 1) * 128],
                  r=[w_br], w=[wb])
            for t_ in range(NTT):
                cs = slice(t_ * TT, (t_ + 1) * TT)
                ot = ot_r.next()
                P.dma("sp", ot[:], oT.t.rearrange("n (kc p) t -> p (n kc) t", p=128)[:, :, cs],
                      r=[oT.s((n, t_)) for n in range(4)], w=[ot])
                ac = ac_r.next()
                for n in range(4):
                    pg = banks[(bi * 2) % 8]
                    pb = banks[(bi * 2 + 1) % 8]
                    bi += 1
                    for k in range(KC):
                        mm(pg[:], wg[:, k, n, :], h_all[:, k, cs], k == 0, k == KC - 1, [wg, h_all.s(t_)], [pg])
                    for kc in range(4):
                        mm(pb[:], wb[:, n * 4 + kc, :], ot[:, n * 4 + kc, :], kc == 0, kc == 3, [wb, ot], [pb])
                    sg = sg_r.next()
                    actf(sg[:], pg[:], AF.Sigmoid, [pg], [sg])
                    if n == 0:
                        tt("dve", ac[:], sg[:], pb[:], ALU.mult, [sg, pb], [ac])
                    else:
                        tt("dve", sg[:], sg[:], pb[:], ALU.mult, [sg, pb], [sg])
                        if n < 3:
                            tt("dve", ac[:], ac[:], sg[:], ALU.add, [ac, sg], [ac])
                        else:
                            tt("dve", mgall[:, mo, cs], ac[:], sg[:], ALU.add, [ac, sg], [mgall.s(t_)])
        A.release(m2)
        wo = A.alloc("mg_wo", [KC, D], BF16)
        P.dma("pool", wo[:], w_out[l].rearrange("(kc p) n -> p kc n", p=128), r=[w_out], w=[wo])
        xt_r = A.ring("mg_xt", [KC, TT], F32, 2)
        if next_norm is not None:
            ne_t = norm_prep(*next_norm)
            nrings = norm_rings()
        for t_ in range(NTT):
            g = 0 if t_ == 0 else 1
            cs = slice(t_ * TT, (t_ + 1) * TT)
            xt = xt_r.next()
            P.dma("sp", xt[:], xT.t.rearrange("(k p) t -> p k t", p=128)[:, :, cs], r=[xT.s(t_)], w=[xt])
            for mo in range(KC):
                bk = banks[2 + (t_ * KC + mo) % 6]
                for k in range(KC):
                    mm(bk[:], wo[:, k, mo * 128:(mo + 1) * 128], mgall[:, k, cs], k == 0, k == KC - 1,
                       [wo, mgall.s(t_)], [bk])
                stt("dve", xt[:, mo, :], bk[:], modv(l, 5, mo, g), xt[:, mo, :], ALU.mult, ALU.add,
                    [bk, mods, xt], [xt])
            P.dma("sp", xT.t.rearrange("(k p) t -> p k t", p=128)[:, :, cs], xt[:], r=[xt], w=[xT.s(t_)])
            if next_norm is not None:
                norm_tile(next_norm[0], next_norm[1], ne_t, t_, xt, nrings)
        A.release(m)

    PADW = 2569
    PBASE = [0, 259, 518]

    def padcol(tok):
        if tok < 256:
            return PBASE[0] + 2 + tok
        if tok < 512:
            return PBASE[1] + 2 + tok - 256
        return PBASE[2] + 2 + tok - 512

    def evac_padded(dst, dst_tr, bank, t_, eng):
        if t_ == 0:
            cp(eng, dst[:, padcol(0):padcol(0) + 256], bank[:, 0:256], [bank], [dst_tr])
            cp(eng, dst[:, padcol(256):padcol(256) + 256], bank[:, 256:512], [bank], [dst_tr])
        else:
            c = padcol(t_ * TT)
            cp(eng, dst[:, c:c + TT], bank[:, :], [bank], [dst_tr])

    def zero_pads(dst, dst_tr):
        for (t0, T, _, _), pb in zip(SEQS, PBASE):
            memset("pool", dst[:, pb:pb + 2], 0.0, [dst_tr])
            memset("pool", dst[:, pb + 2 + T:pb + 3 + T], 0.0, [dst_tr])

    def conv4(dst_fn, src, src_tr, wcol_fn, bias_ap, r_extra, w_tr):
        for si, ((t0, T, _, _), pb) in enumerate(zip(SEQS, PBASE)):
            d = dst_fn(si)
            if bias_ap is not None:
                actf(d, src[:, pb:pb + T], AF.Identity, [src_tr] + r_extra, [w_tr], bias=bias_ap, scale=wcol_fn(0))
            else:
                actf(d, src[:, pb:pb + T], AF.Identity, [src_tr] + r_extra, [w_tr], scale=wcol_fn(0))
            for j in range(1, 4):
                stt("dve", d, src[:, pb + j:pb + j + T], wcol_fn(j), d, ALU.mult, ALU.add, [src_tr, w_tr] + r_extra, [w_tr])

    def branch_A(l):
        m = A.mark()
        pr = A.alloc("a_par", [16 + 4 + 8 + 8 + 8 + 8 + 8], F32)
        load_rowsT(pr[:, 0:16], pr, lru_cw[l], lru_cw, 16, banks[7])
        load_rowsT(pr[:, 16:20], pr, lru_cb[l], lru_cb, 4, banks[7])
        load_rowsT(pr[:, 20:28], pr, lru_br[l], lru_br, 8, banks[7])
        load_rowsT(pr[:, 28:36], pr, lru_bi[l], lru_bi, 8, banks[7])
        load_rowsT(pr[:, 36:44], pr, lru_lam[l], lru_lam, 8, banks[7])
        load_rowsT(pr[:, 52:60], pr, slru_d[l].rearrange("d (c p) -> (d c) p", p=128), slru_d, 8, banks[7])
        actf(pr[:, 36:44], pr[:, 36:44], AF.Sigmoid, [pr], [pr])
        actf(pr[:, 36:44], pr[:, 36:44], AF.Ln, [pr], [pr])
        ts("dve", pr[:, 44:52], pr[:, 36:44], 16.0, None, ALU.mult, None, [pr], [pr])
        ts("dve", pr[:, 36:44], pr[:, 36:44], 8.0, None, ALU.mult, None, [pr], [pr])
        wxy_r = A.ring("a_w", [KC, 256], BF16, 2)
        wbd_r = A.ring("a_wbd", [4, 128], BF16, 2)
        axp = A.alloc("a_axp", [PADW], F32)
        zero_pads(axp, axp.tr)
        xa = A.alloc("a_xa", [NTOK], F32)
        xab = A.alloc("a_xab", [NTOK], BF16)
        gy = A.alloc("a_gy", [NTOK], F32)
        hsum = A.alloc("a_hs", [NTOK], F32)
        hdir = A.alloc("a_hd", [NTOK], F32)
        aa = A.alloc("a_aa", [NTOK], F32)
        uu = A.alloc("a_uu", [NTOK], F32)
        tmp_r = A.ring("a_tmp", [TT], F32, 3)
        ob_r = A.ring("a_ob", [NTOK], BF16, 1)
        fin_r = A.ring("a_fin", [2], F32, 2)
        for cch in range(4):
            wxy = wxy_r.next()
            P.dma("pool", wxy[:, :, 0:128], w_in_ap(l, OFF["ax"] + cch * 128, OFF["ax"] + (cch + 1) * 128), r=[w_in], w=[wxy])
            P.dma("pool", wxy[:, :, 128:256], w_in_ap(l, OFF["ay"] + cch * 128, OFF["ay"] + (cch + 1) * 128), r=[w_in], w=[wxy])
            wbd = wbd_r.next()
            memset("dve", wbd[:], 0.0, [wbd])
            for dr in range(2):
                for gi, wsrc in enumerate((lru_wr, lru_wi)):
                    for b2 in range(2):
                        P.dma("pool", wbd[b2 * 64:(b2 + 1) * 64, dr * 2 + gi, b2 * 64:(b2 + 1) * 64],
                              wsrc[l, dr, 2 * cch + b2], r=[wsrc], w=[wbd])
            for t_ in range(NTT):
                cs = slice(t_ * TT, (t_ + 1) * TT)
                px = banks[(t_ * 2) % 4]
                py = banks[(t_ * 2 + 1) % 4]
                for k in range(KC):
                    mm(px[:], wxy[:, k, 0:128], h_all[:, k, cs], k == 0, k == KC - 1, [wxy, h_all.s(t_)], [px])
                for k in range(KC):
                    mm(py[:], wxy[:, k, 128:256], h_all[:, k, cs], k == 0, k == KC - 1, [wxy, h_all.s(t_)], [py])
                evac_padded(axp, axp.tr, px, t_, "act")
                t1 = tmp_r.next()
                actf(t1[:], py[:], AF.Square, [py], [t1])
                ts("dve", t1[:], t1[:], 0.044715, 1.0, ALU.mult, ALU.add, [t1], [t1])
                tt("dve", t1[:], t1[:], py[:], ALU.mult, [t1, py], [t1])
                actf(t1[:], t1[:], AF.Sigmoid, [t1], [t1], scale=1.5957691216057308)
                tt("dve", gy[:, cs], t1[:], py[:], ALU.mult, [t1, py], [gy.s(t_)])
            conv4(lambda si: xa[:, SEQS[si][0]:SEQS[si][0] + SEQS[si][1]], axp, axp.tr,
                  lambda j: pr[:, j * 4 + cch:j * 4 + cch + 1], pr[:, 16 + cch:17 + cch], [pr], xa.tr)
            cp("act", xab[:], xa[:], [xa], [xab])
            for dr in range(2):
                ccol = pr[:, 36 + dr * 4 + cch:37 + dr * 4 + cch]
                c2col = pr[:, 44 + dr * 4 + cch:45 + dr * 4 + cch]
                for t_ in range(NTT):
                    cs = slice(t_ * TT, (t_ + 1) * TT)
                    pr_ = banks[4 + (t_ * 2) % 4]
                    pi_ = banks[4 + (t_ * 2 + 1) % 4]
                    mm(pr_[:], wbd[:, dr * 2 + 0, :], xab[:, cs], True, True, [wbd, xab], [pr_])
                    mm(pi_[:], wbd[:, dr * 2 + 1, :], xab[:, cs], True, True, [wbd, xab], [pi_])
                    rr = tmp_r.next()
                    actf(rr[:], pr_[:], AF.Sigmoid, [pr_, pr], [rr], bias=pr[:, 20 + dr * 4 + cch:21 + dr * 4 + cch])
                    ii = tmp_r.next()
                    actf(ii[:], pi_[:], AF.Sigmoid, [pi_, pr], [ii], bias=pr[:, 28 + dr * 4 + cch:29 + dr * 4 + cch])
                    actf(aa[:, cs], rr[:], AF.Exp, [rr, pr], [aa.s(t_)], scale=ccol)
                    actf(rr[:], rr[:], AF.Exp, [rr, pr], [rr], scale=c2col)
                    ts("dve", rr[:], rr[:], -1.0, 1.0, ALU.mult, ALU.add, [rr], [rr])
                    actf(rr[:], rr[:], AF.Sqrt, [rr], [rr])
                    tt("dve", ii[:], ii[:], xa[:, cs], ALU.mult, [ii, xa], [ii])
                    tt("dve", uu[:, cs], ii[:], rr[:], ALU.mult, [ii, rr], [uu.s(t_)])
                aatr = [aa.s(t_) for t_ in range(NTT)]
                uutr = [uu.s(t_) for t_ in range(NTT)]
                dst = hsum if dr == 0 else hdir
                for si, (t0, T, isctx, sidx) in enumerate(SEQS):
                    sl = slice(t0, t0 + T)
                    init = 0.0 if isctx else pr[:, 52 + dr * 4 + cch:53 + dr * 4 + cch]
                    if dr == 0:
                        P.dve(lambda e, sl=sl, init=init, dst=dst: e.tensor_tensor_scan(
                            out=dst[:, sl], data0=aa[:, sl], data1=uu[:, sl], initial=init, op0=ALU.mult, op1=ALU.add),
                            r=aatr + uutr + [pr], w=[dst])
                        fcol = t0 + T - 1
                    else:
                        P.dve(lambda e, t0=t0, T=T, init=init, dst=dst: e.tensor_tensor_scan(
                            out=dst[:, t0:t0 + T][:, ::-1], data0=aa[:, t0:t0 + T][:, ::-1], data1=uu[:, t0:t0 + T][:, ::-1],
                            initial=init, op0=ALU.mult, op1=ALU.add), r=aatr + uutr + [pr], w=[dst])
                        fcol = t0
                    if isctx:
                        P.dma("sp", nlru_d[sidx, l, dr, cch * 128:(cch + 1) * 128].rearrange("(p o) -> p o", o=1),
                              dst[:, fcol:fcol + 1], r=[dst], w=[nlru_d.s((sidx, l, dr, cch))])
            tt("dve", hsum[:], hsum[:], hdir[:], ALU.add, [hsum, hdir], [hsum])
            ob = ob_r.next()
            tt("dve", ob[:], hsum[:], gy[:], ALU.mult, [hsum, gy] + [gy.s(t_) for t_ in range(NTT)], [ob])
            P.dma("sp", oT[0, cch * 128:(cch + 1) * 128, :], ob[:], r=[ob], w=[oT.s((0, t_)) for t_ in range(NTT)])
        A.release(m)

    def attn_dense(qT, kT, V, nq_heads, kv_of, qcols, kchunks, out_n, out_head0, qtr, ktr, vtr):
        q0, QT = qcols
        m = A.mark()
        pt_r = A.ring("at_pt", [QT], BF16, 5)
        dr_r = A.ring("at_dr", [QT], F32, 2)
        rc_r = A.ring("at_rc", [QT], F32, 2, parts=64)
        ob_r = A.ring("at_ob", [QT], BF16, 2, parts=64)
        cnt = 0
        for h0 in range(0, nq_heads, 4):
            hs = list(range(h0, min(h0 + 4, nq_heads)))
            for wi_ in range(cfg.get("nwarm", 12)):
                mm(banks[4 + wi_ % 4][:, 0:512], ones_bf[:], h_all[:, 0, 0:512], True, True, [ones_bf], [banks[4 + wi_ % 4]])
            items = [(ki, kc, r_, h) for ki, kc in enumerate(kchunks) for r_, h in enumerate(hs)]
            pts = {}
            LA = 3
            for i in range(len(items) + LA):
                if i < len(items):
                    ki, kc, r_, h = items[i]
                    g = kv_of(h)
                    sb_ = banks[4 + cnt % 4]
                    cnt += 1
                    mm(sb_[:, 0:QT], kT[0:64, g, kc * 128:(kc + 1) * 128], qT[0:64, h, q0:q0 + QT], True, True,
                       [ktr, qtr], [sb_])
                    pt = pt_r.next()
                    actf(pt[:], sb_[:, 0:QT], AF.Exp, [sb_], [pt])
                    pts[i] = pt
                if i >= LA:
                    ki, kc, r_, h = items[i - LA]
                    g = kv_of(h)
                    pt = pts.pop(i - LA)
                    mm(banks[r_][0:65, 0:QT], V[:, kc, g, :], pt[:], ki == 0, ki == len(kchunks) - 1, [vtr, pt], [banks[r_]])
            for r_, h in enumerate(hs):
                po = banks[r_]
                dr = dr_r.next()
                cp("act", dr[64:65, :], po[64:65, 0:QT], [po], [dr])
                db_ = banks[4 + r_ % 2]
                mm(db_[0:64, 0:QT], ones_f[64:65, 0:64], dr[64:65, :], True, True, [ones_f, dr], [db_])
                rc = rc_r.next()
                P.dve(lambda e, rc=rc, QT=QT, db_=db_: e.reciprocal(out=rc[:], in_=db_[0:64, 0:QT]), r=[db_], w=[rc])
                ob = ob_r.next()
                tt("dve", ob[:], po[0:64, 0:QT], rc[:], ALU.mult, [po, rc], [ob])
                row0 = (out_head0 + h) * 64
                P.dma("sp", oT[out_n, row0:row0 + 64, q0:q0 + QT], ob[:], r=[ob], w=otr(out_n, q0, q0 + QT))
        A.release(m)

    def transpose_heads(dst, dst_tr, src_ap_fn, nheads, col0, r, scale=None):
        for h0 in range(0, nheads, 4):
            n = min(4, nheads - h0)
            bk = banks[6 + (h0 // 4) % 2]
            for i in range(n):
                tp(bk[0:64, i * 128:(i + 1) * 128], src_ap_fn(h0 + i), ident[:], r + [ident], [bk])
            src = bk[0:64, 0:n * 128].rearrange("p (a b) -> p a b", a=n, b=128)
            if scale is None:
                cp("act", dst[0:64, h0:h0 + n, col0:col0 + 128], src, [bk], [dst_tr])
            else:
                P.act(lambda e, src=src, h0=h0, n=n: e.mul(out=dst[0:64, h0:h0 + n, col0:col0 + 128], in_=src, mul=scale),
                      r=[bk], w=[dst_tr])

    def branch_B(l):
        m = A.mark()
        wb = A.alloc("b_w", [KC, 768], BF16)
        P.dma("pool", wb[:], w_in_ap(l, OFF["bq"], OFF["bq"] + 768), r=[w_in], w=[wb])
        gq = A.alloc("b_gq", [64], F32)
        gk = A.alloc("b_gk", [64], F32)
        P.dma("sp", gq[:], gqa_qn[l].partition_broadcast(128), r=[gqa_qn], w=[gq])
        P.dma("sp", gk[:], gqa_kn[l].partition_broadcast(128), r=[gqa_kn], w=[gk])
        qT = A.alloc("b_qT", [8, NTOK], BF16, parts=64)
        kT = A.alloc("b_kT", [2, NTOK + 512], BF16, parts=64)
        V = A.alloc("b_V", [24, 2, 65], BF16)
        memset("dve", V[:], 1.0, [V])
        m2 = A.mark()
        sq_r = A.ring("b_sq", [640], F32, 2)
        ss_r = A.ring("b_ss", [10], F32, 2)
        qk_r = A.ring("b_qk", [10, 64], F32, 2)
        rt_r = A.ring("b_rt", [10, 64], F32, 2)
        vv_r = A.ring("b_vv", [128], F32, 2)
        cs_r = A.ring("b_cs", [2, 64], F32, 2)
        def b_chunk(tc):
            tsl = slice(tc * 128, (tc + 1) * 128)
            t_ = tc // 4
            p0 = banks[(tc * 2) % 4]
            p1 = banks[(tc * 2 + 1) % 4]
            for k in range(KC):
                mm(p0[:], h_all[:, k, tsl], wb[:, k, 0:512], k == 0, k == KC - 1, [wb, h_all.s(t_)], [p0])
            for k in range(KC):
                mm(p1[:, 0:256], h_all[:, k, tsl], wb[:, k, 512:768], k == 0, k == KC - 1, [wb, h_all.s(t_)], [p1])
            yield
            sq = sq_r.next()
            actf(sq[:, 0:512], p0[:], AF.Square, [p0], [sq])
            actf(sq[:, 512:640], p1[:, 0:128], AF.Square, [p1], [sq])
            yield
            ss = ss_r.next()
            P.dve(lambda e, ss=ss, sq=sq: e.tensor_reduce(out=ss[:], in_=sq[:].rearrange("p (h d) -> p h d", d=64),
                                                          axis=AX.X, op=ALU.add), r=[sq], w=[ss])
            ts("dve", ss[:], ss[:], 1.0 / 64, EPS, ALU.mult, ALU.add, [ss], [ss])
            actf(ss[:], ss[:], AF.Sqrt, [ss], [ss])
            yield
            P.dve(lambda e, ss=ss: e.reciprocal(out=ss[:], in_=ss[:]), r=[ss], w=[ss])
            qk = qk_r.next()
            tt("dve", qk[:, 0:8, :], p0[:].rearrange("p (h d) -> p h d", d=64),
               ss[:, 0:8].unsqueeze(2).broadcast_to([128, 8, 64]), ALU.mult, [p0, ss], [qk])
            tt("dve", qk[:, 8:10, :], p1[:, 0:128].rearrange("p (h d) -> p h d", d=64),
               ss[:, 8:10].unsqueeze(2).broadcast_to([128, 2, 64]), ALU.mult, [p1, ss], [qk])
            tt("dve", qk[:, 0:8, :], qk[:, 0:8, :], gq[:].unsqueeze(1).broadcast_to([128, 8, 64]), ALU.mult, [qk, gq], [qk])
            tt("dve", qk[:, 8:10, :], qk[:, 8:10, :], gk[:].unsqueeze(1).broadcast_to([128, 2, 64]), ALU.mult, [qk, gk], [qk])
            vv = vv_r.next()
            cp("act", vv[:], p1[:, 128:256], [p1], [vv])
            cp("dve", V[:, tc, :, 0:64], vv[:].rearrange("p (g d) -> p g d", d=64), [vv], [V])
            yield
            if tc < 4:
                sidx = tc // 2
                r0 = (tc % 2) * 128
                P.dma("sp", nak_d[sidx, l, r0:r0 + 128].rearrange("t g d -> t (g d)"), qk[:, 8:10, :].rearrange("p g d -> p (g d)"),
                      r=[qk], w=[nak_d.s((tc, l))])
                P.dma("sp", nav_d[sidx, l, r0:r0 + 128].rearrange("t g d -> t (g d)"), vv[:], r=[vv], w=[nav_d.s((tc, l))])
                src = qk
            else:
                cst = cs_r.next()
                P.dma("sp", cst[:], rope_cs[(tc - 4) * 128:(tc - 3) * 128], r=[rope_cs], w=[cst])
                rt = rt_r.next()
                sw = qk[:].rearrange("p h (a b c) -> p h a b c", a=2, b=2, c=16)[:, :, :, ::-1, :]
                tt("dve", rt[:].rearrange("p h (a b c) -> p h a b c", a=2, b=2, c=16), sw,
                   cst[:, 1, :].rearrange("p (a b c) -> p a b c", a=2, b=2, c=16).unsqueeze(1).broadcast_to([128, 10, 2, 2, 16]),
                   ALU.mult, [qk, cst], [rt])
                tt("dve", qk[:], qk[:], cst[:, 0, :].unsqueeze(1).broadcast_to([128, 10, 64]), ALU.mult, [qk, cst], [qk])
                tt("dve", rt[:], rt[:], qk[:], ALU.add, [rt, qk], [rt])
                src = rt
            yield
            transpose_heads(qT, qT.tr, lambda h, src=src: src[:, h, :], 8, tc * 128, [src], scale=0.125)
            transpose_heads(kT, kT.tr, lambda h, src=src: src[:, 8 + h, :], 2, tc * 128, [src])
        for tc0 in range(0, 20, 2):
            run_rr([b_chunk(tc0), b_chunk(tc0 + 1)])
        ck_r = A.ring("b_ck", [2, 64], F32, 2)
        for i in range(4):
            ck = ck_r.next()
            P.dma("sp", ck[:], cak_d[l, i * 128:(i + 1) * 128], r=[cak_d], w=[ck])
            transpose_heads(kT, kT.tr, lambda h, ck=ck: ck[:, h, :], 2, NTOK + i * 128, [ck])
            cv = ck_r.next()
            P.dma("sp", cv[:], cav_d[l, i * 128:(i + 1) * 128], r=[cav_d], w=[cv])
            cp("dve", V[:, 20 + i, :, 0:64], cv[:], [cv], [V])
        A.release(m2)
        for (t0, T, isctx, sidx) in SEQS:
            if isctx:
                attn_dense(qT, kT, V, 8, lambda h: h // 4, (t0, T), [t0 // 128, t0 // 128 + 1], 1, 0, qT.tr, kT.tr, V.tr)
            else:
                for q0 in range(t0, t0 + T, 512):
                    attn_dense(qT, kT, V, 8, lambda h: h // 4, (q0, 512), list(range(4, 24)), 1, 0, qT.tr, kT.tr, V.tr)
        A.release(m)

    def branch_C(l):
        m = A.mark()
        nmask = A.alloc("c_mask", [64], F32)
        P.dma("sp", nmask[:], namask[:], r=[namask], w=[nmask])
        for hg in range(2):
            mh = A.mark()
            wc = A.alloc("c_w", [KC, 768], BF16)
            for i, nm in enumerate(("cq", "ck", "cv")):
                c0 = OFF[nm] + hg * 256
                P.dma("pool", wc[:, :, i * 256:(i + 1) * 256], w_in_ap(l, c0, c0 + 256), r=[w_in], w=[wc])
            qT = A.alloc("c_qT", [4, NTOK], BF16, parts=64)
            kT = A.alloc("c_kT", [4, NTOK + 512], BF16, parts=64)
            V = A.alloc("c_V", [24, 4, 65], BF16)
            memset("dve", V[:], 1.0, [V])
            bias = A.alloc("c_bias", [14, 4, 64], F32)
            for a_ in range(14):
                P.dma("sp", bias[:, a_, :, :], rpbT[l, a_, :, hg * 4:(hg + 1) * 4, :], r=[rpbT], w=[bias])
            tt("dve", bias[:].rearrange("p a h q -> p (a h) q"), bias[:].rearrange("p a h q -> p (a h) q"),
               nmask[:].unsqueeze(1).broadcast_to([128, 56, 64]), ALU.add, [bias, nmask], [bias])
            m2 = A.mark()
            qk_r = A.ring("c_qk", [768], F32, 2)
            def c_chunk(tc):
                tsl = slice(tc * 128, (tc + 1) * 128)
                t_ = tc // 4
                p0 = banks[(tc * 2) % 4]
                p1 = banks[(tc * 2 + 1) % 4]
                for k in range(KC):
                    mm(p0[:], h_all[:, k, tsl], wc[:, k, 0:512], k == 0, k == KC - 1, [wc, h_all.s(t_)], [p0])
                for k in range(KC):
                    mm(p1[:, 0:256], h_all[:, k, tsl], wc[:, k, 512:768], k == 0, k == KC - 1, [wc, h_all.s(t_)], [p1])
                yield
                qk = qk_r.next()
                cp("act", qk[:, 0:512], p0[:], [p0], [qk])
                cp("dve", qk[:, 512:768], p1[:, 0:256], [p1], [qk])
                cp("dve", V[:, tc, :, 0:64], qk[:, 512:768].rearrange("p (g d) -> p g d", d=64), [qk], [V])
                yield
                if tc < 4:
                    sidx = tc // 2
                    r0 = (tc % 2) * 128
                    P.dma("sp", nnk_d[sidx, l, r0:r0 + 128, hg * 4:(hg + 1) * 4, :].rearrange("t g d -> t (g d)"), qk[:, 256:512],
                          r=[qk], w=[nnk_d.s((tc, l, hg))])
                    P.dma("sp", nnv_d[sidx, l, r0:r0 + 128, hg * 4:(hg + 1) * 4, :].rearrange("t g d -> t (g d)"), qk[:, 512:768],
                          r=[qk], w=[nnv_d.s((tc, l, hg))])
                transpose_heads(qT, qT.tr, lambda h, qk=qk: qk[:, h * 64:(h + 1) * 64], 4, tc * 128, [qk], scale=0.125)
                transpose_heads(kT, kT.tr, lambda h, qk=qk: qk[:, 256 + h * 64:256 + (h + 1) * 64], 4, tc * 128, [qk])
            for tc0 in range(0, 20, 2):
                run_rr([c_chunk(tc0), c_chunk(tc0 + 1)])
            ck_r = A.ring("c_ck", [4, 64], F32, 2)
            for i in range(4):
                ck = ck_r.next()
                P.dma("sp", ck[:], cnk_d[l, i * 128:(i + 1) * 128, hg * 4:(hg + 1) * 4, :], r=[cnk_d], w=[ck])
                transpose_heads(kT, kT.tr, lambda h, ck=ck: ck[:, h, :], 4, NTOK + i * 128, [ck])
                cv = ck_r.next()
                P.dma("sp", cv[:], cnv_d[l, i * 128:(i + 1) * 128, hg * 4:(hg + 1) * 4, :], r=[cnv_d], w=[cv])
                cp("dve", V[:, 20 + i, :, 0:64], cv[:], [cv], [V])
            A.release(m2)
            for (t0, T, isctx, sidx) in SEQS:
                if isctx:
                    attn_dense(qT, kT, V, 4, lambda h: h, (t0, T), [t0 // 128, t0 // 128 + 1], 2, hg * 4, qT.tr, kT.tr, V.tr)
            m3 = A.mark()
            sc_r = A.ring("n_sc", [256], F32, 3)
            pt_r = A.ring("n_pt", [256], BF16, 4)
            dr_r = A.ring("n_dr", [256], F32, 2)
            rc_r = A.ring("n_rc", [256], F32, 2, parts=64)
            ob_r = A.ring("n_ob", [4, 512], BF16, 2, parts=64)
            ob = None
            cnt = 0
            work = []
            for r in range(32):
                rs = min(max(r - 4, 0), 24)
                units = []
                e = (rs // 2) * 2
                while e <= rs + 7:
                    units.append((4 + e // 2, e >= rs, e + 1 <= rs + 7, e - r + 7))
                    e += 2
                for i in range(4):
                    units.append((20 + i, True, True, None))
                for ui, u in enumerate(units):
                    work.append((r, ui, len(units), u))
            LA = 2
            pts = {}
            for wi in range(len(work) + LA):
                if wi < len(work):
                    r, ui, nu, (kc, lo, hi, d0) = work[wi]
                    q0 = 512 + r * 64
                    if ui == 0 and r % cfg.get("na_warm_every", 4) == 0:
                        pe_warm(cfg.get("nwarm_na", 0))
                    sb_ = banks[2 + cnt % 4]
                    cnt += 1
                    for h in range(4):
                        mm(sb_[:, h * 64:(h + 1) * 64], kT[0:64, h, kc * 128:(kc + 1) * 128], qT[0:64, h, q0:q0 + 64], True, True,
                           [kT, qT], [sb_])
                    pt = pt_r.next()
                    if d0 is not None:
                        sc = sc_r.next()
                        tt("dve", sc[:], sb_[:, 0:256], bias[:, d0, :, :].rearrange("p h q -> p (h q)"), ALU.add, [sb_, bias], [sc])
                        actf(pt[:], sc[:], AF.Exp, [sc], [pt])
                    else:
                        actf(pt[:], sb_[:, 0:256], AF.Exp, [sb_], [pt])
                    if not lo:
                        memset("dve", pt[0:64, :], 0.0, [pt])
                    if not hi:
                        memset("dve", pt[64:128, :], 0.0, [pt])
                    pts[wi] = pt
                if wi >= LA:
                    r, ui, nu, (kc, lo, hi, d0) = work[wi - LA]
                    pt = pts.pop(wi - LA)
                    po = banks[r % 2]
                    for h in range(4):
                        mm(po[0:65, h * 64:(h + 1) * 64], V[:, kc, h, :], pt[:, h * 64:(h + 1) * 64], ui == 0 and h == 0,
                           ui == nu - 1, [V, pt], [po], skip_group_check=True)
                    if ui == nu - 1:
                        dr = dr_r.next()
                        cp("act", dr[64:65, :], po[64:65, 0:256], [po], [dr])
                        mm(banks[6][0:64, 0:256], ones_f[64:65, 0:64], dr[64:65, :], True, True, [ones_f, dr], [banks[6]])
                        rc = rc_r.next()
                        P.dve(lambda e, rc=rc: e.reciprocal(out=rc[:], in_=banks[6][0:64, 0:256]), r=[banks[6]], w=[rc])
                        if r % 8 == 0:
                            ob = ob_r.next()
                        tt("dve", ob[:, :, (r % 8) * 64:(r % 8 + 1) * 64], po[0:64, 0:256].rearrange("p (h q) -> p h q", q=64),
                           rc[:].rearrange("p (h q) -> p h q", q=64), ALU.mult, [po, rc], [ob])
                        if r % 8 == 7:
                            c0 = 512 + (r - 7) * 64
                            P.dma("sp", oT[2, hg * 256:(hg + 1) * 256, c0:c0 + 512].rearrange("(h d) t -> d h t", d=64), ob[:],
                                  r=[ob], w=otr(2, c0, c0 + 512))
            A.release(m3)
            A.release(mh)
        A.release(m)

    def pair_copy(dst, dst_tr, src, r, parts=128, engs=("act", "dve")):
        for (t0, T, _, _) in SEQS:
            n = T // 64
            d = dst[0:parts, 2 * t0:2 * t0 + 2 * T].rearrange("p (n a c) -> p n a c", n=n, a=2, c=64)
            s_ = src[0:parts, t0:t0 + T].rearrange("p (n c) -> p n c", c=64)
            cp(engs[0], d[:, :, 0, :], s_, r, [dst_tr])
            cp(engs[1], d[:, :, 1, :], s_[:, ::-1, :], r, [dst_tr])

    STEP0 = [0, 4, 8]

    def branch_D(l):
        m = A.mark()
        msk = A.alloc("d_msk", [6, 128], F32)
        P.dma("sp", msk[:], dnm.t.rearrange("a p q -> p a q"), r=[dnm], w=[msk])
        CM, NEGCM, POSSMT, BMk, HMf, HMb = (msk[:, i, :] for i in range(6))
        pr = A.alloc("d_par", [52], F32)
        load_rowsT(pr[:, 0:48], pr, dn_cw[l], dn_cw, 48, banks[7])
        load_rowsT(pr[:, 48:49], pr, dn_ng[l:l + 1, :], dn_ng, 1, banks[7])
        BG = A.alloc("d_BG", [40, 10], F32)
        memset("dve", BG[:], 0.0, [BG])
        m0_ = A.mark()
        p8 = A.alloc("d_p8", [4], F32)
        load_rowsT(p8[0:8, 0:1], p8, dn_alog[l:l + 1, :], dn_alog, 1, banks[7], C=8)
        load_rowsT(p8[0:8, 1:2], p8, dn_dtb[l:l + 1, :], dn_dtb, 1, banks[7], C=8)
        actf(p8[0:8, 2:3], p8[0:8, 0:1], AF.Exp, [p8], [p8])
        ts("dve", p8[0:8, 2:3], p8[0:8, 2:3], -1.0, None, ALU.mult, None, [p8], [p8])
        wd = A.alloc("d_wbg", [KC, 16], BF16)
        P.dma("pool", wd[:], w_in_ap(l, OFF["db"], OFF["db"] + 16), r=[w_in], w=[wd])
        bet = A.alloc("d_bet", [NTOK], F32)
        gg = A.alloc("d_gg", [NTOK], F32)
        for t_ in range(NTT):
            cs = slice(t_ * TT, (t_ + 1) * TT)
            pb = banks[(t_ * 2) % 4]
            pa = banks[(t_ * 2 + 1) % 4]
            for k in range(KC):
                mm(pb[0:8, :], wd[:, k, 0:8], h_all[:, k, cs], k == 0, k == KC - 1, [wd, h_all.s(t_)], [pb])
            for k in range(KC):
                mm(pa[0:8, :], wd[:, k, 8:16], h_all[:, k, cs], k == 0, k == KC - 1, [wd, h_all.s(t_)], [pa])
            actf(bet[0:8, cs], pb[0:8, :], AF.Sigmoid, [pb], [bet])
            actf(gg[0:8, cs], pa[0:8, :], AF.Exp, [pa, p8], [gg], bias=p8[0:8, 1:2])
            actf(gg[0:8, cs], gg[0:8, cs], AF.Ln, [gg], [gg], bias=1.0)
            ts("dve", gg[0:8, cs], gg[0:8, cs], p8[0:8, 2:3], None, ALU.mult, None, [gg, p8], [gg])
        betP = A.alloc("d_betP", [2 * NTOK], F32)
        ggP = A.alloc("d_ggP", [2 * NTOK], F32)
        pair_copy(betP, betP.tr, bet, [bet], parts=8)
        pair_copy(ggP, ggP.tr, gg, [gg], parts=8)
        for s8 in range(5):
            bk = banks[4 + s8 % 2]
            for i in range(8):
                sg = s8 * 8 + i
                tp(bk[:, i * 16:i * 16 + 8], betP[0:8, sg * 128:(sg + 1) * 128], ident[0:8, 0:8], [betP, ident], [bk])
                tp(bk[:, i * 16 + 8:i * 16 + 16], ggP[0:8, sg * 128:(sg + 1) * 128], ident[0:8, 0:8], [ggP, ident], [bk])
            v = bk[:, 0:128].rearrange("p (s c) -> p s c", c=16)
            d = BG[:, s8 * 8:(s8 + 1) * 8, 0:8]
            cp("act", d[0:64, :, 0:4], v[0:64, :, 0:4], [bk], [BG])
            cp("dve", d[64:128, :, 0:4], v[64:128, :, 4:8], [bk], [BG])
            cp("act", d[0:64, :, 4:8], v[0:64, :, 8:12], [bk], [BG])
            cp("dve", d[64:128, :, 4:8], v[64:128, :, 12:16], [bk], [BG])
        A.release(m0_)
        if cfg.get("d_stop") == 0:
            A.release(m)
            return
        for hd in range(cfg.get("d_heads", 4)):
            mh = A.mark()
            qP = A.alloc("d_qP", [2 * NTOK], F32)
            kP = A.alloc("d_kP", [2 * NTOK], F32)
            vP = A.alloc("d_vP", [2 * NTOK], F32)
            OTp = A.alloc("d_OTp", [2 * NTOK], F32)
            raw = A.alloc("d_raw", [PADW], F32)
            zero_pads(raw, raw.tr)
            cv = A.alloc("d_cv", [NTOK], F32)
            tq_r = A.ring("d_tq", [TT], F32, 2)
            m_w = A.mark()
            w4 = A.alloc("d_w4", [KC, 4, 128], BF16)
            for i, nm in enumerate(("dq", "dk", "dv", "dz")):
                c0 = OFF[nm] + hd * 128
                P.dma("pool", w4[:, :, i, :], w_in_ap(l, c0, c0 + 128), r=[w_in], w=[w4])
            for i, dstP in enumerate((qP, kP, vP)):
                for t_ in range(NTT):
                    cs = slice(t_ * TT, (t_ + 1) * TT)
                    px = banks[t_ % 4]
                    for k in range(KC):
                        mm(px[:], w4[:, k, i, :], h_all[:, k, cs], k == 0, k == KC - 1, [w4, h_all.s(t_)], [px])
                    evac_padded(raw, raw.tr, px, t_, "act")
                c12 = i * 4 + hd
                conv4(lambda si: cv[:, SEQS[si][0]:SEQS[si][0] + SEQS[si][1]], raw, raw.tr,
                      lambda j, c12=c12: pr[:, j * 12 + c12:j * 12 + c12 + 1], None, [pr], cv.tr)
                actf(cv[:], cv[:], AF.Silu, [cv], [cv])
                if i < 2:
                    for t_ in range(NTT):
                        cs = slice(t_ * TT, (t_ + 1) * TT)
                        tq = tq_r.next()
                        actf(tq[:], cv[:, cs], AF.Square, [cv], [tq])
                        bk = banks[4 + t_ % 2]
                        mm(bk[:], ones_f[:], tq[:], True, True, [ones_f, tq], [bk])
                        ts("dve", tq[:], bk[:], EPS, None, ALU.add, None, [bk], [tq])
                        actf(tq[:], tq[:], AF.Sqrt, [tq], [tq])
                        P.dve(lambda e, tq=tq: e.reciprocal(out=tq[:], in_=tq[:]), r=[tq], w=[tq])
                        if i == 0:
                            stt("dve", cv[:, cs], cv[:, cs], 128.0 ** -0.5, tq[:], ALU.mult, ALU.mult, [cv, tq], [cv])
                        else:
                            tt("dve", cv[:, cs], cv[:, cs], tq[:], ALU.mult, [cv, tq], [cv])
                pair_copy(dstP, dstP.tr, cv, [cv])
            for t_ in range(NTT):
                cs = slice(t_ * TT, (t_ + 1) * TT)
                px = banks[t_ % 4]
                for k in range(KC):
                    mm(px[:], w4[:, k, 3, :], h_all[:, k, cs], k == 0, k == KC - 1, [w4, h_all.s(t_)], [px])
                actf(cv[:, cs], px[:], AF.Silu, [px], [cv])
            zs = cv
            A.release(m_w)
            if cfg.get("d_stop") == 1:
                A.release(mh)
                continue
            G = cfg.get("d_G", 4)
            Sf = A.alloc("d_Sf", [128], F32)
            Sb = A.alloc("d_Sb", [128], F32)
            slots = []
            for gslot in range(G):
                sl = {"tmp": A.ring(f"d_tmp{gslot}_", [128], F32, 4)}
                for nm_, n_ in (("N", 1), ("MT", 2), ("M", 2), ("TT", 2), ("u", 1), ("At", 1), ("vb", 1), ("kbg", 1),
                                ("wTf", 1), ("wTb", 1), ("qgf", 1), ("qgb", 1), ("kdf", 1), ("kdb", 1)):
                    sl[nm_] = A.ring(f"d_{nm_}{gslot}_", [128], F32, n_)
                for nm_ in ("wTf", "wTb", "qgf", "qgb", "kdf", "kdb"):
                    for tl in sl[nm_].tiles:
                        memset("pool", tl[:], 0.0, [tl])
                sl["gcs"] = A.ring(f"d_gcs{gslot}_", [8], F32, 1)
                sl["nb"] = A.ring(f"d_nb{gslot}_", [2], F32, 1)
                slots.append(sl)
            vn_r = A.ring("d_vnew", [128], F32, 2)
            bc = [0]

            def nb_():
                b = banks[bc[0] % 8]
                bc[0] += 1
                return b

            def pre_gen(sl, ctx, si, s, ci):
                bX, bY = banks[2 * ci], banks[2 * ci + 1]
                t0, T, isctx, sidx = SEQS[si]
                sg = STEP0[si] + s
                pc = slice(2 * t0 + s * 128, 2 * t0 + (s + 1) * 128)
                kS, qS, vS = kP[:, pc], qP[:, pc], vP[:, pc]
                bcol = BG[:, sg, hd:hd + 1]
                gcol = BG[:, sg, 4 + hd:5 + hd]
                gcol2 = BG[:, sg, 4 + hd:6 + hd]
                tmp = sl["tmp"]
                Gb = tmp.next()
                ts("pool", Gb[:], ones_f[:], gcol, None, ALU.mult, None, [ones_f, BG], [Gb])
                nb = sl["nb"].next()
                ts("pool", nb[:, 0:1], bcol, -1.0, None, ALU.mult, None, [BG], [nb])
                mm(bX[:, 0:128], kS, kS, True, True, [kP], [bX])
                mm(bX[:, 128:256], kS, qS, True, True, [kP, qP], [bX])
                tp(bX[:, 256:384], kS, ident[:], [kP, ident], [bX])
                tp(bX[:, 384:512], vS, ident[:], [vP, ident], [bX])
                yield
                mm(bY[:, 0:128], Gb[:], CM, True, True, [Gb, msk], [bY])
                mm(bY[:, 128:130], CM, gcol2, True, True, [msk, BG], [bY])
                mm(bY[:, 130:132], HMf, gcol2, True, True, [msk, BG], [bY])
                mm(bY[:, 132:134], HMb, gcol2, True, True, [msk, BG], [bY])
                mm(bY[:, 134:136], BMk, gcol2, True, True, [msk, BG], [bY])
                vb = sl["vb"].next()
                ts("dve", vb[:], bX[:, 384:512], bcol, None, ALU.mult, None, [bX, BG], [vb])
                yield
                gcs = sl["gcs"].next()
                cp("dve", gcs[:, 0:4], bY[:, 128:136].rearrange("p (a b) -> p a b", b=2)[:, :, 0], [bY], [gcs])
                t1 = tmp.next()
                stt("dve", t1[:], bY[:, 0:128], gcs[:, 0:1], NEGCM, ALU.subtract, ALU.add, [bY, gcs, msk], [t1])
                t2 = tmp.next()
                stt("dve", t2[:], bY[:, 0:128], gcs[:, 0:1], POSSMT, ALU.subtract, ALU.add, [bY, gcs, msk], [t2])
                ER = tmp.next()
                actf(ER[:], bY[:, 0:128], AF.Exp, [bY], [ER])
                yield
                actf(gcs[:, 4:7], gcs[:, 0:3], AF.Exp, [gcs], [gcs])
                actf(gcs[:, 7:8], gcs[:, 0:1], AF.Exp, [gcs], [gcs], bias=gcs[:, 3:4], scale=-1.0)
                actf(t1[:], t1[:], AF.Exp, [t1], [t1])
                actf(t2[:], t2[:], AF.Exp, [t2], [t2], scale=-1.0)
                Dt, Ds = t1, t2
                qgf = sl["qgf"].next()
                qgb = sl["qgb"].next()
                tt("pool", qgf[:, 0:64], qP[:, pc][:, 0:64], ER[:, 0:64], ALU.mult, [qP, ER], [qgf])
                tt("pool", qgb[:, 64:128], qP[:, pc][:, 64:128], ER[:, 64:128], ALU.mult, [qP, ER], [qgb])
                yield
                tt("pool", nb[:, 1:2], bcol, gcs[:, 4:5], ALU.mult, [BG, gcs], [nb])
                Nm = sl["N"].next()
                stt("dve", Nm[:], bX[:, 0:128], nb[:, 0:1], Ds[:], ALU.mult, ALU.mult, [bX, nb, Ds], [Nm])
                At = sl["At"].next()
                tt("dve", At[:], bX[:, 128:256], Dt[:], ALU.mult, [bX, Dt], [At])
                kdf = sl["kdf"].next()
                kdb = sl["kdb"].next()
                ts("dve", kdf[0:64, :], bX[0:64, 256:384], gcs[0:64, 7:8], None, ALU.mult, None, [bX, gcs], [kdf])
                ts("dve", kdb[64:128, :], bX[64:128, 256:384], gcs[64:128, 7:8], None, ALU.mult, None, [bX, gcs], [kdb])
                yield
                tp(bY[:, 0:128], Nm[:], ident[:], [Nm, ident], [bY])
                kbg = sl["kbg"].next()
                ts("dve", kbg[:], bX[:, 256:384], nb[:, 1:2], None, ALU.mult, None, [bX, nb], [kbg])
                yield
                MT = sl["MT"].next()
                cp("act", MT[:], bY[:, 0:128], [bY], [MT])
                TTt = sl["TT"].next()
                tt("dve", TTt[:], bY[:, 0:128], ident[:], ALU.add, [bY, ident], [TTt])
                yield
                M_ = Nm
                IMp = None
                for kk in range(1, 6):
                    mm(bX[:, 0:128], MT[:], M_[:], True, True, [MT, M_], [bX])
                    if kk < 5:
                        mm(bY[:, 0:128], M_[:], MT[:], True, True, [MT, M_], [bY])
                    if IMp is not None:
                        mm(bX[:, 128:256], IMp[:], TTt[:], True, True, [IMp, TTt], [bX])
                    yield
                    IM = tmp.next()
                    tt("dve", IM[:], bX[:, 0:128], ident[:], ALU.add, [bX, ident], [IM])
                    if kk < 5:
                        Mn = sl["M"].next()
                        cp("act", Mn[:], bX[:, 0:128], [bX], [Mn])
                        MTn = sl["MT"].next()
                        cp("act", MTn[:], bY[:, 0:128], [bY], [MTn])
                    if IMp is not None:
                        TTn = sl["TT"].next()
                        cp("act", TTn[:], bX[:, 128:256], [bX], [TTn])
                        TTt = TTn
                    IMp = IM
                    if kk < 5:
                        M_, MT = Mn, MTn
                    yield
                mm(bX[:, 128:256], IMp[:], TTt[:], True, True, [IMp, TTt], [bX])
                yield
                TTn = sl["TT"].next()
                cp("act", TTn[:], bX[:, 128:256], [bX], [TTn])
                TTt = TTn
                yield
                mm(bX[:, 0:128], TTt[:], vb[:], True, True, [TTt, vb], [bX])
                mm(bY[:, 0:128], kbg[:], TTt[:], True, True, [kbg, TTt], [bY])
                yield
                usb = sl["u"].next()
                cp("act", usb[:], bX[:, 0:128], [bX], [usb])
                wTf = sl["wTf"].next()
                wTb = sl["wTb"].next()
                cp("dve", wTf[:, 0:64], bY[:, 0:64], [bY], [wTf])
                cp("act", wTb[:, 64:128], bY[:, 64:128], [bY], [wTb])
                ctx.update(usb=usb, wTf=wTf, wTb=wTb, qgf=qgf, qgb=qgb, At=At, kdf=kdf, kdb=kdb, gcs=gcs, pc=pc)

            def rec(ctx):
                usb, wTf, wTb, qgf, qgb, At, kdf, kdb, gcs, pc = (ctx[k_] for k_ in
                                                                   ("usb", "wTf", "wTb", "qgf", "qgb", "At", "kdf", "kdb", "gcs", "pc"))
                bV = nb_()
                mm(bV[:, 0:128], wTf[:], Sf[:], True, False, [wTf, Sf], [bV])
                mm(bV[:, 0:128], wTb[:], Sb[:], False, True, [wTb, Sb], [bV])
                vnew = vn_r.next()
                tt("dve", vnew[:], usb[:], bV[:, 0:128], ALU.subtract, [usb, bV], [vnew])
                bO = nb_()
                mm(bO[:, 0:128], Sf[:], qgf[:], True, False, [Sf, qgf], [bO])
                mm(bO[:, 0:128], Sb[:], qgb[:], False, False, [Sb, qgb], [bO])
                mm(bO[:, 0:128], vnew[:], At[:], False, True, [vnew, At], [bO])
                bS = nb_()
                mm(bS[:, 0:128], kdf[:], vnew[:], True, True, [kdf, vnew], [bS])
                mm(bS[:, 128:256], kdb[:], vnew[:], True, True, [kdb, vnew], [bS])
                cp("act", OTp[:, pc], bO[:, 0:128], [bO], [OTp])
                stt("dve", Sf[:], Sf[:], gcs[:, 5:6], bS[:, 0:128], ALU.mult, ALU.add, [Sf, gcs, bS], [Sf])
                stt("dve", Sb[:], Sb[:], gcs[:, 6:7], bS[:, 128:256], ALU.mult, ALU.add, [Sb, gcs, bS], [Sb])

            for si, (t0, T, isctx, sidx) in enumerate(SEQS):
                n = T // 64
                if isctx:
                    memset("dve", Sf[:], 0.0, [Sf])
                    memset("dve", Sb[:], 0.0, [Sb])
                else:
                    P.dma("sp", Sf[:], sdn_d[l, 0, hd], r=[sdn_d], w=[Sf])
                    P.dma("sp", Sb[:], sdn_d[l, 1, hd], r=[sdn_d], w=[Sb])
                nst = min(n, cfg.get("d_nsteps", 99))
                for g0 in range(0, nst, G):
                    pe_warm(cfg.get("nwarm_d", 0))
                    ss = list(range(g0, min(g0 + G, nst)))
                    ctxs = [dict() for _ in ss]
                    alive = [pre_gen(slots[i], ctxs[i], si, s_, i) for i, s_ in enumerate(ss)]
                    while alive:
                        for g_ in list(alive):
                            try:
                                next(g_)
                            except StopIteration:
                                alive.remove(g_)
                    for c_ in ctxs:
                        rec(c_)
                if isctx:
                    P.dma("sp", ndn_d[sidx, l, 0, hd], Sf[:], r=[Sf], w=[ndn_d.s((sidx, l, 0, hd))])
                    P.dma("sp", ndn_d[sidx, l, 1, hd], Sb[:], r=[Sb], w=[ndn_d.s((sidx, l, 1, hd))])
            osum = raw
            for (t0, T, _, _) in SEQS:
                n = T // 64
                v = OTp[:, 2 * t0:2 * t0 + 2 * T].rearrange("p (n a c) -> p n a c", n=n, a=2, c=64)
                tt("dve", osum[:, t0:t0 + T].rearrange("p (n c) -> p n c", c=64), v[:, :, 0, :], v[:, ::-1, 1, :], ALU.add,
                   [OTp], [raw])
            ob_r = A.ring("d_ob", [TT], BF16, 2)
            for t_ in range(NTT):
                cs = slice(t_ * TT, (t_ + 1) * TT)
                tq = tq_r.next()
                actf(tq[:], osum[:, cs], AF.Square, [raw], [tq])
                bk = banks[4 + t_ % 2]
                mm(bk[:], ones_f[:], tq[:], True, True, [ones_f, tq], [bk])
                ts("dve", tq[:], bk[:], 1.0 / 128, EPS, ALU.mult, ALU.add, [bk], [tq])
                actf(tq[:], tq[:], AF.Sqrt, [tq], [tq])
                P.dve(lambda e, tq=tq: e.reciprocal(out=tq[:], in_=tq[:]), r=[tq], w=[tq])
                stt("dve", tq[:], osum[:, cs], pr[:, 48:49], tq[:], ALU.mult, ALU.mult, [raw, pr, tq], [tq])
                ob = ob_r.next()
                tt("dve", ob[:], tq[:], zs[:, cs], ALU.mult, [tq, cv], [ob])
                P.dma("sp", oT[3, hd * 128:(hd + 1) * 128, cs], ob[:], r=[ob], w=[oT.s((3, t_))])
            A.release(mh)
        A.release(m)

    def mixer(l, next_norm=None):
        for n, (nm, fn) in enumerate((("A", branch_A), ("B", branch_B), ("C", branch_C), ("D", branch_D))):
            if nm in branches:
                fn(l)
            else:
                zero_branch(n)
        merge_stage(l, next_norm)

    for l in range(nl):
        last = (l == nl - 1)
        if do_ffn:
            if l == 0:
                norm_stage(l, 0)
            ffn_stage(l, 0, 2, next_norm=((l, 1) if do_mixer else (l, 2)))
        elif do_mixer:
            norm_stage(l, 1)
        if do_mixer:
            mixer(l, next_norm=((l, 2) if do_ffn else None))
        if do_ffn:
            ffn_stage(l, 1, 8, next_norm=(None if last else (l + 1, 0)))
    final_stage()
    nc = P.finish()
    return nc, P, A


def _consts():
    half = 32
    inv = np.power(10000.0, -np.arange(0, half, 2, dtype=np.float32) / half).astype(np.float32)
    pos = np.arange(2048)
    ang_r = (pos // 64).astype(np.float32)[:, None] * inv[None, :]
    ang_c = (pos % 64).astype(np.float32)[:, None] * inv[None, :]
    cs = np.zeros((2048, 2, 64), np.float32)
    for (o, ang) in ((0, ang_r), (32, ang_c)):
        c = np.cos(ang).astype(np.float32)
        s = np.sin(ang).astype(np.float32)
        cs[:, 0, o:o + 16] = c
        cs[:, 0, o + 16:o + 32] = c
        cs[:, 1, o:o + 16] = -s
        cs[:, 1, o + 16:o + 32] = s
    kc = np.arange(64)[:, None]
    qc = np.arange(64)[None, :]
    cstart = np.clip(qc - 8, 0, 48)
    inwin = (kc >= cstart) & (kc < cstart + 16)
    mask = np.where(inwin, 0.0, NEG).astype(np.float32)
    mask = np.concatenate([mask, mask], axis=0)
    return cs, mask


def _dnmasks():
    p = np.arange(128)
    k = p[:, None]
    i = p[None, :]
    same = (k // 64) == (i // 64)
    fwd = (k < 64)
    cm = same & np.where(fwd, k <= i, k >= i)
    smt = same & np.where(fwd, i < k, i > k)
    out = np.zeros((6, 128, 128), np.float32)
    out[0] = cm
    out[1] = np.where(cm, 0.0, NEG)
    out[2] = np.where(smt, 0.0, -NEG)
    out[3] = same
    out[4] = np.broadcast_to(fwd, (128, 128))
    out[5] = np.broadcast_to(~fwd, (128, 128))
    return out


def _rpb_layout(na_rpb):
    L = na_rpb.shape[0]
    kc = np.arange(64)[:, None]
    qc = np.arange(64)[None, :]
    cidx = np.clip(kc - qc + 15, 0, 30)
    out = np.empty((L, 14, 2, 64, 8, 64), np.float32)
    for d0 in range(14):
        for j in range(2):
            row = d0 + j
            blk = na_rpb[:, :, row, :][:, :, cidx]
            out[:, d0, j] = np.transpose(blk, (0, 2, 1, 3))
    return np.ascontiguousarray(out.reshape(L, 14, 128, 8, 64))


def make_in_maps(inp, ncores=8, ld=DEPTH):
    f = lambda a: np.ascontiguousarray(np.asarray(a, dtype=np.float32))
    cs, mask = _consts()
    shared = {
        "w_mod": f(inp["w_mod"]), "b_mod": f(inp["b_mod"]).reshape(DEPTH, 72, 128),
        "norm_g": f(inp["norm_g"]).reshape(DEPTH, 24, 128),
        "w_ffn_gate": f(inp["w_ffn_gate"]), "w_ffn_up": f(inp["w_ffn_up"]), "w_ffn_down": f(inp["w_ffn_down"]),
        "w_in": f(inp["w_in"]),
        "lru_conv_w": f(inp["lru_conv_w"]).reshape(DEPTH, 16, 128), "lru_conv_b": f(inp["lru_conv_b"]).reshape(DEPTH, 4, 128),
        "lru_w_r": f(inp["lru_w_r"]), "lru_b_r": f(inp["lru_b_r"]).reshape(DEPTH, 8, 128),
        "lru_w_i": f(inp["lru_w_i"]), "lru_b_i": f(inp["lru_b_i"]).reshape(DEPTH, 8, 128),
        "lru_lambda": f(inp["lru_lambda"]).reshape(DEPTH, 8, 128),
        "gqa_q_norm": f(inp["gqa_q_norm"]), "gqa_k_norm": f(inp["gqa_k_norm"]),
        "rpbT": _rpb_layout(f(inp["na_rpb"])), "namask": mask,
        "dn_conv_w": f(inp["dn_conv_w"]).reshape(DEPTH, 48, 128),
        "dn_a_log": f(inp["dn_a_log"]).reshape(DEPTH, 8), "dn_dt_bias": f(inp["dn_dt_bias"]).reshape(DEPTH, 8),
        "dn_norm_g": f(inp["dn_norm_g"]),
        "w_branch": f(inp["w_branch"]), "w_out": f(inp["w_out"]),
        "final_norm_g": f(inp["final_norm_g"]).reshape(8, 128), "rope_cs": cs, "dnmasks": _dnmasks(),
    }
    maps = []
    for c in range(ncores):
        s = c % 4
        m = dict(shared)
        m["xp"] = f(inp["x_prompt"][2 * c:2 * c + 2]).reshape(512, D)
        m["xs"] = f(inp["x_sample"][s])
        m["c2"] = np.stack([f(inp["c_ctx"]), f(inp["c"][s])], axis=0)
        m["cak"] = f(inp["cache_attn_k"][s])
        m["cav"] = f(inp["cache_attn_v"][s])
        m["cnk"] = f(inp["cache_na_k"][s])
        m["cnv"] = f(inp["cache_na_v"][s])
        m["slru"] = f(inp["state_lru"][s])
        m["sdn"] = f(inp["state_delta"][s])
        maps.append(m)
    if ld != DEPTH:
        for m in maps:
            for k in list(m.keys()):
                a = m[k]
                if k in ('cak','cav','cnk','cnv','slru','sdn') or (a.shape[0] == DEPTH and k not in ('xp','xs','c2','namask','rope_cs','final_norm_g','dnmasks')):
                    m[k] = np.ascontiguousarray(a[:ld])
    return maps


_NC_CACHE = {}


def kernel(**inputs):
    if "nc" not in _NC_CACHE:
        _NC_CACHE["nc"] = build({})[0]
    nc = _NC_CACHE["nc"]
    maps = make_in_maps(inputs)
    res = run_bass_kernel_spmd(nc, maps, core_ids=list(range(8)))
    R = res.results
    y_p = np.concatenate([R[c]["y_p"].reshape(2, 256, D) for c in range(8)], axis=0)
    y_s = np.stack([R[c]["y_s"] for c in range(4)], axis=0)
    cat = lambda k: np.concatenate([R[c][k] for c in range(8)], axis=0)
    return (y_p.astype(np.float32), y_s.astype(np.float32), cat("new_ak"), cat("new_av"), cat("new_nk"),
            cat("new_nv"), cat("new_lru"), cat("new_dn"))
```

```python
import contextlib
import math
import numpy as np
import concourse.bass as bass
import concourse.mybir as mybir
from concourse.bass_utils import run_bass_kernel_spmd

F32 = mybir.dt.float32
BF16 = mybir.dt.bfloat16
AF = mybir.ActivationFunctionType
ALU = mybir.AluOpType
AX = mybir.AxisListType

SAME_ENGINE_SYNC = True

DEPTH = 4
D = 1024
KC = 8
DFF = 2816
JC = 22
NTOK = 2560
TT = 512
NTT = 5
NIN = 9488
EPS = 1e-6
SEQS = [(0, 256, True, 0), (256, 256, True, 1), (512, 2048, False, 0)]
OFF = {}
_o = 0
for _n, _w in [("ax", 512), ("ay", 512), ("bq", 512), ("bk", 128), ("bv", 128), ("cq", 512), ("ck", 512),
               ("cv", 512), ("dq", 512), ("dk", 512), ("dv", 512), ("dz", 512), ("db", 8), ("da", 8), ("g", 4096)]:
    OFF[_n] = _o
    _o += _w
NEG = -30000.0


class Tr:
    __slots__ = ("w", "r", "x")

    def __init__(self, fence=()):
        self.w = None
        self.r = list(fence)
        self.x = False


class Tile:
    def __init__(self, t, name, fence=()):
        self.t = t
        self.name = name
        self.fence = list(fence)
        self.tr = Tr(self.fence)
        self.sub = {}

    def __getitem__(self, k):
        return self.t[k]

    def s(self, key):
        tr = self.sub.get(key)
        if tr is None:
            tr = self.sub[key] = Tr(self.fence)
        return tr

    def all_ops(self):
        out = set()
        for tr in [self.tr] + list(self.sub.values()):
            if tr.w is not None:
                out.add(tr.w)
            out.update(tr.r)
        return out


class Op:
    __slots__ = ("eng", "fn", "deps", "dma", "sig", "sigidx", "dsem", "dval", "id")


def _trs(lst):
    out = []
    for x in lst:
        if x is None:
            continue
        if isinstance(x, Tile):
            out.append(x.tr)
        elif isinstance(x, Tr):
            out.append(x)
        else:
            out.extend(_trs(x))
    return out


class Prog:
    ENGS = ("pe", "act", "dve", "pool", "sp")

    def __init__(self, ndma_sems=8):
        self.nc = bass.Bass("TRN2", target_bir_lowering=False)
        self.es = contextlib.ExitStack()
        self.ops = []
        self.by_eng = {e: [] for e in self.ENGS}
        self.ndma = ndma_sems
        self.dma_count = {e: 0 for e in self.ENGS}

    def sb(self, name, shape, dtype):
        t = self.es.enter_context(self.nc.sbuf_tensor(name, list(shape), dtype))
        return Tile(t, name)

    def ps(self, name, shape, dtype):
        t = self.es.enter_context(self.nc.psum_tensor(name, list(shape), dtype))
        tl = Tile(t, name)
        tl.tr.x = True
        return tl

    def dram(self, name, shape, dtype, kind="Internal"):
        t = self.nc.dram_tensor(name, list(shape), dtype, kind=kind)
        return Tile(t, name)

    def _add(self, eng, fn, r, w, dma):
        op = Op()
        op.eng = eng
        op.fn = fn
        op.dma = dma
        op.sig = False
        op.sigidx = None
        op.dsem = None
        op.dval = None
        op.id = len(self.ops)
        deps = set()
        rt = _trs(r)
        wt = _trs(w)
        xr = [t for t in rt if t.x]
        if xr:
            rt = [t for t in rt if not t.x]
            wt = wt + [t for t in xr if t not in wt]
        for tr in rt:
            if tr.w is not None:
                deps.add(tr.w)
        for tr in wt:
            if tr.w is not None:
                deps.add(tr.w)
            deps.update(tr.r)
        deps.discard(op.id)
        last = {}
        red = set()
        for d in deps:
            p = self.ops[d]
            if p.dma:
                red.add(d)
            elif last.get(p.eng, -1) < d:
                last[p.eng] = d
        red.update(last.values())
        op.deps = red
        for tr in rt:
            tr.r.append(op.id)
        for tr in wt:
            tr.w = op.id
            tr.r = []
        self.ops.append(op)
        self.by_eng[eng].append(op)
        if dma:
            j = self.dma_count[eng]
            self.dma_count[eng] = j + 1
            op.dsem = (eng, j % self.ndma)
            op.dval = 16 * (j // self.ndma + 1)
        return op

    def pe(self, fn, r=(), w=()):
        return self._add("pe", fn, r, w, False)

    def act(self, fn, r=(), w=()):
        return self._add("act", fn, r, w, False)

    def dve(self, fn, r=(), w=()):
        return self._add("dve", fn, r, w, False)

    def pool(self, fn, r=(), w=()):
        return self._add("pool", fn, r, w, False)

    def dma(self, q, out, in_, r=(), w=(), **kw):
        return self._add(q, lambda e: e.dma_start(out=out, in_=in_, **kw), r, w, True)

    def finish(self):
        nc = self.nc
        ops = self.ops
        for op in ops:
            for d in op.deps:
                p = ops[d]
                if p.dma:
                    continue
                if p.eng == op.eng and (op.eng == "pe" or not SAME_ENGINE_SYNC) and not op.dma:
                    continue
                p.sig = True
        for e in self.ENGS:
            c = 0
            for op in self.by_eng[e]:
                if op.sig:
                    c += 1
                    op.sigidx = c
        es = self.es
        csem = {e: es.enter_context(nc.semaphore(f"c_{e}")) for e in self.ENGS}
        dsem = {}
        for e in self.ENGS:
            if self.dma_count[e]:
                for i in range(self.ndma):
                    dsem[(e, i)] = es.enter_context(nc.semaphore(f"d_{e}{i}"))
        block = es.enter_context(nc.Block())
        ndma = self.ndma
        nwaits = [0]

        def gen(ename):
            def body(eng):
                waited = {}

                def wait(sem, val):
                    if waited.get(sem, 0) >= val:
                        return
                    waited[sem] = val
                    eng.wait_ge(sem, val)
                    nwaits[0] += 1

                for op in self.by_eng[ename]:
                    for d in sorted(op.deps):
                        p = ops[d]
                        if p.dma:
                            wait(dsem[p.dsem], p.dval)
                        else:
                            if p.eng == ename and (ename == "pe" or not SAME_ENGINE_SYNC) and not op.dma:
                                continue
                            wait(csem[p.eng], p.sigidx)
                    if op.dma and op.dval > 16:
                        wait(dsem[op.dsem], op.dval - 16)
                    ins = op.fn(eng)
                    if op.dma:
                        ins.then_inc(dsem[op.dsem], 16)
                    elif op.sig:
                        ins.then_inc(csem[ename], 1)
                if self.dma_count[ename]:
                    n = self.dma_count[ename]
                    for i in range(min(ndma, n)):
                        cnt = (n - 1 - i) // ndma + 1
                        wait(dsem[(ename, i)], 16 * cnt)
            return body

        block.tensor(gen("pe"))
        block.scalar(gen("act"))
        block.vector(gen("dve"))
        block.gpsimd(gen("pool"))
        block.sync(gen("sp"))
        es.close()
        self.nwaits = nwaits[0]
        return nc


class Ring:
    def __init__(self, tiles):
        self.tiles = tiles
        self.i = 0

    def next(self):
        t = self.tiles[self.i % len(self.tiles)]
        self.i += 1
        return t


class Arena:
    def __init__(self, P, nwords):
        self.P = P
        self.t = P.es.enter_context(P.nc.sbuf_tensor("arena", [128, nwords], F32))
        self.nwords = nwords
        self.top = 0
        self.allocs = []
        self.peak = 0

    def mark(self):
        return self.top

    def release(self, m):
        self.top = m

    def alloc(self, name, fshape, dtype, parts=128):
        fshape = list(fshape)
        n = int(np.prod(fshape))
        nw = n if dtype == F32 else (n + 1) // 2
        nw = (nw + 7) // 8 * 8
        lo = self.top
        hi = lo + nw
        assert hi <= self.nwords, f"arena overflow {name} {hi}>{self.nwords}"
        self.top = hi
        self.peak = max(self.peak, hi)
        ap = self.t[0:parts, lo:hi]
        if dtype != F32:
            ap = ap.bitcast(dtype)
        ap = ap[:, 0:n]
        if len(fshape) == 2:
            ap = ap.rearrange("p (a b) -> p a b", a=fshape[0], b=fshape[1])
        elif len(fshape) == 3:
            ap = ap.rearrange("p (a b c) -> p a b c", a=fshape[0], b=fshape[1], c=fshape[2])
        elif len(fshape) == 4:
            ap = ap.rearrange("p (a b c d) -> p a b c d", a=fshape[0], b=fshape[1], c=fshape[2], d=fshape[3])
        fence = set()
        keep = []
        for (l2, h2, tl) in self.allocs:
            if l2 < hi and lo < h2:
                fence |= tl.all_ops()
                if lo <= l2 and h2 <= hi:
                    continue
            keep.append((l2, h2, tl))
        self.allocs = keep
        tile = Tile(ap, name, fence)
        self.allocs.append((lo, hi, tile))
        return tile

    def ring(self, name, fshape, dtype, n, parts=128):
        return Ring([self.alloc(f"{name}{i}", fshape, dtype, parts) for i in range(n)])


def build(cfg):
    nl = cfg.get("nl", DEPTH)
    LD = cfg.get("ld", DEPTH)
    dbg = cfg.get("dbg", ())
    do_mixer = cfg.get("mixer", True)
    do_ffn = cfg.get("ffn", True)
    branches = cfg.get("branches", "ABCD")
    P = Prog()
    nc = P.nc
    A = Arena(P, cfg.get("arena_words", 50000))

    def din(name, shape):
        return P.dram(name, shape, F32, kind="ExternalInput")

    def dout(name, shape):
        return P.dram(name, shape, F32, kind="ExternalOutput")

    xp_d = din("xp", [512, D])
    xs_d = din("xs", [2048, D])
    c2_d = din("c2", [2, D])
    cak_d = din("cak", [LD, 512, 2, 64])
    cav_d = din("cav", [LD, 512, 2, 64])
    cnk_d = din("cnk", [LD, 512, 8, 64])
    cnv_d = din("cnv", [LD, 512, 8, 64])
    slru_d = din("slru", [LD, 2, 512])
    sdn_d = din("sdn", [LD, 2, 4, 128, 128])
    w_mod = din("w_mod", [LD, D, 9 * D])
    b_mod = din("b_mod", [LD, 72, 128])
    norm_g = din("norm_g", [LD, 24, 128])
    w_g = din("w_ffn_gate", [LD, 2, D, DFF])
    w_u = din("w_ffn_up", [LD, 2, D, DFF])
    w_dn = din("w_ffn_down", [LD, 2, DFF, D])
    w_in = din("w_in", [LD, D, NIN])
    lru_cw = din("lru_conv_w", [LD, 16, 128])
    lru_cb = din("lru_conv_b", [LD, 4, 128])
    lru_wr = din("lru_w_r", [LD, 2, 8, 64, 64])
    lru_br = din("lru_b_r", [LD, 8, 128])
    lru_wi = din("lru_w_i", [LD, 2, 8, 64, 64])
    lru_bi = din("lru_b_i", [LD, 8, 128])
    lru_lam = din("lru_lambda", [LD, 8, 128])
    gqa_qn = din("gqa_q_norm", [LD, 64])
    gqa_kn = din("gqa_k_norm", [LD, 64])
    rpbT = din("rpbT", [LD, 14, 128, 8, 64])
    namask = din("namask", [128, 64])
    dn_cw = din("dn_conv_w", [LD, 48, 128])
    dn_alog = din("dn_a_log", [LD, 8])
    dn_dtb = din("dn_dt_bias", [LD, 8])
    dn_ng = din("dn_norm_g", [LD, 128])
    w_br = din("w_branch", [LD, 4, 512, D])
    w_out = din("w_out", [LD, D, D])
    fin_g = din("final_norm_g", [8, 128])
    rope_cs = din("rope_cs", [2048, 2, 64])
    dnm = din("dnmasks", [6, 128, 128])

    yp_d = dout("y_p", [512, D])
    ys_d = dout("y_s", [2048, D])
    nak_d = dout("new_ak", [2, LD, 256, 2, 64])
    nav_d = dout("new_av", [2, LD, 256, 2, 64])
    nnk_d = dout("new_nk", [2, LD, 256, 8, 64])
    nnv_d = dout("new_nv", [2, LD, 256, 8, 64])
    nlru_d = dout("new_lru", [2, LD, 2, 512])
    ndn_d = dout("new_dn", [2, LD, 2, 4, 128, 128])

    def scr(name, shape, dtype):
        return P.dram(name, shape, dtype, kind=("ExternalOutput" if name in dbg else "Internal"))

    xT = scr("xT", [D, NTOK], F32)
    actd = scr("actd", [DFF, NTOK], BF16)
    oT = scr("oT", [4, 512, NTOK], BF16)

    banks = [P.ps(f"bank{i}", [128, 512], F32) for i in range(8)]
    ident = P.sb("ident", [128, 128], F32)
    ones_bf = P.sb("ones_bf", [128, 128], BF16)
    ones_f = P.sb("ones_f", [128, 128], F32)
    mods = P.sb("mods", [128, DEPTH * 9 * 8 * 2], F32)
    ng_sb = P.sb("ng_sb", [128, DEPTH * 24], F32)
    fing_sb = P.sb("fing_sb", [128, 8], F32)
    silc = P.sb("silc", [128, 16], BF16)
    eff = P.sb("eff", [128, 2 * 8 * 2], F32)
    eff_ring = Ring([P.sb(f"effr{i}", [128, 8 * 2 * 2], F32) for i in range(4)])

    def modv(l, i, k, g):
        o = ((l * 9 + i) * 8 + k) * 2 + g
        return mods[:, o:o + 1]

    def mm(out, lhsT, rhs, start, stop, r, w, **kw):
        P.pe(lambda e: e.matmul(out, lhsT=lhsT, rhs=rhs, start=start, stop=stop, **kw), r=r, w=w)

    def tp(out, in_, idn, r, w):
        P.pe(lambda e: e.transpose(out=out, in_=in_, identity=idn), r=r, w=w)

    def actf(out, in_, func, r, w, bias=None, scale=None):
        kw = {}
        if bias is not None:
            kw["bias"] = bias
        if scale is not None:
            kw["scale"] = scale
        P.act(lambda e: e.activation(out=out, in_=in_, func=func, **kw), r=r, w=w)

    def tt(eng, out, in0, in1, op, r, w):
        getattr(P, eng)(lambda e: e.tensor_tensor(out=out, in0=in0, in1=in1, op=op), r=r, w=w)

    def ts(eng, out, in0, s1, s2, op0, op1, r, w):
        if s2 is None:
            getattr(P, eng)(lambda e: e.tensor_scalar(out=out, in0=in0, scalar1=s1, scalar2=None, op0=op0), r=r, w=w)
        else:
            getattr(P, eng)(lambda e: e.tensor_scalar(out=out, in0=in0, scalar1=s1, scalar2=s2, op0=op0, op1=op1), r=r, w=w)

    def stt(eng, out, in0, scalar, in1, op0, op1, r, w):
        getattr(P, eng)(lambda e: e.scalar_tensor_tensor(out=out, in0=in0, scalar=scalar, in1=in1, op0=op0, op1=op1), r=r, w=w)

    def cp(eng, out, in_, r, w):
        if eng == "act":
            P.act(lambda e: e.copy(out=out, in_=in_), r=r, w=w)
        else:
            getattr(P, eng)(lambda e: e.tensor_copy(out=out, in_=in_), r=r, w=w)

    def memset(eng, ap, val, w):
        getattr(P, eng)(lambda e: e.memset(ap, val), w=w)

    def pe_warm(n):
        for wi_ in range(n):
            bk_ = banks[wi_ % 8]
            mm(bk_[:, 0:512], ones_bf[:], h_all[:, 0, 0:512], True, True, [ones_bf], [bk_])

    def run_rr(gens):
        alive = list(gens)
        while alive:
            for g_ in list(alive):
                try:
                    next(g_)
                except StopIteration:
                    alive.remove(g_)

    memset("pool", ident[:], 0.0, [ident])
    P.pool(lambda e: e.affine_select(out=ident[:], in_=ident[:], pattern=[[-1, 128]], compare_op=ALU.not_equal,
                                     fill=1.0, base=0, channel_multiplier=1), r=[ident], w=[ident])
    memset("pool", ones_bf[:], 1.0, [ones_bf])
    memset("pool", ones_f[:], 1.0, [ones_f])

    stage_r = Ring([P.sb(f"vstage{i}", [128, 128], F32) for i in range(2)])
    m0 = A.mark()

    def load_rowsT(dst_ap, dst_tile, src_ap, src_tile, R, bank, C=128):
        st = stage_r.next()
        P.dma("sp", st[0:R, 0:C], src_ap, r=[src_tile], w=[st])
        tp(bank[0:C, 0:R], st[0:R, 0:C], ident[0:R, 0:R], [st, ident], [bank])
        cp("dve", dst_ap, bank[0:C, 0:R], [bank], [dst_tile])

    xin_r = A.ring("xin", [D], F32, 2)
    xst_r = A.ring("xst", [8, 128], F32, 2)
    for tc in range(20):
        src = xp_d[tc * 128:(tc + 1) * 128, :] if tc < 4 else xs_d[(tc - 4) * 128:(tc - 3) * 128, :]
        srct = xp_d if tc < 4 else xs_d
        xi = xin_r.next()
        P.dma("sp", xi[:], src, r=[srct], w=[xi])
        xs_ = xst_r.next()
        for half in range(2):
            bk = banks[(tc * 2 + half) % 4]
            for q in range(4):
                k = half * 4 + q
                tp(bk[:, q * 128:(q + 1) * 128], xi[:, k * 128:(k + 1) * 128], ident[:], [xi, ident], [bk])
            cp("dve" if half == 0 else "act", xs_[:, half * 4:(half + 1) * 4, :],
               bk[:].rearrange("p (a b) -> p a b", a=4, b=128), [bk], [xs_])
        P.dma("sp", xT.t.rearrange("(k p) t -> p k t", p=128)[:, :, tc * 128:(tc + 1) * 128], xs_[:],
              r=[xs_], w=[xT.s(tc // 4)])

    csb = A.alloc("csb", [16], F32)
    load_rowsT(csb[:], csb, c2_d.t.rearrange("g (k p) -> (g k) p", p=128), c2_d, 16, banks[4])
    actf(silc[:], csb[:], AF.Silu, [csb], [silc])
    for l in range(nl):
        load_rowsT(ng_sb[:, l * 24:(l + 1) * 24], ng_sb, norm_g[l], norm_g, 24, banks[5])
    load_rowsT(fing_sb[:], fing_sb, fin_g[:], fin_g, 8, banks[5])
    def mods_gen(l, bm_sb, wm_r):
        load_rowsT(bm_sb[:], bm_sb, b_mod[l], b_mod, 72, banks[5])
        for i in range(9):
            wt = wm_r.next()
            P.dma("pool", wt[:], w_mod[l].rearrange("(kc p) n -> p kc n", p=128)[:, :, i * 1024:(i + 1) * 1024],
                  r=[w_mod], w=[wt])
            bk = banks[6 + (i % 2)]
            for m in range(8):
                for k in range(8):
                    mm(bk[:, m * 2:m * 2 + 2], wt[:, k, m * 128:(m + 1) * 128],
                       silc[:].rearrange("p (g k) -> p k g", g=2)[:, k, :], k == 0, k == 7, [wt, silc], [bk])
            o = (l * 9 + i) * 16
            for g in range(2):
                tt("dve", mods[:, o:o + 16].rearrange("p (k g) -> p k g", g=2)[:, :, g],
                   bk[:, 0:16].rearrange("p (k g) -> p k g", g=2)[:, :, g], bm_sb[:, i * 8:(i + 1) * 8], ALU.add,
                   [bk, bm_sb], [mods])
            yield

    bm_sb0 = A.alloc("bm_sb", [72], F32)
    wm_r0 = A.ring("wm", [8, 1024], BF16, 2)
    for _ in mods_gen(0, bm_sb0, wm_r0):
        pass
    A.release(m0)

    h_all = A.alloc("h_all", [KC, NTOK], BF16)
    m_base = A.mark()

    def norm_prep(l, i):
        e_t = eff_ring.next()
        o_s = (l * 9 + 3 * i + 1) * 16
        ts("dve", e_t[:, 0:16], mods[:, o_s:o_s + 16], 1.0, None, ALU.add, None, [mods], [e_t])
        for g in range(2):
            v = e_t[:, 0:16].rearrange("p (k g) -> p k g", g=2)[:, :, g]
            tt("dve", v, v, ng_sb[:, l * 24 + i * 8:l * 24 + i * 8 + 8], ALU.mult, [e_t, ng_sb], [e_t])
        return e_t

    def norm_rings():
        return (A.ring("n_sq", [KC, TT], BF16, 2), A.ring("n_rs", [TT], F32, 2), A.ring("n_tm", [TT], F32, 3))

    def norm_tile(l, i, e_t, t_, xt, rings):
        sq_r, rs_r, tm_r = rings
        g = 0 if t_ == 0 else 1
        cs = slice(t_ * TT, (t_ + 1) * TT)
        sq = sq_r.next()
        actf(sq[:], xt[:], AF.Square, [xt], [sq])
        bk = banks[t_ % 2]
        for k in range(KC):
            mm(bk[:], ones_bf[:], sq[:, k, :], k == 0, k == KC - 1, [ones_bf, sq], [bk])
        rs = rs_r.next()
        ts("dve", rs[:], bk[:], 1.0 / D, EPS, ALU.mult, ALU.add, [bk], [rs])
        actf(rs[:], rs[:], AF.Sqrt, [rs], [rs])
        P.dve(lambda e, rs=rs: e.reciprocal(out=rs[:], in_=rs[:]), r=[rs], w=[rs])
        for k in range(KC):
            tm = tm_r.next()
            tt("dve", tm[:], xt[:, k, :], rs[:], ALU.mult, [xt, rs], [tm])
            actf(h_all[:, k, cs], tm[:], AF.Identity, [tm, e_t, mods], [h_all.s(t_)],
                 bias=modv(l, 3 * i, k, g), scale=e_t[:, k * 2 + g:k * 2 + g + 1])

    def norm_stage(l, i):
        m = A.mark()
        e_t = norm_prep(l, i)
        xt_r = A.ring("n_xt", [KC, TT], F32, 2)
        rings = norm_rings()
        for t_ in range(NTT):
            cs = slice(t_ * TT, (t_ + 1) * TT)
            xt = xt_r.next()
            P.dma("sp", xt[:], xT.t.rearrange("(k p) t -> p k t", p=128)[:, :, cs], r=[xT.s(t_)], w=[xt])
            norm_tile(l, i, e_t, t_, xt, rings)
        A.release(m)

    def ffn_stage(l, f, gi, next_norm=None):
        m0_ = A.mark()
        wd = A.alloc("f_wd", [JC, D], BF16)
        for j0 in range(0, JC, 6):
            nj = min(6, JC - j0)
            P.dma("pool", wd[:, j0:j0 + nj, :], w_dn[l, f].rearrange("(j p) n -> p j n", p=128)[:, j0:j0 + nj, :],
                  r=[w_dn], w=[wd.s(j0)])
        wd_trs = [wd.s(j0) for j0 in range(0, JC, 6)]
        m = A.mark()
        mg_ = None
        if f == 1 and l + 1 < nl:
            mg_ = mods_gen(l + 1, A.alloc("bm_sbn", [72], F32), A.ring("wmn", [8, 1024], BF16, 2))
        e_t = eff_ring.next()
        o_g = (l * 9 + gi) * 16
        ts("dve", e_t[:, 16:32], mods[:, o_g:o_g + 16], 0.5, None, ALU.mult, None, [mods], [e_t])
        wg_r = A.ring("f_wg", [KC, 512], BF16, 2)
        wu_r = A.ring("f_wu", [KC, 512], BF16, 2)
        sg_r = A.ring("f_sg", [TT], F32, 2)
        ao_r = A.ring("f_ao", [TT], BF16, 4)
        bi = 0
        for j0 in range(0, JC, 4):
            nj = min(4, JC - j0)
            wg = wg_r.next()
            wu = wu_r.next()
            P.dma("pool", wg[:, :, 0:nj * 128], w_g[l, f].rearrange("(kc p) n -> p kc n", p=128)[:, :, j0 * 128:(j0 + nj) * 128],
                  r=[w_g], w=[wg])
            P.dma("pool", wu[:, :, 0:nj * 128], w_u[l, f].rearrange("(kc p) n -> p kc n", p=128)[:, :, j0 * 128:(j0 + nj) * 128],
                  r=[w_u], w=[wu])
            for t_ in range(NTT):
                cs = slice(t_ * TT, (t_ + 1) * TT)
                for jj in range(nj):
                    j = j0 + jj
                    pg = banks[(bi * 2) % 8]
                    pu = banks[(bi * 2 + 1) % 8]
                    bi += 1
                    for k in range(KC):
                        mm(pg[:], wg[:, k, jj * 128:(jj + 1) * 128], h_all[:, k, cs], k == 0, k == KC - 1,
                           [wg, h_all.s(t_)], [pg])
                    for k in range(KC):
                        mm(pu[:], wu[:, k, jj * 128:(jj + 1) * 128], h_all[:, k, cs], k == 0, k == KC - 1,
                           [wu, h_all.s(t_)], [pu])
                    sg = sg_r.next()
                    actf(sg[:], pg[:], AF.Silu, [pg], [sg])
                    ao = ao_r.next()
                    tt("dve", ao[:], sg[:], pu[:], ALU.mult, [sg, pu], [ao])
                    P.dma("sp", actd[j * 128:(j + 1) * 128, cs], ao[:], r=[ao], w=[actd.s((j, t_))])
            if mg_ is not None:
                for _ in range(2):
                    next(mg_, None)
        if mg_ is not None:
            for _ in mg_:
                pass
        A.release(m)
        m = A.mark()
        ai_r = A.ring("f_ai", [JC, TT], BF16, 2)
        xt_r = A.ring("f_xt", [KC, TT], F32, 2)
        if next_norm is not None:
            ne_t = norm_prep(*next_norm)
            nrings = norm_rings()
        for t_ in range(NTT):
            g = 0 if t_ == 0 else 1
            cs = slice(t_ * TT, (t_ + 1) * TT)
            ai = ai_r.next()
            P.dma("sp", ai[:], actd.t.rearrange("(j p) t -> p j t", p=128)[:, :, cs],
                  r=[actd.s((j, t_)) for j in range(JC)], w=[ai])
            xt = xt_r.next()
            P.dma("sp", xt[:], xT.t.rearrange("(k p) t -> p k t", p=128)[:, :, cs], r=[xT.s(t_)], w=[xt])
            for mo in range(KC):
                bk = banks[2 + (t_ * KC + mo) % 6]
                for j in range(JC):
                    mm(bk[:], wd[:, j, mo * 128:(mo + 1) * 128], ai[:, j, :], j == 0, j == JC - 1,
                       [wd_trs, ai], [bk])
                stt("dve", xt[:, mo, :], bk[:], e_t[:, 16 + mo * 2 + g:16 + mo * 2 + g + 1], xt[:, mo, :],
                    ALU.mult, ALU.add, [bk, e_t, xt], [xt])
            P.dma("sp", xT.t.rearrange("(k p) t -> p k t", p=128)[:, :, cs], xt[:], r=[xt], w=[xT.s(t_)])
            if next_norm is not None:
                norm_tile(next_norm[0], next_norm[1], ne_t, t_, xt, nrings)
        A.release(m0_)

    def final_stage():
        m = A.mark()
        xt_r = A.ring("fn_xt", [KC, TT], F32, 2)
        sq_r = A.ring("fn_sq", [KC, TT], BF16, 2)
        rs_r = A.ring("fn_rs", [TT], F32, 2)
        yo_r = A.ring("fn_yo", [D], F32, 2)
        for t_ in range(NTT):
            cs = slice(t_ * TT, (t_ + 1) * TT)
            xt = xt_r.next()
            P.dma("sp", xt[:], xT.t.rearrange("(k p) t -> p k t", p=128)[:, :, cs], r=[xT.s(t_)], w=[xt])
            sq = sq_r.next()
            actf(sq[:], xt[:], AF.Square, [xt], [sq])
            bk = banks[t_ % 2]
            for k in range(KC):
                mm(bk[:], ones_bf[:], sq[:, k, :], k == 0, k == KC - 1, [ones_bf, sq], [bk])
            rs = rs_r.next()
            ts("dve", rs[:], bk[:], 1.0 / D, EPS, ALU.mult, ALU.add, [bk], [rs])
            actf(rs[:], rs[:], AF.Sqrt, [rs], [rs])
            P.dve(lambda e, rs=rs: e.reciprocal(out=rs[:], in_=rs[:]), r=[rs], w=[rs])
            for k in range(KC):
                stt("dve", xt[:, k, :], xt[:, k, :], fing_sb[:, k:k + 1], rs[:], ALU.mult, ALU.mult,
                    [xt, fing_sb, rs], [xt])
            for c4 in range(4):
                tc = t_ * 4 + c4
                yo = yo_r.next()
                for half in range(2):
                    bk2 = banks[2 + (tc * 2 + half) % 4]
                    for q in range(4):
                        k = half * 4 + q
                        tp(bk2[:, q * 128:(q + 1) * 128], xt[:, k, c4 * 128:(c4 + 1) * 128], ident[:], [xt, ident], [bk2])
                    cp("act" if half == 0 else "dve", yo[:, half * 512:(half + 1) * 512], bk2[:], [bk2], [yo])
                if tc < 4:
                    P.dma("sp", yp_d[tc * 128:(tc + 1) * 128, :], yo[:], r=[yo], w=[yp_d.s(tc)])
                else:
                    P.dma("sp", ys_d[(tc - 4) * 128:(tc - 3) * 128, :], yo[:], r=[yo], w=[ys_d.s(tc)])
        A.release(m)

    def w_in_ap(l, c0, c1):
        return w_in[l].rearrange("(kc p) n -> p kc n", p=128)[:, :, c0:c1]

    def zero_branch(n):
        m = A.mark()
        z = A.alloc("zb", [NTOK], BF16)
        memset("dve", z[:], 0.0, [z])
        for c in range(4):
            P.dma("sp", oT[n, c * 128:(c + 1) * 128, :], z[:], r=[z], w=[oT.s((n, t_)) for t_ in range(NTT)])
        A.release(m)

    def otr(n, lo, hi):
        return [oT.s((n, t_)) for t_ in range(lo // TT, (hi - 1) // TT + 1)]

    def merge_stage(l, next_norm=None):
        m = A.mark()
        mgall = A.alloc("mg_all", [KC, NTOK], BF16)
        m2 = A.mark()
        wg_r = A.ring("mg_wg", [KC, 4, 128], BF16, 2)
        wb_r = A.ring("mg_wb", [16, 128], BF16, 2)
        ot_r = A.ring("mg_ot", [16, TT], BF16, 2)
        sg_r = A.ring("mg_sg", [TT], F32, 3)
        ac_r = A.ring("mg_ac", [TT], F32, 2)
        bi = 0
        for mo in range(KC):
            wg = wg_r.next()
            for n in range(4):
                c0 = OFF["g"] + n * 1024 + mo * 128
                P.dma("pool", wg[:, :, n, :], w_in_ap(l, c0, c0 + 128), r=[w_in], w=[wg])
            wb = wb_r.next()
            P.dma("pool", wb[:], w_br[l].rearrange("n (kc p) d -> p (n kc) d", p=128)[:, :, mo * 128:(mo + 1) * 128],
                  r=[w_br], w=[wb])
            for t_ in range(NTT):
                cs = slice(t_ * TT, (t_ + 1) * TT)
                ot = ot_r.next()
                P.dma("sp", ot[:], oT.t.rearrange("n (kc p) t -> p (n kc) t", p=128)[:, :, cs],
                      r=[oT.s((n, t_)) for n in range(4)], w=[ot])
                ac = ac_r.next()
                for n in range(4):
                    pg = banks[(bi * 2) % 8]
                    pb = banks[(bi * 2 + 1) % 8]
                    bi += 1
                    for k in range(KC):
                        mm(pg[:], wg[:, k, n, :], h_all[:, k, cs], k == 0, k == KC - 1, [wg, h_all.s(t_)], [pg])
                    for kc in range(4):
                        mm(pb[:], wb[:, n * 4 + kc, :], ot[:, n * 4 + kc, :], kc == 0, kc == 3, [wb, ot], [pb])
                    sg = sg_r.next()
                    actf(sg[:], pg[:], AF.Sigmoid, [pg], [sg])
                    if n == 0:
                        tt("dve", ac[:], sg[:], pb[:], ALU.mult, [sg, pb], [ac])
                    else:
                        tt("dve", sg[:], sg[:], pb[:], ALU.mult, [sg, pb], [sg])
                        if n < 3:
                            tt("dve", ac[:], ac[:], sg[:], ALU.add, [ac, sg], [ac])
                        else:
                            tt("dve", mgall[:, mo, cs], ac[:], sg[:], ALU.add, [ac, sg], [mgall.s(t_)])
        A.release(m2)
        wo = A.alloc("mg_wo", [KC, D], BF16)
        P.dma("pool", wo[:], w_out[l].rearrange("(kc p) n -> p kc n", p=128), r=[w_out], w=[wo])
        xt_r = A.ring("mg_xt", [KC, TT], F32, 2)
        if next_norm is not None:
            ne_t = norm_prep(*next_norm)
            nrings = norm_rings()
        for t_ in range(NTT):
            g = 0 if t_ == 0 else 1
            cs = slice(t_ * TT, (t_ + 1) * TT)
            xt = xt_r.next()
            P.dma("sp", xt[:], xT.t.rearrange("(k p) t -> p k t", p=128)[:, :, cs], r=[xT.s(t_)], w=[xt])
            for mo in range(KC):
                bk = banks[2 + (t_ * KC + mo) % 6]
                for k in range(KC):
                    mm(bk[:], wo[:, k, mo * 128:(mo + 1) * 128], mgall[:, k, cs], k == 0, k == KC - 1,
                       [wo, mgall.s(t_)], [bk])
                stt("dve", xt[:, mo, :], bk[:], modv(l, 5, mo, g), xt[:, mo, :], ALU.mult, ALU.add,
                    [bk, mods, xt], [xt])
            P.dma("sp", xT.t.rearrange("(k p) t -> p k t", p=128)[:, :, cs], xt[:], r=[xt], w=[xT.s(t_)])
            if next_norm is not None:
                norm_tile(next_norm[0], next_norm[1], ne_t, t_, xt, nrings)
        A.release(m)

    PADW = 2569
    PBASE = [0, 259, 518]

    def padcol(tok):
        if tok < 256:
            return PBASE[0] + 2 + tok
        if tok < 512:
            return PBASE[1] + 2 + tok - 256
        return PBASE[2] + 2 + tok - 512

    def evac_padded(dst, dst_tr, bank, t_, eng):
        if t_ == 0:
            cp(eng, dst[:, padcol(0):padcol(0) + 256], bank[:, 0:256], [bank], [dst_tr])
            cp(eng, dst[:, padcol(256):padcol(256) + 256], bank[:, 256:512], [bank], [dst_tr])
        else:
            c = padcol(t_ * TT)
            cp(eng, dst[:, c:c + TT], bank[:, :], [bank], [dst_tr])

    def zero_pads(dst, dst_tr):
        for (t0, T, _, _), pb in zip(SEQS, PBASE):
            memset("pool", dst[:, pb:pb + 2], 0.0, [dst_tr])
            memset("pool", dst[:, pb + 2 + T:pb + 3 + T], 0.0, [dst_tr])

    def conv4(dst_fn, src, src_tr, wcol_fn, bias_ap, r_extra, w_tr):
        for si, ((t0, T, _, _), pb) in enumerate(zip(SEQS, PBASE)):
            d = dst_fn(si)
            if bias_ap is not None:
                actf(d, src[:, pb:pb + T], AF.Identity, [src_tr] + r_extra, [w_tr], bias=bias_ap, scale=wcol_fn(0))
            else:
                actf(d, src[:, pb:pb + T], AF.Identity, [src_tr] + r_extra, [w_tr], scale=wcol_fn(0))
            for j in range(1, 4):
                stt("dve", d, src[:, pb + j:pb + j + T], wcol_fn(j), d, ALU.mult, ALU.add, [src_tr, w_tr] + r_extra, [w_tr])

    def branch_A(l):
        m = A.mark()
        pr = A.alloc("a_par", [16 + 4 + 8 + 8 + 8 + 8 + 8], F32)
        load_rowsT(pr[:, 0:16], pr, lru_cw[l], lru_cw, 16, banks[7])
        load_rowsT(pr[:, 16:20], pr, lru_cb[l], lru_cb, 4, banks[7])
        load_rowsT(pr[:, 20:28], pr, lru_br[l], lru_br, 8, banks[7])
        load_rowsT(pr[:, 28:36], pr, lru_bi[l], lru_bi, 8, banks[7])
        load_rowsT(pr[:, 36:44], pr, lru_lam[l], lru_lam, 8, banks[7])
        load_rowsT(pr[:, 52:60], pr, slru_d[l].rearrange("d (c p) -> (d c) p", p=128), slru_d, 8, banks[7])
        actf(pr[:, 36:44], pr[:, 36:44], AF.Sigmoid, [pr], [pr])
        actf(pr[:, 36:44], pr[:, 36:44], AF.Ln, [pr], [pr])
        ts("dve", pr[:, 44:52], pr[:, 36:44], 16.0, None, ALU.mult, None, [pr], [pr])
        ts("dve", pr[:, 36:44], pr[:, 36:44], 8.0, None, ALU.mult, None, [pr], [pr])
        wxy_r = A.ring("a_w", [KC, 256], BF16, 2)
        wbd_r = A.ring("a_wbd", [4, 128], BF16, 2)
        axp = A.alloc("a_axp", [PADW], F32)
        zero_pads(axp, axp.tr)
        xa = A.alloc("a_xa", [NTOK], F32)
        xab = A.alloc("a_xab", [NTOK], BF16)
        gy = A.alloc("a_gy", [NTOK], F32)
        hsum = A.alloc("a_hs", [NTOK], F32)
        hdir = A.alloc("a_hd", [NTOK], F32)
        aa = A.alloc("a_aa", [NTOK], F32)
        uu = A.alloc("a_uu", [NTOK], F32)
        tmp_r = A.ring("a_tmp", [TT], F32, 3)
        ob_r = A.ring("a_ob", [NTOK], BF16, 1)
        fin_r = A.ring("a_fin", [2], F32, 2)
        for cch in range(4):
            wxy = wxy_r.next()
            P.dma("pool", wxy[:, :, 0:128], w_in_ap(l, OFF["ax"] + cch * 128, OFF["ax"] + (cch + 1) * 128), r=[w_in], w=[wxy])
            P.dma("pool", wxy[:, :, 128:256], w_in_ap(l, OFF["ay"] + cch * 128, OFF["ay"] + (cch + 1) * 128), r=[w_in], w=[wxy])
            wbd = wbd_r.next()
            memset("dve", wbd[:], 0.0, [wbd])
            for dr in range(2):
                for gi, wsrc in enumerate((lru_wr, lru_wi)):
                    for b2 in range(2):
                        P.dma("pool", wbd[b2 * 64:(b2 + 1) * 64, dr * 2 + gi, b2 * 64:(b2 + 1) * 64],
                              wsrc[l, dr, 2 * cch + b2], r=[wsrc], w=[wbd])
            for t_ in range(NTT):
                cs = slice(t_ * TT, (t_ + 1) * TT)
                px = banks[(t_ * 2) % 4]
                py = banks[(t_ * 2 + 1) % 4]
                for k in range(KC):
                    mm(px[:], wxy[:, k, 0:128], h_all[:, k, cs], k == 0, k == KC - 1, [wxy, h_all.s(t_)], [px])
                for k in range(KC):
                    mm(py[:], wxy[:, k, 128:256], h_all[:, k, cs], k == 0, k == KC - 1, [wxy, h_all.s(t_)], [py])
                evac_padded(axp, axp.tr, px, t_, "act")
                t1 = tmp_r.next()
                actf(t1[:], py[:], AF.Square, [py], [t1])
                ts("dve", t1[:], t1[:], 0.044715, 1.0, ALU.mult, ALU.add, [t1], [t1])
                tt("dve", t1[:], t1[:], py[:], ALU.mult, [t1, py], [t1])
                actf(t1[:], t1[:], AF.Sigmoid, [t1], [t1], scale=1.5957691216057308)
                tt("dve", gy[:, cs], t1[:], py[:], ALU.mult, [t1, py], [gy.s(t_)])
            conv4(lambda si: xa[:, SEQS[si][0]:SEQS[si][0] + SEQS[si][1]], axp, axp.tr,
                  lambda j: pr[:, j * 4 + cch:j * 4 + cch + 1], pr[:, 16 + cch:17 + cch], [pr], xa.tr)
            cp("act", xab[:], xa[:], [xa], [xab])
            for dr in range(2):
                ccol = pr[:, 36 + dr * 4 + cch:37 + dr * 4 + cch]
                c2col = pr[:, 44 + dr * 4 + cch:45 + dr * 4 + cch]
                for t_ in range(NTT):
                    cs = slice(t_ * TT, (t_ + 1) * TT)
                    pr_ = banks[4 + (t_ * 2) % 4]
                    pi_ = banks[4 + (t_ * 2 + 1) % 4]
                    mm(pr_[:], wbd[:, dr * 2 + 0, :], xab[:, cs], True, True, [wbd, xab], [pr_])
                    mm(pi_[:], wbd[:, dr * 2 + 1, :], xab[:, cs], True, True, [wbd, xab], [pi_])
                    rr = tmp_r.next()
                    actf(rr[:], pr_[:], AF.Sigmoid, [pr_, pr], [rr], bias=pr[:, 20 + dr * 4 + cch:21 + dr * 4 + cch])
                    ii = tmp_r.next()
                    actf(ii[:], pi_[:], AF.Sigmoid, [pi_, pr], [ii], bias=pr[:, 28 + dr * 4 + cch:29 + dr * 4 + cch])
                    actf(aa[:, cs], rr[:], AF.Exp, [rr, pr], [aa.s(t_)], scale=ccol)
                    actf(rr[:], rr[:], AF.Exp, [rr, pr], [rr], scale=c2col)
                    ts("dve", rr[:], rr[:], -1.0, 1.0, ALU.mult, ALU.add, [rr], [rr])
                    actf(rr[:], rr[:], AF.Sqrt, [rr], [rr])
                    tt("dve", ii[:], ii[:], xa[:, cs], ALU.mult, [ii, xa], [ii])
                    tt("dve", uu[:, cs], ii[:], rr[:], ALU.mult, [ii, rr], [uu.s(t_)])
                aatr = [aa.s(t_) for t_ in range(NTT)]
                uutr = [uu.s(t_) for t_ in range(NTT)]
                dst = hsum if dr == 0 else hdir
                for si, (t0, T, isctx, sidx) in enumerate(SEQS):
                    sl = slice(t0, t0 + T)
                    init = 0.0 if isctx else pr[:, 52 + dr * 4 + cch:53 + dr * 4 + cch]
                    if dr == 0:
                        P.dve(lambda e, sl=sl, init=init, dst=dst: e.tensor_tensor_scan(
                            out=dst[:, sl], data0=aa[:, sl], data1=uu[:, sl], initial=init, op0=ALU.mult, op1=ALU.add),
                            r=aatr + uutr + [pr], w=[dst])
                        fcol = t0 + T - 1
                    else:
                        P.dve(lambda e, t0=t0, T=T, init=init, dst=dst: e.tensor_tensor_scan(
                            out=dst[:, t0:t0 + T][:, ::-1], data0=aa[:, t0:t0 + T][:, ::-1], data1=uu[:, t0:t0 + T][:, ::-1],
                            initial=init, op0=ALU.mult, op1=ALU.add), r=aatr + uutr + [pr], w=[dst])
                        fcol = t0
                    if isctx:
                        P.dma("sp", nlru_d[sidx, l, dr, cch * 128:(cch + 1) * 128].rearrange("(p o) -> p o", o=1),
                              dst[:, fcol:fcol + 1], r=[dst], w=[nlru_d.s((sidx, l, dr, cch))])
            tt("dve", hsum[:], hsum[:], hdir[:], ALU.add, [hsum, hdir], [hsum])
            ob = ob_r.next()
            tt("dve", ob[:], hsum[:], gy[:], ALU.mult, [hsum, gy] + [gy.s(t_) for t_ in range(NTT)], [ob])
            P.dma("sp", oT[0, cch * 128:(cch + 1) * 128, :], ob[:], r=[ob], w=[oT.s((0, t_)) for t_ in range(NTT)])
        A.release(m)

    def attn_dense(qT, kT, V, nq_heads, kv_of, qcols, kchunks, out_n, out_head0, qtr, ktr, vtr):
        q0, QT = qcols
        m = A.mark()
        pt_r = A.ring("at_pt", [QT], BF16, 5)
        dr_r = A.ring("at_dr", [QT], F32, 2)
        rc_r = A.ring("at_rc", [QT], F32, 2, parts=64)
        ob_r = A.ring("at_ob", [QT], BF16, 2, parts=64)
        cnt = 0
        for h0 in range(0, nq_heads, 4):
            hs = list(range(h0, min(h0 + 4, nq_heads)))
            for wi_ in range(cfg.get("nwarm", 12)):
                mm(banks[4 + wi_ % 4][:, 0:512], ones_bf[:], h_all[:, 0, 0:512], True, True, [ones_bf], [banks[4 + wi_ % 4]])
            items = [(ki, kc, r_, h) for ki, kc in enumerate(kchunks) for r_, h in enumerate(hs)]
            pts = {}
            LA = 3
            for i in range(len(items) + LA):
                if i < len(items):
                    ki, kc, r_, h = items[i]
                    g = kv_of(h)
                    sb_ = banks[4 + cnt % 4]
                    cnt += 1
                    mm(sb_[:, 0:QT], kT[0:64, g, kc * 128:(kc + 1) * 128], qT[0:64, h, q0:q0 + QT], True, True,
                       [ktr, qtr], [sb_])
                    pt = pt_r.next()
                    actf(pt[:], sb_[:, 0:QT], AF.Exp, [sb_], [pt])
                    pts[i] = pt
                if i >= LA:
                    ki, kc, r_, h = items[i - LA]
                    g = kv_of(h)
                    pt = pts.pop(i - LA)
                    mm(banks[r_][0:65, 0:QT], V[:, kc, g, :], pt[:], ki == 0, ki == len(kchunks) - 1, [vtr, pt], [banks[r_]])
            for r_, h in enumerate(hs):
                po = banks[r_]
                dr = dr_r.next()
                cp("act", dr[64:65, :], po[64:65, 0:QT], [po], [dr])
                db_ = banks[4 + r_ % 2]
                mm(db_[0:64, 0:QT], ones_f[64:65, 0:64], dr[64:65, :], True, True, [ones_f, dr], [db_])
                rc = rc_r.next()
                P.dve(lambda e, rc=rc, QT=QT, db_=db_: e.reciprocal(out=rc[:], in_=db_[0:64, 0:QT]), r=[db_], w=[rc])
                ob = ob_r.next()
                tt("dve", ob[:], po[0:64, 0:QT], rc[:], ALU.mult, [po, rc], [ob])
                row0 = (out_head0 + h) * 64
                P.dma("sp", oT[out_n, row0:row0 + 64, q0:q0 + QT], ob[:], r=[ob], w=otr(out_n, q0, q0 + QT))
        A.release(m)

    def transpose_heads(dst, dst_tr, src_ap_fn, nheads, col0, r, scale=None):
        for h0 in range(0, nheads, 4):
            n = min(4, nheads - h0)
            bk = banks[6 + (h0 // 4) % 2]
            for i in range(n):
                tp(bk[0:64, i * 128:(i + 1) * 128], src_ap_fn(h0 + i), ident[:], r + [ident], [bk])
            src = bk[0:64, 0:n * 128].rearrange("p (a b) -> p a b", a=n, b=128)
            if scale is None:
                cp("act", dst[0:64, h0:h0 + n, col0:col0 + 128], src, [bk], [dst_tr])
            else:
                P.act(lambda e, src=src, h0=h0, n=n: e.mul(out=dst[0:64, h0:h0 + n, col0:col0 + 128], in_=src, mul=scale),
                      r=[bk], w=[dst_tr])

    def branch_B(l):
        m = A.mark()
        wb = A.alloc("b_w", [KC, 768], BF16)
        P.dma("pool", wb[:], w_in_ap(l, OFF["bq"], OFF["bq"] + 768), r=[w_in], w=[wb])
        gq = A.alloc("b_gq", [64], F32)
        gk = A.alloc("b_gk", [64], F32)
        P.dma("sp", gq[:], gqa_qn[l].partition_broadcast(128), r=[gqa_qn], w=[gq])
        P.dma("sp", gk[:], gqa_kn[l].partition_broadcast(128), r=[gqa_kn], w=[gk])
        qT = A.alloc("b_qT", [8, NTOK], BF16, parts=64)
        kT = A.alloc("b_kT", [2, NTOK + 512], BF16, parts=64)
        V = A.alloc("b_V", [24, 2, 65], BF16)
        memset("dve", V[:], 1.0, [V])
        m2 = A.mark()
        sq_r = A.ring("b_sq", [640], F32, 2)
        ss_r = A.ring("b_ss", [10], F32, 2)
        qk_r = A.ring("b_qk", [10, 64], F32, 2)
        rt_r = A.ring("b_rt", [10, 64], F32, 2)
        vv_r = A.ring("b_vv", [128], F32, 2)
        cs_r = A.ring("b_cs", [2, 64], F32, 2)
        def b_chunk(tc):
            tsl = slice(tc * 128, (tc + 1) * 128)
            t_ = tc // 4
            p0 = banks[(tc * 2) % 4]
            p1 = banks[(tc * 2 + 1) % 4]
            for k in range(KC):
                mm(p0[:], h_all[:, k, tsl], wb[:, k, 0:512], k == 0, k == KC - 1, [wb, h_all.s(t_)], [p0])
            for k in range(KC):
                mm(p1[:, 0:256], h_all[:, k, tsl], wb[:, k, 512:768], k == 0, k == KC - 1, [wb, h_all.s(t_)], [p1])
            yield
            sq = sq_r.next()
            actf(sq[:, 0:512], p0[:], AF.Square, [p0], [sq])
            actf(sq[:, 512:640], p1[:, 0:128], AF.Square, [p1], [sq])
            yield
            ss = ss_r.next()
            P.dve(lambda e, ss=ss, sq=sq: e.tensor_reduce(out=ss[:], in_=sq[:].rearrange("p (h d) -> p h d", d=64),
                                                          axis=AX.X, op=ALU.add), r=[sq], w=[ss])
            ts("dve", ss[:], ss[:], 1.0 / 64, EPS, ALU.mult, ALU.add, [ss], [ss])
            actf(ss[:], ss[:], AF.Sqrt, [ss], [ss])
            yield
            P.dve(lambda e, ss=ss: e.reciprocal(out=ss[:], in_=ss[:]), r=[ss], w=[ss])
            qk = qk_r.next()
            tt("dve", qk[:, 0:8, :], p0[:].rearrange("p (h d) -> p h d", d=64),
               ss[:, 0:8].unsqueeze(2).broadcast_to([128, 8, 64]), ALU.mult, [p0, ss], [qk])
            tt("dve", qk[:, 8:10, :], p1[:, 0:128].rearrange("p (h d) -> p h d", d=64),
               ss[:, 8:10].unsqueeze(2).broadcast_to([128, 2, 64]), ALU.mult, [p1, ss], [qk])
            tt("dve", qk[:, 0:8, :], qk[:, 0:8, :], gq[:].unsqueeze(1).broadcast_to([128, 8, 64]), ALU.mult, [qk, gq], [qk])
            tt("dve", qk[:, 8:10, :], qk[:, 8:10, :], gk[:].unsqueeze(1).broadcast_to([128, 2, 64]), ALU.mult, [qk, gk], [qk])
            vv = vv_r.next()
            cp("act", vv[:], p1[:, 128:256], [p1], [vv])
            cp("dve", V[:, tc, :, 0:64], vv[:].rearrange("p (g d) -> p g d", d=64), [vv], [V])
            yield
            if tc < 4:
                sidx = tc // 2
                r0 = (tc % 2) * 128
                P.dma("sp", nak_d[sidx, l, r0:r0 + 128].rearrange("t g d -> t (g d)"), qk[:, 8:10, :].rearrange("p g d -> p (g d)"),
                      r=[qk], w=[nak_d.s((tc, l))])
                P.dma("sp", nav_d[sidx, l, r0:r0 + 128].rearrange("t g d -> t (g d)"), vv[:], r=[vv], w=[nav_d.s((tc, l))])
                src = qk
            else:
                cst = cs_r.next()
                P.dma("sp", cst[:], rope_cs[(tc - 4) * 128:(tc - 3) * 128], r=[rope_cs], w=[cst])
                rt = rt_r.next()
                sw = qk[:].rearrange("p h (a b c) -> p h a b c", a=2, b=2, c=16)[:, :, :, ::-1, :]
                tt("dve", rt[:].rearrange("p h (a b c) -> p h a b c", a=2, b=2, c=16), sw,
                   cst[:, 1, :].rearrange("p (a b c) -> p a b c", a=2, b=2, c=16).unsqueeze(1).broadcast_to([128, 10, 2, 2, 16]),
                   ALU.mult, [qk, cst], [rt])
                tt("dve", qk[:], qk[:], cst[:, 0, :].unsqueeze(1).broadcast_to([128, 10, 64]), ALU.mult, [qk, cst], [qk])
                tt("dve", rt[:], rt[:], qk[:], ALU.add, [rt, qk], [rt])
                src = rt
            yield
            transpose_heads(qT, qT.tr, lambda h, src=src: src[:, h, :], 8, tc * 128, [src], scale=0.125)
            transpose_heads(kT, kT.tr, lambda h, src=src: src[:, 8 + h, :], 2, tc * 128, [src])
        for tc0 in range(0, 20, 2):
            run_rr([b_chunk(tc0), b_chunk(tc0 + 1)])
        ck_r = A.ring("b_ck", [2, 64], F32, 2)
        for i in range(4):
            ck = ck_r.next()
            P.dma("sp", ck[:], cak_d[l, i * 128:(i + 1) * 128], r=[cak_d], w=[ck])
            transpose_heads(kT, kT.tr, lambda h, ck=ck: ck[:, h, :], 2, NTOK + i * 128, [ck])
            cv = ck_r.next()
            P.dma("sp", cv[:], cav_d[l, i * 128:(i + 1) * 128], r=[cav_d], w=[cv])
            cp("dve", V[:, 20 + i, :, 0:64], cv[:], [cv], [V])
        A.release(m2)
        for (t0, T, isctx, sidx) in SEQS:
            if isctx:
                attn_dense(qT, kT, V, 8, lambda h: h // 4, (t0, T), [t0 // 128, t0 // 128 + 1], 1, 0, qT.tr, kT.tr, V.tr)
            else:
                for q0 in range(t0, t0 + T, 512):
                    attn_dense(qT, kT, V, 8, lambda h: h // 4, (q0, 512), list(range(4, 24)), 1, 0, qT.tr, kT.tr, V.tr)
        A.release(m)

    def branch_C(l):
        m = A.mark()
        nmask = A.alloc("c_mask", [64], F32)
        P.dma("sp", nmask[:], namask[:], r=[namask], w=[nmask])
        for hg in range(2):
            mh = A.mark()
            wc = A.alloc("c_w", [KC, 768], BF16)
            for i, nm in enumerate(("cq", "ck", "cv")):
                c0 = OFF[nm] + hg * 256
                P.dma("pool", wc[:, :, i * 256:(i + 1) * 256], w_in_ap(l, c0, c0 + 256), r=[w_in], w=[wc])
            qT = A.alloc("c_qT", [4, NTOK], BF16, parts=64)
            kT = A.alloc("c_kT", [4, NTOK + 512], BF16, parts=64)
            V = A.alloc("c_V", [24, 4, 65], BF16)
            memset("dve", V[:], 1.0, [V])
            bias = A.alloc("c_bias", [14, 4, 64], F32)
            for a_ in range(14):
                P.dma("sp", bias[:, a_, :, :], rpbT[l, a_, :, hg * 4:(hg + 1) * 4, :], r=[rpbT], w=[bias])
            tt("dve", bias[:].rearrange("p a h q -> p (a h) q"), bias[:].rearrange("p a h q -> p (a h) q"),
               nmask[:].unsqueeze(1).broadcast_to([128, 56, 64]), ALU.add, [bias, nmask], [bias])
            m2 = A.mark()
            qk_r = A.ring("c_qk", [768], F32, 2)
            def c_chunk(tc):
                tsl = slice(tc * 128, (tc + 1) * 128)
                t_ = tc // 4
                p0 = banks[(tc * 2) % 4]
                p1 = banks[(tc * 2 + 1) % 4]
                for k in range(KC):
                    mm(p0[:], h_all[:, k, tsl], wc[:, k, 0:512], k == 0, k == KC - 1, [wc, h_all.s(t_)], [p0])
                for k in range(KC):
                    mm(p1[:, 0:256], h_all[:, k, tsl], wc[:, k, 512:768], k == 0, k == KC - 1, [wc, h_all.s(t_)], [p1])
                yield
                qk = qk_r.next()
                cp("act", qk[:, 0:512], p0[:], [p0], [qk])
                cp("dve", qk[:, 512:768], p1[:, 0:256], [p1], [qk])
                cp("dve", V[:, tc, :, 0:64], qk[:, 512:768].rearrange("p (g d) -> p g d", d=64), [qk], [V])
                yield
                if tc < 4:
                    sidx = tc // 2
                    r0 = (tc % 2) * 128
                    P.dma("sp", nnk_d[sidx, l, r0:r0 + 128, hg * 4:(hg + 1) * 4, :].rearrange("t g d -> t (g d)"), qk[:, 256:512],
                          r=[qk], w=[nnk_d.s((tc, l, hg))])
                    P.dma("sp", nnv_d[sidx, l, r0:r0 + 128, hg * 4:(hg + 1) * 4, :].rearrange("t g d -> t (g d)"), qk[:, 512:768],
                          r=[qk], w=[nnv_d.s((tc, l, hg))])
                transpose_heads(qT, qT.tr, lambda h, qk=qk: qk[:, h * 64:(h + 1) * 64], 4, tc * 128, [qk], scale=0.125)
                transpose_heads(kT, kT.tr, lambda h, qk=qk: qk[:, 256 + h * 64:256 + (h + 1) * 64], 4, tc * 128, [qk])
            for tc0 in range(0, 20, 2):
                run_rr([c_chunk(tc0), c_chunk(tc0 + 1)])
            ck_r = A.ring("c_ck", [4, 64], F32, 2)
            for i in range(4):
                ck = ck_r.next()
                P.dma("sp", ck[:], cnk_d[l, i * 128:(i + 1) * 128, hg * 4:(hg + 1) * 4, :], r=[cnk_d], w=[ck])
                transpose_heads(kT, kT.tr, lambda h, ck=ck: ck[:, h, :], 4, NTOK + i * 128, [ck])
                cv = ck_r.next()
                P.dma("sp", cv[:], cnv_d[l, i * 128:(i + 1) * 128, hg * 4:(hg + 1) * 4, :], r=[cnv_d], w=[cv])
                cp("dve", V[:, 20 + i, :, 0:64], cv[:], [cv], [V])
            A.release(m2)
            for (t0, T, isctx, sidx) in SEQS:
                if isctx:
                    attn_dense(qT, kT, V, 4, lambda h: h, (t0, T), [t0 // 128, t0 // 128 + 1], 2, hg * 4, qT.tr, kT.tr, V.tr)
            m3 = A.mark()
            sc_r = A.ring("n_sc", [256], F32, 3)
            pt_r = A.ring("n_pt", [256], BF16, 4)
            dr_r = A.ring("n_dr", [256], F32, 2)
            rc_r = A.ring("n_rc", [256], F32, 2, parts=64)
            ob_r = A.ring("n_ob", [4, 512], BF16, 2, parts=64)
            ob = None
            cnt = 0
            work = []
            for r in range(32):
                rs = min(max(r - 4, 0), 24)
                units = []
                e = (rs // 2) * 2
                while e <= rs + 7:
                    units.append((4 + e // 2, e >= rs, e + 1 <= rs + 7, e - r + 7))
                    e += 2
                for i in range(4):
                    units.append((20 + i, True, True, None))
                for ui, u in enumerate(units):
                    work.append((r, ui, len(units), u))
            LA = 2
            pts = {}
            for wi in range(len(work) + LA):
                if wi < len(work):
                    r, ui, nu, (kc, lo, hi, d0) = work[wi]
                    q0 = 512 + r * 64
                    if ui == 0 and r % cfg.get("na_warm_every", 4) == 0:
                        pe_warm(cfg.get("nwarm_na", 0))
                    sb_ = banks[2 + cnt % 4]
                    cnt += 1
                    for h in range(4):
                        mm(sb_[:, h * 64:(h + 1) * 64], kT[0:64, h, kc * 128:(kc + 1) * 128], qT[0:64, h, q0:q0 + 64], True, True,
                           [kT, qT], [sb_])
                    pt = pt_r.next()
                    if d0 is not None:
                        sc = sc_r.next()
                        tt("dve", sc[:], sb_[:, 0:256], bias[:, d0, :, :].rearrange("p h q -> p (h q)"), ALU.add, [sb_, bias], [sc])
                        actf(pt[:], sc[:], AF.Exp, [sc], [pt])
                    else:
                        actf(pt[:], sb_[:, 0:256], AF.Exp, [sb_], [pt])
                    if not lo:
                        memset("dve", pt[0:64, :], 0.0, [pt])
                    if not hi:
                        memset("dve", pt[64:128, :], 0.0, [pt])
                    pts[wi] = pt
                if wi >= LA:
                    r, ui, nu, (kc, lo, hi, d0) = work[wi - LA]
                    pt = pts.pop(wi - LA)
                    po = banks[r % 2]
                    for h in range(4):
                        mm(po[0:65, h * 64:(h + 1) * 64], V[:, kc, h, :], pt[:, h * 64:(h + 1) * 64], ui == 0 and h == 0,
                           ui == nu - 1, [V, pt], [po], skip_group_check=True)
                    if ui == nu - 1:
                        dr = dr_r.next()
                        cp("act", dr[64:65, :], po[64:65, 0:256], [po], [dr])
                        mm(banks[6][0:64, 0:256], ones_f[64:65, 0:64], dr[64:65, :], True, True, [ones_f, dr], [banks[6]])
                        rc = rc_r.next()
                        P.dve(lambda e, rc=rc: e.reciprocal(out=rc[:], in_=banks[6][0:64, 0:256]), r=[banks[6]], w=[rc])
                        if r % 8 == 0:
                            ob = ob_r.next()
                        tt("dve", ob[:, :, (r % 8) * 64:(r % 8 + 1) * 64], po[0:64, 0:256].rearrange("p (h q) -> p h q", q=64),
                           rc[:].rearrange("p (h q) -> p h q", q=64), ALU.mult, [po, rc], [ob])
                        if r % 8 == 7:
                            c0 = 512 + (r - 7) * 64
                            P.dma("sp", oT[2, hg * 256:(hg + 1) * 256, c0:c0 + 512].rearrange("(h d) t -> d h t", d=64), ob[:],
                                  r=[ob], w=otr(2, c0, c0 + 512))
            A.release(m3)
            A.release(mh)
        A.release(m)

    def pair_copy(dst, dst_tr, src, r, parts=128, engs=("act", "dve")):
        for (t0, T, _, _) in SEQS:
            n = T // 64
            d = dst[0:parts, 2 * t0:2 * t0 + 2 * T].rearrange("p (n a c) -> p n a c", n=n, a=2, c=64)
            s_ = src[0:parts, t0:t0 + T].rearrange("p (n c) -> p n c", c=64)
            cp(engs[0], d[:, :, 0, :], s_, r, [dst_tr])
            cp(engs[1], d[:, :, 1, :], s_[:, ::-1, :], r, [dst_tr])

    STEP0 = [0, 4, 8]

    def branch_D(l):
        m = A.mark()
        msk = A.alloc("d_msk", [6, 128], F32)
        P.dma("sp", msk[:], dnm.t.rearrange("a p q -> p a q"), r=[dnm], w=[msk])
        CM, NEGCM, POSSMT, BMk, HMf, HMb = (msk[:, i, :] for i in range(6))
        pr = A.alloc("d_par", [52], F32)
        load_rowsT(pr[:, 0:48], pr, dn_cw[l], dn_cw, 48, banks[7])
        load_rowsT(pr[:, 48:49], pr, dn_ng[l:l + 1, :], dn_ng, 1, banks[7])
        BG = A.alloc("d_BG", [40, 10], F32)
        memset("dve", BG[:], 0.0, [BG])
        m0_ = A.mark()
        p8 = A.alloc("d_p8", [4], F32)
        load_rowsT(p8[0:8, 0:1], p8, dn_alog[l:l + 1, :], dn_alog, 1, banks[7], C=8)
        load_rowsT(p8[0:8, 1:2], p8, dn_dtb[l:l + 1, :], dn_dtb, 1, banks[7], C=8)
        actf(p8[0:8, 2:3], p8[0:8, 0:1], AF.Exp, [p8], [p8])
        ts("dve", p8[0:8, 2:3], p8[0:8, 2:3], -1.0, None, ALU.mult, None, [p8], [p8])
        wd = A.alloc("d_wbg", [KC, 16], BF16)
        P.dma("pool", wd[:], w_in_ap(l, OFF["db"], OFF["db"] + 16), r=[w_in], w=[wd])
        bet = A.alloc("d_bet", [NTOK], F32)
        gg = A.alloc("d_gg", [NTOK], F32)
        for t_ in range(NTT):
            cs = slice(t_ * TT, (t_ + 1) * TT)
            pb = banks[(t_ * 2) % 4]
            pa = banks[(t_ * 2 + 1) % 4]
            for k in range(KC):
                mm(pb[0:8, :], wd[:, k, 0:8], h_all[:, k, cs], k == 0, k == KC - 1, [wd, h_all.s(t_)], [pb])
            for k in range(KC):
                mm(pa[0:8, :], wd[:, k, 8:16], h_all[:, k, cs], k == 0, k == KC - 1, [wd, h_all.s(t_)], [pa])
            actf(bet[0:8, cs], pb[0:8, :], AF.Sigmoid, [pb], [bet])
            actf(gg[0:8, cs], pa[0:8, :], AF.Exp, [pa, p8], [gg], bias=p8[0:8, 1:2])
            actf(gg[0:8, cs], gg[0:8, cs], AF.Ln, [gg], [gg], bias=1.0)
            ts("dve", gg[0:8, cs], gg[0:8, cs], p8[0:8, 2:3], None, ALU.mult, None, [gg, p8], [gg])
        betP = A.alloc("d_betP", [2 * NTOK], F32)
        ggP = A.alloc("d_ggP", [2 * NTOK], F32)
        pair_copy(betP, betP.tr, bet, [bet], parts=8)
        pair_copy(ggP, ggP.tr, gg, [gg], parts=8)
        for s8 in range(5):
            bk = banks[4 + s8 % 2]
            for i in range(8):
                sg = s8 * 8 + i
                tp(bk[:, i * 16:i * 16 + 8], betP[0:8, sg * 128:(sg + 1) * 128], ident[0:8, 0:8], [betP, ident], [bk])
                tp(bk[:, i * 16 + 8:i * 16 + 16], ggP[0:8, sg * 128:(sg + 1) * 128], ident[0:8, 0:8], [ggP, ident], [bk])
            v = bk[:, 0:128].rearrange("p (s c) -> p s c", c=16)
            d = BG[:, s8 * 8:(s8 + 1) * 8, 0:8]
            cp("act", d[0:64, :, 0:4], v[0:64, :, 0:4], [bk], [BG])
            cp("dve", d[64:128, :, 0:4], v[64:128, :, 4:8], [bk], [BG])
            cp("act", d[0:64, :, 4:8], v[0:64, :, 8:12], [bk], [BG])
            cp("dve", d[64:128, :, 4:8], v[64:128, :, 12:16], [bk], [BG])
        A.release(m0_)
        if cfg.get("d_stop") == 0:
            A.release(m)
            return
        for hd in range(cfg.get("d_heads", 4)):
            mh = A.mark()
            qP = A.alloc("d_qP", [2 * NTOK], F32)
            kP = A.alloc("d_kP", [2 * NTOK], F32)
            vP = A.alloc("d_vP", [2 * NTOK], F32)
            OTp = A.alloc("d_OTp", [2 * NTOK], F32)
            raw = A.alloc("d_raw", [PADW], F32)
            zero_pads(raw, raw.tr)
            cv = A.alloc("d_cv", [NTOK], F32)
            tq_r = A.ring("d_tq", [TT], F32, 2)
            m_w = A.mark()
            w4 = A.alloc("d_w4", [KC, 4, 128], BF16)
            for i, nm in enumerate(("dq", "dk", "dv", "dz")):
                c0 = OFF[nm] + hd * 128
                P.dma("pool", w4[:, :, i, :], w_in_ap(l, c0, c0 + 128), r=[w_in], w=[w4])
            for i, dstP in enumerate((qP, kP, vP)):
                for t_ in range(NTT):
                    cs = slice(t_ * TT, (t_ + 1) * TT)
                    px = banks[t_ % 4]
                    for k in range(KC):
                        mm(px[:], w4[:, k, i, :], h_all[:, k, cs], k == 0, k == KC - 1, [w4, h_all.s(t_)], [px])
                    evac_padded(raw, raw.tr, px, t_, "act")
                c12 = i * 4 + hd
                conv4(lambda si: cv[:, SEQS[si][0]:SEQS[si][0] + SEQS[si][1]], raw, raw.tr,
                      lambda j, c12=c12: pr[:, j * 12 + c12:j * 12 + c12 + 1], None, [pr], cv.tr)
                actf(cv[:], cv[:], AF.Silu, [cv], [cv])
                if i < 2:
                    for t_ in range(NTT):
                        cs = slice(t_ * TT, (t_ + 1) * TT)
                        tq = tq_r.next()
                        actf(tq[:], cv[:, cs], AF.Square, [cv], [tq])
                        bk = banks[4 + t_ % 2]
                        mm(bk[:], ones_f[:], tq[:], True, True, [ones_f, tq], [bk])
                        ts("dve", tq[:], bk[:], EPS, None, ALU.add, None, [bk], [tq])
                        actf(tq[:], tq[:], AF.Sqrt, [tq], [tq])
                        P.dve(lambda e, tq=tq: e.reciprocal(out=tq[:], in_=tq[:]), r=[tq], w=[tq])
                        if i == 0:
                            stt("dve", cv[:, cs], cv[:, cs], 128.0 ** -0.5, tq[:], ALU.mult, ALU.mult, [cv, tq], [cv])
                        else:
                            tt("dve", cv[:, cs], cv[:, cs], tq[:], ALU.mult, [cv, tq], [cv])
                pair_copy(dstP, dstP.tr, cv, [cv])
            for t_ in range(NTT):
                cs = slice(t_ * TT, (t_ + 1) * TT)
                px = banks[t_ % 4]
                for k in range(KC):
                    mm(px[:], w4[:, k, 3, :], h_all[:, k, cs], k == 0, k == KC - 1, [w4, h_all.s(t_)], [px])
                actf(cv[:, cs], px[:], AF.Silu, [px], [cv])
            zs = cv
            A.release(m_w)
            if cfg.get("d_stop") == 1:
                A.release(mh)
                continue
            G = cfg.get("d_G", 4)
            Sf = A.alloc("d_Sf", [128], F32)
            Sb = A.alloc("d_Sb", [128], F32)
            slots = []
            for gslot in range(G):
                sl = {"tmp": A.ring(f"d_tmp{gslot}_", [128], F32, 4)}
                for nm_, n_ in (("N", 1), ("MT", 2), ("M", 2), ("TT", 2), ("u", 1), ("At", 1), ("vb", 1), ("kbg", 1),
                                ("wTf", 1), ("wTb", 1), ("qgf", 1), ("qgb", 1), ("kdf", 1), ("kdb", 1)):
                    sl[nm_] = A.ring(f"d_{nm_}{gslot}_", [128], F32, n_)
                for nm_ in ("wTf", "wTb", "qgf", "qgb", "kdf", "kdb"):
                    for tl in sl[nm_].tiles:
                        memset("pool", tl[:], 0.0, [tl])
                sl["gcs"] = A.ring(f"d_gcs{gslot}_", [8], F32, 1)
                sl["nb"] = A.ring(f"d_nb{gslot}_", [2], F32, 1)
                slots.append(sl)
            vn_r = A.ring("d_vnew", [128], F32, 2)
            bc = [0]

            def nb_():
                b = banks[bc[0] % 8]
                bc[0] += 1
                return b

            def pre_gen(sl, ctx, si, s, ci):
                bX, bY = banks[2 * ci], banks[2 * ci + 1]
                t0, T, isctx, sidx = SEQS[si]
                sg = STEP0[si] + s
                pc = slice(2 * t0 + s * 128, 2 * t0 + (s + 1) * 128)
                kS, qS, vS = kP[:, pc], qP[:, pc], vP[:, pc]
                bcol = BG[:, sg, hd:hd + 1]
                gcol = BG[:, sg, 4 + hd:5 + hd]
                gcol2 = BG[:, sg, 4 + hd:6 + hd]
                tmp = sl["tmp"]
                Gb = tmp.next()
                ts("pool", Gb[:], ones_f[:], gcol, None, ALU.mult, None, [ones_f, BG], [Gb])
                nb = sl["nb"].next()
                ts("pool", nb[:, 0:1], bcol, -1.0, None, ALU.mult, None, [BG], [nb])
                mm(bX[:, 0:128], kS, kS, True, True, [kP], [bX])
                mm(bX[:, 128:256], kS, qS, True, True, [kP, qP], [bX])
                tp(bX[:, 256:384], kS, ident[:], [kP, ident], [bX])
                tp(bX[:, 384:512], vS, ident[:], [vP, ident], [bX])
                yield
                mm(bY[:, 0:128], Gb[:], CM, True, True, [Gb, msk], [bY])
                mm(bY[:, 128:130], CM, gcol2, True, True, [msk, BG], [bY])
                mm(bY[:, 130:132], HMf, gcol2, True, True, [msk, BG], [bY])
                mm(bY[:, 132:134], HMb, gcol2, True, True, [msk, BG], [bY])
                mm(bY[:, 134:136], BMk, gcol2, True, True, [msk, BG], [bY])
                vb = sl["vb"].next()
                ts("dve", vb[:], bX[:, 384:512], bcol, None, ALU.mult, None, [bX, BG], [vb])
                yield
                gcs = sl["gcs"].next()
                cp("dve", gcs[:, 0:4], bY[:, 128:136].rearrange("p (a b) -> p a b", b=2)[:, :, 0], [bY], [gcs])
                t1 = tmp.next()
                stt("dve", t1[:], bY[:, 0:128], gcs[:, 0:1], NEGCM, ALU.subtract, ALU.add, [bY, gcs, msk], [t1])
                t2 = tmp.next()
                stt("dve", t2[:], bY[:, 0:128], gcs[:, 0:1], POSSMT, ALU.subtract, ALU.add, [bY, gcs, msk], [t2])
                ER = tmp.next()
                actf(ER[:], bY[:, 0:128], AF.Exp, [bY], [ER])
                yield
                actf(gcs[:, 4:7], gcs[:, 0:3], AF.Exp, [gcs], [gcs])
                actf(gcs[:, 7:8], gcs[:, 0:1], AF.Exp, [gcs], [gcs], bias=gcs[:, 3:4], scale=-1.0)
                actf(t1[:], t1[:], AF.Exp, [t1], [t1])
                actf(t2[:], t2[:], AF.Exp, [t2], [t2], scale=-1.0)
                Dt, Ds = t1, t2
                qgf = sl["qgf"].next()
                qgb = sl["qgb"].next()
                tt("pool", qgf[:, 0:64], qP[:, pc][:, 0:64], ER[:, 0:64], ALU.mult, [qP, ER], [qgf])
                tt("pool", qgb[:, 64:128], qP[:, pc][:, 64:128], ER[:, 64:128], ALU.mult, [qP, ER], [qgb])
                yield
                tt("pool", nb[:, 1:2], bcol, gcs[:, 4:5], ALU.mult, [BG, gcs], [nb])
                Nm = sl["N"].next()
                stt("dve", Nm[:], bX[:, 0:128], nb[:, 0:1], Ds[:], ALU.mult, ALU.mult, [bX, nb, Ds], [Nm])
                At = sl["At"].next()
                tt("dve", At[:], bX[:, 128:256], Dt[:], ALU.mult, [bX, Dt], [At])
                kdf = sl["kdf"].next()
                kdb = sl["kdb"].next()
                ts("dve", kdf[0:64, :], bX[0:64, 256:384], gcs[0:64, 7:8], None, ALU.mult, None, [bX, gcs], [kdf])
                ts("dve", kdb[64:128, :], bX[64:128, 256:384], gcs[64:128, 7:8], None, ALU.mult, None, [bX, gcs], [kdb])
                yield
                tp(bY[:, 0:128], Nm[:], ident[:], [Nm, ident], [bY])
                kbg = sl["kbg"].next()
                ts("dve", kbg[:], bX[:, 256:384], nb[:, 1:2], None, ALU.mult, None, [bX, nb], [kbg])
                yield
                MT = sl["MT"].next()
                cp("act", MT[:], bY[:, 0:128], [bY], [MT])
                TTt = sl["TT"].next()
                tt("dve", TTt[:], bY[:, 0:128], ident[:], ALU.add, [bY, ident], [TTt])
                yield
                M_ = Nm
                IMp = None
                for kk in range(1, 6):
                    mm(bX[:, 0:128], MT[:], M_[:], True, True, [MT, M_], [bX])
                    if kk < 5:
                        mm(bY[:, 0:128], M_[:], MT[:], True, True, [MT, M_], [bY])
                    if IMp is not None:
                        mm(bX[:, 128:256], IMp[:], TTt[:], True, True, [IMp, TTt], [bX])
                    yield
                    IM = tmp.next()
                    tt("dve", IM[:], bX[:, 0:128], ident[:], ALU.add, [bX, ident], [IM])
                    if kk < 5:
                        Mn = sl["M"].next()
                        cp("act", Mn[:], bX[:, 0:128], [bX], [Mn])
                        MTn = sl["MT"].next()
                        cp("act", MTn[:], bY[:, 0:128], [bY], [MTn])
                    if IMp is not None:
                        TTn = sl["TT"].next()
                        cp("act", TTn[:], bX[:, 128:256], [bX], [TTn])
                        TTt = TTn
                    IMp = IM
                    if kk < 5:
                        M_, MT = Mn, MTn
                    yield
                mm(bX[:, 128:256], IMp[:], TTt[:], True, True, [IMp, TTt], [bX])
                yield
                TTn = sl["TT"].next()
                cp("act", TTn[:], bX[:, 128:256], [bX], [TTn])
                TTt = TTn
                yield
                mm(bX[:, 0:128], TTt[:], vb[:], True, True, [TTt, vb], [bX])
                mm(bY[:, 0:128], kbg[:], TTt[:], True, True, [kbg, TTt], [bY])
                yield
                usb = sl["u"].next()
                cp("act", usb[:], bX[:, 0:128], [bX], [usb])
                wTf = sl["wTf"].next()
                wTb = sl["wTb"].next()
                cp("dve", wTf[:, 0:64], bY[:, 0:64], [bY], [wTf])
                cp("act", wTb[:, 64:128], bY[:, 64:128], [bY], [wTb])
                ctx.update(usb=usb, wTf=wTf, wTb=wTb, qgf=qgf, qgb=qgb, At=At, kdf=kdf, kdb=kdb, gcs=gcs, pc=pc)

            def rec(ctx):
                usb, wTf, wTb, qgf, qgb, At, kdf, kdb, gcs, pc = (ctx[k_] for k_ in
                                                                   ("usb", "wTf", "wTb", "qgf", "qgb", "At", "kdf", "kdb", "gcs", "pc"))
                bV = nb_()
                mm(bV[:, 0:128], wTf[:], Sf[:], True, False, [wTf, Sf], [bV])
                mm(bV[:, 0:128], wTb[:], Sb[:], False, True, [wTb, Sb], [bV])
                vnew = vn_r.next()
                tt("dve", vnew[:], usb[:], bV[:, 0:128], ALU.subtract, [usb, bV], [vnew])
                bO = nb_()
                mm(bO[:, 0:128], Sf[:], qgf[:], True, False, [Sf, qgf], [bO])
                mm(bO[:, 0:128], Sb[:], qgb[:], False, False, [Sb, qgb], [bO])
                mm(bO[:, 0:128], vnew[:], At[:], False, True, [vnew, At], [bO])
                bS = nb_()
                mm(bS[:, 0:128], kdf[:], vnew[:], True, True, [kdf, vnew], [bS])
                mm(bS[:, 128:256], kdb[:], vnew[:], True, True, [kdb, vnew], [bS])
                cp("act", OTp[:, pc], bO[:, 0:128], [bO], [OTp])
                stt("dve", Sf[:], Sf[:], gcs[:, 5:6], bS[:, 0:128], ALU.mult, ALU.add, [Sf, gcs, bS], [Sf])
                stt("dve", Sb[:], Sb[:], gcs[:, 6:7], bS[:, 128:256], ALU.mult, ALU.add, [Sb, gcs, bS], [Sb])

            for si, (t0, T, isctx, sidx) in enumerate(SEQS):
                n = T // 64
                if isctx:
                    memset("dve", Sf[:], 0.0, [Sf])
                    memset("dve", Sb[:], 0.0, [Sb])
                else:
                    P.dma("sp", Sf[:], sdn_d[l, 0, hd], r=[sdn_d], w=[Sf])
                    P.dma("sp", Sb[:], sdn_d[l, 1, hd], r=[sdn_d], w=[Sb])
                nst = min(n, cfg.get("d_nsteps", 99))
                for g0 in range(0, nst, G):
                    pe_warm(cfg.get("nwarm_d", 0))
                    ss = list(range(g0, min(g0 + G, nst)))
                    ctxs = [dict() for _ in ss]
                    alive = [pre_gen(slots[i], ctxs[i], si, s_, i) for i, s_ in enumerate(ss)]
                    while alive:
                        for g_ in list(alive):
                            try:
                                next(g_)
                            except StopIteration:
                                alive.remove(g_)
                    for c_ in ctxs:
                        rec(c_)
                if isctx:
                    P.dma("sp", ndn_d[sidx, l, 0, hd], Sf[:], r=[Sf], w=[ndn_d.s((sidx, l, 0, hd))])
                    P.dma("sp", ndn_d[sidx, l, 1, hd], Sb[:], r=[Sb], w=[ndn_d.s((sidx, l, 1, hd))])
            osum = raw
            for (t0, T, _, _) in SEQS:
                n = T // 64
                v = OTp[:, 2 * t0:2 * t0 + 2 * T].rearrange("p (n a c) -> p n a c", n=n, a=2, c=64)
                tt("dve", osum[:, t0:t0 + T].rearrange("p (n c) -> p n c", c=64), v[:, :, 0, :], v[:, ::-1, 1, :], ALU.add,
                   [OTp], [raw])
            ob_r = A.ring("d_ob", [TT], BF16, 2)
            for t_ in range(NTT):
                cs = slice(t_ * TT, (t_ + 1) * TT)
                tq = tq_r.next()
                actf(tq[:], osum[:, cs], AF.Square, [raw], [tq])
                bk = banks[4 + t_ % 2]
                mm(bk[:], ones_f[:], tq[:], True, True, [ones_f, tq], [bk])
                ts("dve", tq[:], bk[:], 1.0 / 128, EPS, ALU.mult, ALU.add, [bk], [tq])
                actf(tq[:], tq[:], AF.Sqrt, [tq], [tq])
                P.dve(lambda e, tq=tq: e.reciprocal(out=tq[:], in_=tq[:]), r=[tq], w=[tq])
                stt("dve", tq[:], osum[:, cs], pr[:, 48:49], tq[:], ALU.mult, ALU.mult, [raw, pr, tq], [tq])
                ob = ob_r.next()
                tt("dve", ob[:], tq[:], zs[:, cs], ALU.mult, [tq, cv], [ob])
                P.dma("sp", oT[3, hd * 128:(hd + 1) * 128, cs], ob[:], r=[ob], w=[oT.s((3, t_))])
            A.release(mh)
        A.release(m)

    def mixer(l, next_norm=None):
        for n, (nm, fn) in enumerate((("A", branch_A), ("B", branch_B), ("C", branch_C), ("D", branch_D))):
            if nm in branches:
                fn(l)
            else:
                zero_branch(n)
        merge_stage(l, next_norm)

    for l in range(nl):
        last = (l == nl - 1)
        if do_ffn:
            if l == 0:
                norm_stage(l, 0)
            ffn_stage(l, 0, 2, next_norm=((l, 1) if do_mixer else (l, 2)))
        elif do_mixer:
            norm_stage(l, 1)
        if do_mixer:
            mixer(l, next_norm=((l, 2) if do_ffn else None))
        if do_ffn:
            ffn_stage(l, 1, 8, next_norm=(None if last else (l + 1, 0)))
    final_stage()
    nc = P.finish()
    return nc, P, A


def _consts():
    half = 32
    inv = np.power(10000.0, -np.arange(0, half, 2, dtype=np.float32) / half).astype(np.float32)
    pos = np.arange(2048)
    ang_r = (pos // 64).astype(np.float32)[:, None] * inv[None, :]
    ang_c = (pos % 64).astype(np.float32)[:, None] * inv[None, :]
    cs = np.zeros((2048, 2, 64), np.float32)
    for (o, ang) in ((0, ang_r), (32, ang_c)):
        c = np.cos(ang).astype(np.float32)
        s = np.sin(ang).astype(np.float32)
        cs[:, 0, o:o + 16] = c
        cs[:, 0, o + 16:o + 32] = c
        cs[:, 1, o:o + 16] = -s
        cs[:, 1, o + 16:o + 32] = s
    kc = np.arange(64)[:, None]
    qc = np.arange(64)[None, :]
    cstart = np.clip(qc - 8, 0, 48)
    inwin = (kc >= cstart) & (kc < cstart + 16)
    mask = np.where(inwin, 0.0, NEG).astype(np.float32)
    mask = np.concatenate([mask, mask], axis=0)
    return cs, mask


def _dnmasks():
    p = np.arange(128)
    k = p[:, None]
    i = p[None, :]
    same = (k // 64) == (i // 64)
    fwd = (k < 64)
    cm = same & np.where(fwd, k <= i, k >= i)
    smt = same & np.where(fwd, i < k, i > k)
    out = np.zeros((6, 128, 128), np.float32)
    out[0] = cm
    out[1] = np.where(cm, 0.0, NEG)
    out[2] = np.where(smt, 0.0, -NEG)
    out[3] = same
    out[4] = np.broadcast_to(fwd, (128, 128))
    out[5] = np.broadcast_to(~fwd, (128, 128))
    return out


def _rpb_layout(na_rpb):
    L = na_rpb.shape[0]
    kc = np.arange(64)[:, None]
    qc = np.arange(64)[None, :]
    cidx = np.clip(kc - qc + 15, 0, 30)
    out = np.empty((L, 14, 2, 64, 8, 64), np.float32)
    for d0 in range(14):
        for j in range(2):
            row = d0 + j
            blk = na_rpb[:, :, row, :][:, :, cidx]
            out[:, d0, j] = np.transpose(blk, (0, 2, 1, 3))
    return np.ascontiguousarray(out.reshape(L, 14, 128, 8, 64))


def make_in_maps(inp, ncores=8, ld=DEPTH):
    f = lambda a: np.ascontiguousarray(np.asarray(a, dtype=np.float32))
    cs, mask = _consts()
    shared = {
        "w_mod": f(inp["w_mod"]), "b_mod": f(inp["b_mod"]).reshape(DEPTH, 72, 128),
        "norm_g": f(inp["norm_g"]).reshape(DEPTH, 24, 128),
        "w_ffn_gate": f(inp["w_ffn_gate"]), "w_ffn_up": f(inp["w_ffn_up"]), "w_ffn_down": f(inp["w_ffn_down"]),
        "w_in": f(inp["w_in"]),
        "lru_conv_w": f(inp["lru_conv_w"]).reshape(DEPTH, 16, 128), "lru_conv_b": f(inp["lru_conv_b"]).reshape(DEPTH, 4, 128),
        "lru_w_r": f(inp["lru_w_r"]), "lru_b_r": f(inp["lru_b_r"]).reshape(DEPTH, 8, 128),
        "lru_w_i": f(inp["lru_w_i"]), "lru_b_i": f(inp["lru_b_i"]).reshape(DEPTH, 8, 128),
        "lru_lambda": f(inp["lru_lambda"]).reshape(DEPTH, 8, 128),
        "gqa_q_norm": f(inp["gqa_q_norm"]), "gqa_k_norm": f(inp["gqa_k_norm"]),
        "rpbT": _rpb_layout(f(inp["na_rpb"])), "namask": mask,
        "dn_conv_w": f(inp["dn_conv_w"]).reshape(DEPTH, 48, 128),
        "dn_a_log": f(inp["dn_a_log"]).reshape(DEPTH, 8), "dn_dt_bias": f(inp["dn_dt_bias"]).reshape(DEPTH, 8),
        "dn_norm_g": f(inp["dn_norm_g"]),
        "w_branch": f(inp["w_branch"]), "w_out": f(inp["w_out"]),
        "final_norm_g": f(inp["final_norm_g"]).reshape(8, 128), "rope_cs": cs, "dnmasks": _dnmasks(),
    }
    maps = []
    for c in range(ncores):
        s = c % 4
        m = dict(shared)
        m["xp"] = f(inp["x_prompt"][2 * c:2 * c + 2]).reshape(512, D)
        m["xs"] = f(inp["x_sample"][s])
        m["c2"] = np.stack([f(inp["c_ctx"]), f(inp["c"][s])], axis=0)
        m["cak"] = f(inp["cache_attn_k"][s])
        m["cav"] = f(inp["cache_attn_v"][s])
        m["cnk"] = f(inp["cache_na_k"][s])
        m["cnv"] = f(inp["cache_na_v"][s])
        m["slru"] = f(inp["state_lru"][s])
        m["sdn"] = f(inp["state_delta"][s])
        maps.append(m)
    if ld != DEPTH:
        for m in maps:
            for k in list(m.keys()):
                a = m[k]
                if k in ('cak','cav','cnk','cnv','slru','sdn') or (a.shape[0] == DEPTH and k not in ('xp','xs','c2','namask','rope_cs','final_norm_g','dnmasks')):
                    m[k] = np.ascontiguousarray(a[:ld])
    return maps


_NC_CACHE = {}


def kernel(**inputs):
    if "nc" not in _NC_CACHE:
        _NC_CACHE["nc"] = build({})[0]
    nc = _NC_CACHE["nc"]
    maps = make_in_maps(inputs)
    res = run_bass_kernel_spmd(nc, maps, core_ids=list(range(8)))
    R = res.results
    y_p = np.concatenate([R[c]["y_p"].reshape(2, 256, D) for c in range(8)], axis=0)
    y_s = np.stack([R[c]["y_s"] for c in range(4)], axis=0)
    cat = lambda k: np.concatenate([R[c][k] for c in range(8)], axis=0)
    return (y_p.astype(np.float32), y_s.astype(np.float32), cat("new_ak"), cat("new_av"), cat("new_nk"),
            cat("new_nv"), cat("new_lru"), cat("new_dn"))
```

```python
import contextlib
import math
import numpy as np
import concourse.bass as bass
import concourse.mybir as mybir
from concourse.bass_utils import run_bass_kernel_spmd

F32 = mybir.dt.float32
BF16 = mybir.dt.bfloat16
AF = mybir.ActivationFunctionType
ALU = mybir.AluOpType
AX = mybir.AxisListType

SAME_ENGINE_SYNC = True

DEPTH = 4
D = 1024
KC = 8
DFF = 2816
JC = 22
NTOK = 2560
TT = 512
NTT = 5
NIN = 9488
EPS = 1e-6
SEQS = [(0, 256, True, 0), (256, 256, True, 1), (512, 2048, False, 0)]
OFF = {}
_o = 0
for _n, _w in [("ax", 512), ("ay", 512), ("bq", 512), ("bk", 128), ("bv", 128), ("cq", 512), ("ck", 512),
               ("cv", 512), ("dq", 512), ("dk", 512), ("dv", 512), ("dz", 512), ("db", 8), ("da", 8), ("g", 4096)]:
    OFF[_n] = _o
    _o += _w
NEG = -30000.0


class Tr:
    __slots__ = ("w", "r", "x")

    def __init__(self, fence=()):
        self.w = None
        self.r = list(fence)
        self.x = False


class Tile:
    def __init__(self, t, name, fence=()):
        self.t = t
        self.name = name
        self.fence = list(fence)
        self.tr = Tr(self.fence)
        self.sub = {}

    def __getitem__(self, k):
        return self.t[k]

    def s(self, key):
        tr = self.sub.get(key)
        if tr is None:
            tr = self.sub[key] = Tr(self.fence)
        return tr

    def all_ops(self):
        out = set()
        for tr in [self.tr] + list(self.sub.values()):
            if tr.w is not None:
                out.add(tr.w)
            out.update(tr.r)
        return out


class Op:
    __slots__ = ("eng", "fn", "deps", "dma", "sig", "sigidx", "dsem", "dval", "id")


def _trs(lst):
    out = []
    for x in lst:
        if x is None:
            continue
        if isinstance(x, Tile):
            out.append(x.tr)
        elif isinstance(x, Tr):
            out.append(x)
        else:
            out.extend(_trs(x))
    return out


class Prog:
    ENGS = ("pe", "act", "dve", "pool", "sp")

    def __init__(self, ndma_sems=8):
        self.nc = bass.Bass("TRN2", target_bir_lowering=False)
        self.es = contextlib.ExitStack()
        self.ops = []
        self.by_eng = {e: [] for e in self.ENGS}
        self.ndma = ndma_sems
        self.dma_count = {e: 0 for e in self.ENGS}

    def sb(self, name, shape, dtype):
        t = self.es.enter_context(self.nc.sbuf_tensor(name, list(shape), dtype))
        return Tile(t, name)

    def ps(self, name, shape, dtype):
        t = self.es.enter_context(self.nc.psum_tensor(name, list(shape), dtype))
        tl = Tile(t, name)
        tl.tr.x = True
        return tl

    def dram(self, name, shape, dtype, kind="Internal"):
        t = self.nc.dram_tensor(name, list(shape), dtype, kind=kind)
        return Tile(t, name)

    def _add(self, eng, fn, r, w, dma):
        op = Op()
        op.eng = eng
        op.fn = fn
        op.dma = dma
        op.sig = False
        op.sigidx = None
        op.dsem = None
        op.dval = None
        op.id = len(self.ops)
        deps = set()
        rt = _trs(r)
        wt = _trs(w)
        xr = [t for t in rt if t.x]
        if xr:
            rt = [t for t in rt if not t.x]
            wt = wt + [t for t in xr if t not in wt]
        for tr in rt:
            if tr.w is not None:
                deps.add(tr.w)
        for tr in wt:
            if tr.w is not None:
                deps.add(tr.w)
            deps.update(tr.r)
        deps.discard(op.id)
        last = {}
        red = set()
        for d in deps:
            p = self.ops[d]
            if p.dma:
                red.add(d)
            elif last.get(p.eng, -1) < d:
                last[p.eng] = d
        red.update(last.values())
        op.deps = red
        for tr in rt:
            tr.r.append(op.id)
        for tr in wt:
            tr.w = op.id
            tr.r = []
        self.ops.append(op)
        self.by_eng[eng].append(op)
        if dma:
            j = self.dma_count[eng]
            self.dma_count[eng] = j + 1
            op.dsem = (eng, j % self.ndma)
            op.dval = 16 * (j // self.ndma + 1)
        return op

    def pe(self, fn, r=(), w=()):
        return self._add("pe", fn, r, w, False)

    def act(self, fn, r=(), w=()):
        return self._add("act", fn, r, w, False)

    def dve(self, fn, r=(), w=()):
        return self._add("dve", fn, r, w, False)

    def pool(self, fn, r=(), w=()):
        return self._add("pool", fn, r, w, False)

    def dma(self, q, out, in_, r=(), w=(), **kw):
        return self._add(q, lambda e: e.dma_start(out=out, in_=in_, **kw), r, w, True)

    def finish(self):
        nc = self.nc
        ops = self.ops
        for op in ops:
            for d in op.deps:
                p = ops[d]
                if p.dma:
                    continue
                if p.eng == op.eng and (op.eng == "pe" or not SAME_ENGINE_SYNC) and not op.dma:
                    continue
                p.sig = True
        for e in self.ENGS:
            c = 0
            for op in self.by_eng[e]:
                if op.sig:
                    c += 1
                    op.sigidx = c
        es = self.es
        csem = {e: es.enter_context(nc.semaphore(f"c_{e}")) for e in self.ENGS}
        dsem = {}
        for e in self.ENGS:
            if self.dma_count[e]:
                for i in range(self.ndma):
                    dsem[(e, i)] = es.enter_context(nc.semaphore(f"d_{e}{i}"))
        block = es.enter_context(nc.Block())
        ndma = self.ndma
        nwaits = [0]

        def gen(ename):
            def body(eng):
                waited = {}

                def wait(sem, val):
                    if waited.get(sem, 0) >= val:
                        return
                    waited[sem] = val
                    eng.wait_ge(sem, val)
                    nwaits[0] += 1

                for op in self.by_eng[ename]:
                    for d in sorted(op.deps):
                        p = ops[d]
                        if p.dma:
                            wait(dsem[p.dsem], p.dval)
                        else:
                            if p.eng == ename and (ename == "pe" or not SAME_ENGINE_SYNC) and not op.dma:
                                continue
                            wait(csem[p.eng], p.sigidx)
                    if op.dma and op.dval > 16:
                        wait(dsem[op.dsem], op.dval - 16)
                    ins = op.fn(eng)
                    if op.dma:
                        ins.then_inc(dsem[op.dsem], 16)
                    elif op.sig:
                        ins.then_inc(csem[ename], 1)
                if self.dma_count[ename]:
                    n = self.dma_count[ename]
                    for i in range(min(ndma, n)):
                        cnt = (n - 1 - i) // ndma + 1
                        wait(dsem[(ename, i)], 16 * cnt)
            return body

        block.tensor(gen("pe"))
        block.scalar(gen("act"))
        block.vector(gen("dve"))
        block.gpsimd(gen("pool"))
        block.sync(gen("sp"))
        es.close()
        self.nwaits = nwaits[0]
        return nc


class Ring:
    def __init__(self, tiles):
        self.tiles = tiles
        self.i = 0

    def next(self):
        t = self.tiles[self.i % len(self.tiles)]
        self.i += 1
        return t


class Arena:
    def __init__(self, P, nwords):
        self.P = P
        self.t = P.es.enter_context(P.nc.sbuf_tensor("arena", [128, nwords], F32))
        self.nwords = nwords
        self.top = 0
        self.allocs = []
        self.peak = 0

    def mark(self):
        return self.top

    def release(self, m):
        self.top = m

    def alloc(self, name, fshape, dtype, parts=128):
        fshape = list(fshape)
        n = int(np.prod(fshape))
        nw = n if dtype == F32 else (n + 1) // 2
        nw = (nw + 7) // 8 * 8
        lo = self.top
        hi = lo + nw
        assert hi <= self.nwords, f"arena overflow {name} {hi}>{self.nwords}"
        self.top = hi
        self.peak = max(self.peak, hi)
        ap = self.t[0:parts, lo:hi]
        if dtype != F32:
            ap = ap.bitcast(dtype)
        ap = ap[:, 0:n]
        if len(fshape) == 2:
            ap = ap.rearrange("p (a b) -> p a b", a=fshape[0], b=fshape[1])
        elif len(fshape) == 3:
            ap = ap.rearrange("p (a b c) -> p a b c", a=fshape[0], b=fshape[1], c=fshape[2])
        elif len(fshape) == 4:
            ap = ap.rearrange("p (a b c d) -> p a b c d", a=fshape[0], b=fshape[1], c=fshape[2], d=fshape[3])
        fence = set()
        keep = []
        for (l2, h2, tl) in self.allocs:
            if l2 < hi and lo < h2:
                fence |= tl.all_ops()
                if lo <= l2 and h2 <= hi:
                    continue
            keep.append((l2, h2, tl))
        self.allocs = keep
        tile = Tile(ap, name, fence)
        self.allocs.append((lo, hi, tile))
        return tile

    def ring(self, name, fshape, dtype, n, parts=128):
        return Ring([self.alloc(f"{name}{i}", fshape, dtype, parts) for i in range(n)])


def build(cfg):
    nl = cfg.get("nl", DEPTH)
    LD = cfg.get("ld", DEPTH)
    dbg = cfg.get("dbg", ())
    do_mixer = cfg.get("mixer", True)
    do_ffn = cfg.get("ffn", True)
    branches = cfg.get("branches", "ABCD")
    P = Prog()
    nc = P.nc
    A = Arena(P, cfg.get("arena_words", 50000))

    def din(name, shape):
        return P.dram(name, shape, F32, kind="ExternalInput")

    def dout(name, shape):
        return P.dram(name, shape, F32, kind="ExternalOutput")

    xp_d = din("xp", [512, D])
    xs_d = din("xs", [2048, D])
    c2_d = din("c2", [2, D])
    cak_d = din("cak", [LD, 512, 2, 64])
    cav_d = din("cav", [LD, 512, 2, 64])
    cnk_d = din("cnk", [LD, 512, 8, 64])
    cnv_d = din("cnv", [LD, 512, 8, 64])
    slru_d = din("slru", [LD, 2, 512])
    sdn_d = din("sdn", [LD, 2, 4, 128, 128])
    w_mod = din("w_mod", [LD, D, 9 * D])
    b_mod = din("b_mod", [LD, 72, 128])
    norm_g = din("norm_g", [LD, 24, 128])
    w_g = din("w_ffn_gate", [LD, 2, D, DFF])
    w_u = din("w_ffn_up", [LD, 2, D, DFF])
    w_dn = din("w_ffn_down", [LD, 2, DFF, D])
    w_in = din("w_in", [LD, D, NIN])
    lru_cw = din("lru_conv_w", [LD, 16, 128])
    lru_cb = din("lru_conv_b", [LD, 4, 128])
    lru_wr = din("lru_w_r", [LD, 2, 8, 64, 64])
    lru_br = din("lru_b_r", [LD, 8, 128])
    lru_wi = din("lru_w_i", [LD, 2, 8, 64, 64])
    lru_bi = din("lru_b_i", [LD, 8, 128])
    lru_lam = din("lru_lambda", [LD, 8, 128])
    gqa_qn = din("gqa_q_norm", [LD, 64])
    gqa_kn = din("gqa_k_norm", [LD, 64])
    rpbT = din("rpbT", [LD, 14, 128, 8, 64])
    namask = din("namask", [128, 64])
    dn_cw = din("dn_conv_w", [LD, 48, 128])
    dn_alog = din("dn_a_log", [LD, 8])
    dn_dtb = din("dn_dt_bias", [LD, 8])
    dn_ng = din("dn_norm_g", [LD, 128])
    w_br = din("w_branch", [LD, 4, 512, D])
    w_out = din("w_out", [LD, D, D])
    fin_g = din("final_norm_g", [8, 128])
    rope_cs = din("rope_cs", [2048, 2, 64])
    dnm = din("dnmasks", [6, 128, 128])

    yp_d = dout("y_p", [512, D])
    ys_d = dout("y_s", [2048, D])
    nak_d = dout("new_ak", [2, LD, 256, 2, 64])
    nav_d = dout("new_av", [2, LD, 256, 2, 64])
    nnk_d = dout("new_nk", [2, LD, 256, 8, 64])
    nnv_d = dout("new_nv", [2, LD, 256, 8, 64])
    nlru_d = dout("new_lru", [2, LD, 2, 512])
    ndn_d = dout("new_dn", [2, LD, 2, 4, 128, 128])

    def scr(name, shape, dtype):
        return P.dram(name, shape, dtype, kind=("ExternalOutput" if name in dbg else "Internal"))

    xT = scr("xT", [D, NTOK], F32)
    actd = scr("actd", [DFF, NTOK], BF16)
    oT = scr("oT", [4, 512, NTOK], BF16)

    banks = [P.ps(f"bank{i}", [128, 512], F32) for i in range(8)]
    ident = P.sb("ident", [128, 128], F32)
    ones_bf = P.sb("ones_bf", [128, 128], BF16)
    ones_f = P.sb("ones_f", [128, 128], F32)
    mods = P.sb("mods", [128, DEPTH * 9 * 8 * 2], F32)
    ng_sb = P.sb("ng_sb", [128, DEPTH * 24], F32)
    fing_sb = P.sb("fing_sb", [128, 8], F32)
    silc = P.sb("silc", [128, 16], BF16)
    eff = P.sb("eff", [128, 2 * 8 * 2], F32)
    eff_ring = Ring([P.sb(f"effr{i}", [128, 8 * 2 * 2], F32) for i in range(4)])

    def modv(l, i, k, g):
        o = ((l * 9 + i) * 8 + k) * 2 + g
        return mods[:, o:o + 1]

    def mm(out, lhsT, rhs, start, stop, r, w, **kw):
        P.pe(lambda e: e.matmul(out, lhsT=lhsT, rhs=rhs, start=start, stop=stop, **kw), r=r, w=w)

    def tp(out, in_, idn, r, w):
        P.pe(lambda e: e.transpose(out=out, in_=in_, identity=idn), r=r, w=w)

    def actf(out, in_, func, r, w, bias=None, scale=None):
        kw = {}
        if bias is not None:
            kw["bias"] = bias
        if scale is not None:
            kw["scale"] = scale
        P.act(lambda e: e.activation(out=out, in_=in_, func=func, **kw), r=r, w=w)

    def tt(eng, out, in0, in1, op, r, w):
        getattr(P, eng)(lambda e: e.tensor_tensor(out=out, in0=in0, in1=in1, op=op), r=r, w=w)

    def ts(eng, out, in0, s1, s2, op0, op1, r, w):
        if s2 is None:
            getattr(P, eng)(lambda e: e.tensor_scalar(out=out, in0=in0, scalar1=s1, scalar2=None, op0=op0), r=r, w=w)
        else:
            getattr(P, eng)(lambda e: e.tensor_scalar(out=out, in0=in0, scalar1=s1, scalar2=s2, op0=op0, op1=op1), r=r, w=w)

    def stt(eng, out, in0, scalar, in1, op0, op1, r, w):
        getattr(P, eng)(lambda e: e.scalar_tensor_tensor(out=out, in0=in0, scalar=scalar, in1=in1, op0=op0, op1=op1), r=r, w=w)

    def cp(eng, out, in_, r, w):
        if eng == "act":
            P.act(lambda e: e.copy(out=out, in_=in_), r=r, w=w)
        else:
            getattr(P, eng)(lambda e: e.tensor_copy(out=out, in_=in_), r=r, w=w)

    def memset(eng, ap, val, w):
        getattr(P, eng)(lambda e: e.memset(ap, val), w=w)

    def pe_warm(n):
        for wi_ in range(n):
            bk_ = banks[wi_ % 8]
            mm(bk_[:, 0:512], ones_bf[:], h_all[:, 0, 0:512], True, True, [ones_bf], [bk_])

    def run_rr(gens):
        alive = list(gens)
        while alive:
            for g_ in list(alive):
                try:
                    next(g_)
                except StopIteration:
                    alive.remove(g_)

    memset("pool", ident[:], 0.0, [ident])
    P.pool(lambda e: e.affine_select(out=ident[:], in_=ident[:], pattern=[[-1, 128]], compare_op=ALU.not_equal,
                                     fill=1.0, base=0, channel_multiplier=1), r=[ident], w=[ident])
    memset("pool", ones_bf[:], 1.0, [ones_bf])
    memset("pool", ones_f[:], 1.0, [ones_f])

    stage_r = Ring([P.sb(f"vstage{i}", [128, 128], F32) for i in range(2)])
    m0 = A.mark()

    def load_rowsT(dst_ap, dst_tile, src_ap, src_tile, R, bank, C=128):
        st = stage_r.next()
        P.dma("sp", st[0:R, 0:C], src_ap, r=[src_tile], w=[st])
        tp(bank[0:C, 0:R], st[0:R, 0:C], ident[0:R, 0:R], [st, ident], [bank])
        cp("dve", dst_ap, bank[0:C, 0:R], [bank], [dst_tile])

    xin_r = A.ring("xin", [D], F32, 2)
    xst_r = A.ring("xst", [8, 128], F32, 2)
    for tc in range(20):
        src = xp_d[tc * 128:(tc + 1) * 128, :] if tc < 4 else xs_d[(tc - 4) * 128:(tc - 3) * 128, :]
        srct = xp_d if tc < 4 else xs_d
        xi = xin_r.next()
        P.dma("sp", xi[:], src, r=[srct], w=[xi])
        xs_ = xst_r.next()
        for half in range(2):
            bk = banks[(tc * 2 + half) % 4]
            for q in range(4):
                k = half * 4 + q
                tp(bk[:, q * 128:(q + 1) * 128], xi[:, k * 128:(k + 1) * 128], ident[:], [xi, ident], [bk])
            cp("dve" if half == 0 else "act", xs_[:, half * 4:(half + 1) * 4, :],
               bk[:].rearrange("p (a b) -> p a b", a=4, b=128), [bk], [xs_])
        P.dma("sp", xT.t.rearrange("(k p) t -> p k t", p=128)[:, :, tc * 128:(tc + 1) * 128], xs_[:],
              r=[xs_], w=[xT.s(tc // 4)])

    csb = A.alloc("csb", [16], F32)
    load_rowsT(csb[:], csb, c2_d.t.rearrange("g (k p) -> (g k) p", p=128), c2_d, 16, banks[4])
    actf(silc[:], csb[:], AF.Silu, [csb], [silc])
    for l in range(nl):
        load_rowsT(ng_sb[:, l * 24:(l + 1) * 24], ng_sb, norm_g[l], norm_g, 24, banks[5])
    load_rowsT(fing_sb[:], fing_sb, fin_g[:], fin_g, 8, banks[5])
    def mods_gen(l, bm_sb, wm_r):
        load_rowsT(bm_sb[:], bm_sb, b_mod[l], b_mod, 72, banks[5])
        for i in range(9):
            wt = wm_r.next()
            P.dma("pool", wt[:], w_mod[l].rearrange("(kc p) n -> p kc n", p=128)[:, :, i * 1024:(i + 1) * 1024],
                  r=[w_mod], w=[wt])
            bk = banks[6 + (i % 2)]
            for m in range(8):
                for k in range(8):
                    mm(bk[:, m * 2:m * 2 + 2], wt[:, k, m * 128:(m + 1) * 128],
                       silc[:].rearrange("p (g k) -> p k g", g=2)[:, k, :], k == 0, k == 7, [wt, silc], [bk])
            o = (l * 9 + i) * 16
            for g in range(2):
                tt("dve", mods[:, o:o + 16].rearrange("p (k g) -> p k g", g=2)[:, :, g],
                   bk[:, 0:16].rearrange("p (k g) -> p k g", g=2)[:, :, g], bm_sb[:, i * 8:(i + 1) * 8], ALU.add,
                   [bk, bm_sb], [mods])
            yield

    bm_sb0 = A.alloc("bm_sb", [72], F32)
    wm_r0 = A.ring("wm", [8, 1024], BF16, 2)
    for _ in mods_gen(0, bm_sb0, wm_r0):
        pass
    A.release(m0)

    h_all = A.alloc("h_all", [KC, NTOK], BF16)
    m_base = A.mark()

    def norm_prep(l, i):
        e_t = eff_ring.next()
        o_s = (l * 9 + 3 * i + 1) * 16
        ts("dve", e_t[:, 0:16], mods[:, o_s:o_s + 16], 1.0, None, ALU.add, None, [mods], [e_t])
        for g in range(2):
            v = e_t[:, 0:16].rearrange("p (k g) -> p k g", g=2)[:, :, g]
            tt("dve", v, v, ng_sb[:, l * 24 + i * 8:l * 24 + i * 8 + 8], ALU.mult, [e_t, ng_sb], [e_t])
        return e_t

    def norm_rings():
        return (A.ring("n_sq", [KC, TT], BF16, 2), A.ring("n_rs", [TT], F32, 2), A.ring("n_tm", [TT], F32, 3))

    def norm_tile(l, i, e_t, t_, xt, rings):
        sq_r, rs_r, tm_r = rings
        g = 0 if t_ == 0 else 1
        cs = slice(t_ * TT, (t_ + 1) * TT)
        sq = sq_r.next()
        actf(sq[:], xt[:], AF.Square, [xt], [sq])
        bk = banks[t_ % 2]
        for k in range(KC):
            mm(bk[:], ones_bf[:], sq[:, k, :], k == 0, k == KC - 1, [ones_bf, sq], [bk])
        rs = rs_r.next()
        ts("dve", rs[:], bk[:], 1.0 / D, EPS, ALU.mult, ALU.add, [bk], [rs])
        actf(rs[:], rs[:], AF.Sqrt, [rs], [rs])
        P.dve(lambda e, rs=rs: e.reciprocal(out=rs[:], in_=rs[:]), r=[rs], w=[rs])
        for k in range(KC):
            tm = tm_r.next()
            tt("dve", tm[:], xt[:, k, :], rs[:], ALU.mult, [xt, rs], [tm])
            actf(h_all[:, k, cs], tm[:], AF.Identity, [tm, e_t, mods], [h_all.s(t_)],
                 bias=modv(l, 3 * i, k, g), scale=e_t[:, k * 2 + g:k * 2 + g + 1])

    def norm_stage(l, i):
        m = A.mark()
        e_t = norm_prep(l, i)
        xt_r = A.ring("n_xt", [KC, TT], F32, 2)
        rings = norm_rings()
        for t_ in range(NTT):
            cs = slice(t_ * TT, (t_ + 1) * TT)
            xt = xt_r.next()
            P.dma("sp", xt[:], xT.t.rearrange("(k p) t -> p k t", p=128)[:, :, cs], r=[xT.s(t_)], w=[xt])
            norm_tile(l, i, e_t, t_, xt, rings)
        A.release(m)

    def ffn_stage(l, f, gi, next_norm=None):
        m0_ = A.mark()
        wd = A.alloc("f_wd", [JC, D], BF16)
        for j0 in range(0, JC, 6):
            nj = min(6, JC - j0)
            P.dma("pool", wd[:, j0:j0 + nj, :], w_dn[l, f].rearrange("(j p) n -> p j n", p=128)[:, j0:j0 + nj, :],
                  r=[w_dn], w=[wd.s(j0)])
        wd_trs = [wd.s(j0) for j0 in range(0, JC, 6)]
        m = A.mark()
        mg_ = None
        if f == 1 and l + 1 < nl:
            mg_ = mods_gen(l + 1, A.alloc("bm_sbn", [72], F32), A.ring("wmn", [8, 1024], BF16, 2))
        e_t = eff_ring.next()
        o_g = (l * 9 + gi) * 16
        ts("dve", e_t[:, 16:32], mods[:, o_g:o_g + 16], 0.5, None, ALU.mult, None, [mods], [e_t])
        wg_r = A.ring("f_wg", [KC, 512], BF16, 2)
        wu_r = A.ring("f_wu", [KC, 512], BF16, 2)
        sg_r = A.ring("f_sg", [TT], F32, 2)
        ao_r = A.ring("f_ao", [TT], BF16, 4)
        bi = 0
        for j0 in range(0, JC, 4):
            nj = min(4, JC - j0)
            wg = wg_r.next()
            wu = wu_r.next()
            P.dma("pool", wg[:, :, 0:nj * 128], w_g[l, f].rearrange("(kc p) n -> p kc n", p=128)[:, :, j0 * 128:(j0 + nj) * 128],
                  r=[w_g], w=[wg])
            P.dma("pool", wu[:, :, 0:nj * 128], w_u[l, f].rearrange("(kc p) n -> p kc n", p=128)[:, :, j0 * 128:(j0 + nj) * 128],
                  r=[w_u], w=[wu])
            for t_ in range(NTT):
                cs = slice(t_ * TT, (t_ + 1) * TT)
                for jj in range(nj):
                    j = j0 + jj
                    pg = banks[(bi * 2) % 8]
                    pu = banks[(bi * 2 + 1) % 8]
                    bi += 1
                    for k in range(KC):
                        mm(pg[:], wg[:, k, jj * 128:(jj + 1) * 128], h_all[:, k, cs], k == 0, k == KC - 1,
                           [wg, h_all.s(t_)], [pg])
                    for k in range(KC):
                        mm(pu[:], wu[:, k, jj * 128:(jj + 1) * 128], h_all[:, k, cs], k == 0, k == KC - 1,
                           [wu, h_all.s(t_)], [pu])
                    sg = sg_r.next()
                    actf(sg[:], pg[:], AF.Silu, [pg], [sg])
                    ao = ao_r.next()
                    tt("dve", ao[:], sg[:], pu[:], ALU.mult, [sg, pu], [ao])
                    P.dma("sp", actd[j * 128:(j + 1) * 128, cs], ao[:], r=[ao], w=[actd.s((j, t_))])
            if mg_ is not None:
                for _ in range(2):
                    next(mg_, None)
        if mg_ is not None:
            for _ in mg_:
                pass
        A.release(m)
        m = A.mark()
        ai_r = A.ring("f_ai", [JC, TT], BF16, 2)
        xt_r = A.ring("f_xt", [KC, TT], F32, 2)
        if next_norm is not None:
            ne_t = norm_prep(*next_norm)
            nrings = norm_rings()
        for t_ in range(NTT):
            g = 0 if t_ == 0 else 1
            cs = slice(t_ * TT, (t_ + 1) * TT)
            ai = ai_r.next()
            P.dma("sp", ai[:], actd.t.rearrange("(j p) t -> p j t", p=128)[:, :, cs],
                  r=[actd.s((j, t_)) for j in range(JC)], w=[ai])
            xt = xt_r.next()
            P.dma("sp", xt[:], xT.t.rearrange("(k p) t -> p k t", p=128)[:, :, cs], r=[xT.s(t_)], w=[xt])
            for mo in range(KC):
                bk = banks[2 + (t_ * KC + mo) % 6]
                for j in range(JC):
                    mm(bk[:], wd[:, j, mo * 128:(mo + 1) * 128], ai[:, j, :], j == 0, j == JC - 1,
                       [wd_trs, ai], [bk])
                stt("dve", xt[:, mo, :], bk[:], e_t[:, 16 + mo * 2 + g:16 + mo * 2 + g + 1], xt[:, mo, :],
                    ALU.mult, ALU.add, [bk, e_t, xt], [xt])
            P.dma("sp", xT.t.rearrange("(k p) t -> p k t", p=128)[:, :, cs], xt[:], r=[xt], w=[xT.s(t_)])
            if next_norm is not None:
                norm_tile(next_norm[0], next_norm[1], ne_t, t_, xt, nrings)
        A.release(m0_)

    def final_stage():
        m = A.mark()
        xt_r = A.ring("fn_xt", [KC, TT], F32, 2)
        sq_r = A.ring("fn_sq", [KC, TT], BF16, 2)
        rs_r = A.ring("fn_rs", [TT], F32, 2)
        yo_r = A.ring("fn_yo", [D], F32, 2)
        for t_ in range(NTT):
            cs = slice(t_ * TT, (t_ + 1) * TT)
            xt = xt_r.next()
            P.dma("sp", xt[:], xT.t.rearrange("(k p) t -> p k t", p=128)[:, :, cs], r=[xT.s(t_)], w=[xt])
            sq = sq_r.next()
            actf(sq[:], xt[:], AF.Square, [xt], [sq])
            bk = banks[t_ % 2]
            for k in range(KC):
                mm(bk[:], ones_bf[:], sq[:, k, :], k == 0, k == KC - 1, [ones_bf, sq], [bk])
            rs = rs_r.next()
            ts("dve", rs[:], bk[:], 1.0 / D, EPS, ALU.mult, ALU.add, [bk], [rs])
            actf(rs[:], rs[:], AF.Sqrt, [rs], [rs])
            P.dve(lambda e, rs=rs: e.reciprocal(out=rs[:], in_=rs[:]), r=[rs], w=[rs])
            for k in range(KC):
                stt("dve", xt[:, k, :], xt[:, k, :], fing_sb[:, k:k + 1], rs[:], ALU.mult, ALU.mult,
                    [xt, fing_sb, rs], [xt])
            for c4 in range(4):
                tc = t_ * 4 + c4
                yo = yo_r.next()
                for half in range(2):
                    bk2 = banks[2 + (tc * 2 + half) % 4]
                    for q in range(4):
                        k = half * 4 + q
                        tp(bk2[:, q * 128:(q + 1) * 128], xt[:, k, c4 * 128:(c4 + 1) * 128], ident[:], [xt, ident], [bk2])
                    cp("act" if half == 0 else "dve", yo[:, half * 512:(half + 1) * 512], bk2[:], [bk2], [yo])
                if tc < 4:
                    P.dma("sp", yp_d[tc * 128:(tc + 1) * 128, :], yo[:], r=[yo], w=[yp_d.s(tc)])
                else:
                    P.dma("sp", ys_d[(tc - 4) * 128:(tc - 3) * 128, :], yo[:], r=[yo], w=[ys_d.s(tc)])
        A.release(m)

    def w_in_ap(l, c0, c1):
        return w_in[l].rearrange("(kc p) n -> p kc n", p=128)[:, :, c0:c1]

    def zero_branch(n):
        m = A.mark()
        z = A.alloc("zb", [NTOK], BF16)
        memset("dve", z[:], 0.0, [z])
        for c in range(4):
            P.dma("sp", oT[n, c * 128:(c + 1) * 128, :], z[:], r=[z], w=[oT.s((n, t_)) for t_ in range(NTT)])
        A.release(m)

    def otr(n, lo, hi):
        return [oT.s((n, t_)) for t_ in range(lo // TT, (hi - 1) // TT + 1)]

    def merge_stage(l, next_norm=None):
        m = A.mark()
        mgall = A.alloc("mg_all", [KC, NTOK], BF16)
        m2 = A.mark()
        wg_r = A.ring("mg_wg", [KC, 4, 128], BF16, 2)
        wb_r = A.ring("mg_wb", [16, 128], BF16, 2)
        ot_r = A.ring("mg_ot", [16, TT], BF16, 2)
        sg_r = A.ring("mg_sg", [TT], F32, 3)
        ac_r = A.ring("mg_ac", [TT], F32, 2)
        bi = 0
        for mo in range(KC):
            wg = wg_r.next()
            for n in range(4):
                c0 = OFF["g"] + n * 1024 + mo * 128
                P.dma("pool", wg[:, :, n, :], w_in_ap(l, c0, c0 + 128), r=[w_in], w=[wg])
            wb = wb_r.next()
            P.dma("pool", wb[:], w_br[l].rearrange("n (kc p) d -> p (n kc) d", p=128)[:, :, mo * 128:(mo + 1) * 128],
                  r=[w_br], w=[wb])
            for t_ in range(NTT):
                cs = slice(t_ * TT, (t_ + 1) * TT)
                ot = ot_r.next()
                P.dma("sp", ot[:], oT.t.rearrange("n (kc p) t -> p (n kc) t", p=128)[:, :, cs],
                      r=[oT.s((n, t_)) for n in range(4)], w=[ot])
                ac = ac_r.next()
                for n in range(4):
                    pg = banks[(bi * 2) % 8]
                    pb = banks[(bi * 2 + 1) % 8]
                    bi += 1
                    for k in range(KC):
                        mm(pg[:], wg[:, k, n, :], h_all[:, k, cs], k == 0, k == KC - 1, [wg, h_all.s(t_)], [pg])
                    for kc in range(4):
                        mm(pb[:], wb[:, n * 4 + kc, :], ot[:, n * 4 + kc, :], kc == 0, kc == 3, [wb, ot], [pb])
                    sg = sg_r.next()
                    actf(sg[:], pg[:], AF.Sigmoid, [pg], [sg])
                    if n == 0:
                        tt("dve", ac[:], sg[:], pb[:], ALU.mult, [sg, pb], [ac])
                    else:
                        tt("dve", sg[:], sg[:], pb[:], ALU.mult, [sg, pb], [sg])
                        if n < 3:
                            tt("dve", ac[:], ac[:], sg[:], ALU.add, [ac, sg], [ac])
                        else:
                            tt("dve", mgall[:, mo, cs], ac[:], sg[:], ALU.add, [ac, sg], [mgall.s(t_)])
        A.release(m2)
        wo = A.alloc("mg_wo", [KC, D], BF16)
        P.dma("pool", wo[:], w_out[l].rearrange("(kc p) n -> p kc n", p=128), r=[w_out], w=[wo])
        xt_r = A.ring("mg_xt", [KC, TT], F32, 2)
        if next_norm is not None:
            ne_t = norm_prep(*next_norm)
            nrings = norm_rings()
        for t_ in range(NTT):
            g = 0 if t_ == 0 else 1
            cs = slice(t_ * TT, (t_ + 1) * TT)
            xt = xt_r.next()
            P.dma("sp", xt[:], xT.t.rearrange("(k p) t -> p k t", p=128)[:, :, cs], r=[xT.s(t_)], w=[xt])
            for mo in range(KC):
                bk = banks[2 + (t_ * KC + mo) % 6]
                for k in range(KC):
                    mm(bk[:], wo[:, k, mo * 128:(mo + 1) * 128], mgall[:, k, cs], k == 0, k == KC - 1,
                       [wo, mgall.s(t_)], [bk])
                stt("dve", xt[:, mo, :], bk[:], modv(l, 5, mo, g), xt[:, mo, :], ALU.mult, ALU.add,
                    [bk, mods, xt], [xt])
            P.dma("sp", xT.t.rearrange("(k p) t -> p k t", p=128)[:, :, cs], xt[:], r=[xt], w=[xT.s(t_)])
            if next_norm is not None:
                norm_tile(next_norm[0], next_norm[1], ne_t, t_, xt, nrings)
        A.release(m)

    PADW = 2569
    PBASE = [0, 259, 518]

    def padcol(tok):
        if tok < 256:
            return PBASE[0] + 2 + tok
        if tok < 512:
            return PBASE[1] + 2 + tok - 256
        return PBASE[2] + 2 + tok - 512

    def evac_padded(dst, dst_tr, bank, t_, eng):
        if t_ == 0:
            cp(eng, dst[:, padcol(0):padcol(0) + 256], bank[:, 0:256], [bank], [dst_tr])
            cp(eng, dst[:, padcol(256):padcol(256) + 256], bank[:, 256:512], [bank], [dst_tr])
        else:
            c = padcol(t_ * TT)
            cp(eng, dst[:, c:c + TT], bank[:, :], [bank], [dst_tr])

    def zero_pads(dst, dst_tr):
        for (t0, T, _, _), pb in zip(SEQS, PBASE):
            memset("pool", dst[:, pb:pb + 2], 0.0, [dst_tr])
            memset("pool", dst[:, pb + 2 + T:pb + 3 + T], 0.0, [dst_tr])

    def conv4(dst_fn, src, src_tr, wcol_fn, bias_ap, r_extra, w_tr):
        for si, ((t0, T, _, _), pb) in enumerate(zip(SEQS, PBASE)):
            d = dst_fn(si)
            if bias_ap is not None:
                actf(d, src[:, pb:pb + T], AF.Identity, [src_tr] + r_extra, [w_tr], bias=bias_ap, scale=wcol_fn(0))
            else:
                actf(d, src[:, pb:pb + T], AF.Identity, [src_tr] + r_extra, [w_tr], scale=wcol_fn(0))
            for j in range(1, 4):
                stt("dve", d, src[:, pb + j:pb + j + T], wcol_fn(j), d, ALU.mult, ALU.add, [src_tr, w_tr] + r_extra, [w_tr])

    def branch_A(l):
        m = A.mark()
        pr = A.alloc("a_par", [16 + 4 + 8 + 8 + 8 + 8 + 8], F32)
        load_rowsT(pr[:, 0:16], pr, lru_cw[l], lru_cw, 16, banks[7])
        load_rowsT(pr[:, 16:20], pr, lru_cb[l], lru_cb, 4, banks[7])
        load_rowsT(pr[:, 20:28], pr, lru_br[l], lru_br, 8, banks[7])
        load_rowsT(pr[:, 28:36], pr, lru_bi[l], lru_bi, 8, banks[7])
        load_rowsT(pr[:, 36:44], pr, lru_lam[l], lru_lam, 8, banks[7])
        load_rowsT(pr[:, 52:60], pr, slru_d[l].rearrange("d (c p) -> (d c) p", p=128), slru_d, 8, banks[7])
        actf(pr[:, 36:44], pr[:, 36:44], AF.Sigmoid, [pr], [pr])
        actf(pr[:, 36:44], pr[:, 36:44], AF.Ln, [pr], [pr])
        ts("dve", pr[:, 44:52], pr[:, 36:44], 16.0, None, ALU.mult, None, [pr], [pr])
        ts("dve", pr[:, 36:44], pr[:, 36:44], 8.0, None, ALU.mult, None, [pr], [pr])
        wxy_r = A.ring("a_w", [KC, 256], BF16, 2)
        wbd_r = A.ring("a_wbd", [4, 128], BF16, 2)
        axp = A.alloc("a_axp", [PADW], F32)
        zero_pads(axp, axp.tr)
        xa = A.alloc("a_xa", [NTOK], F32)
        xab = A.alloc("a_xab", [NTOK], BF16)
        gy = A.alloc("a_gy", [NTOK], F32)
        hsum = A.alloc("a_hs", [NTOK], F32)
        hdir = A.alloc("a_hd", [NTOK], F32)
        aa = A.alloc("a_aa", [NTOK], F32)
        uu = A.alloc("a_uu", [NTOK], F32)
        tmp_r = A.ring("a_tmp", [TT], F32, 3)
        ob_r = A.ring("a_ob", [NTOK], BF16, 1)
        fin_r = A.ring("a_fin", [2], F32, 2)
        for cch in range(4):
            wxy = wxy_r.next()
            P.dma("pool", wxy[:, :, 0:128], w_in_ap(l, OFF["ax"] + cch * 128, OFF["ax"] + (cch + 1) * 128), r=[w_in], w=[wxy])
            P.dma("pool", wxy[:, :, 128:256], w_in_ap(l, OFF["ay"] + cch * 128, OFF["ay"] + (cch + 1) * 128), r=[w_in], w=[wxy])
            wbd = wbd_r.next()
            memset("dve", wbd[:], 0.0, [wbd])
            for dr in range(2):
                for gi, wsrc in enumerate((lru_wr, lru_wi)):
                    for b2 in range(2):
                        P.dma("pool", wbd[b2 * 64:(b2 + 1) * 64, dr * 2 + gi, b2 * 64:(b2 + 1) * 64],
                              wsrc[l, dr, 2 * cch + b2], r=[wsrc], w=[wbd])
            for t_ in range(NTT):
                cs = slice(t_ * TT, (t_ + 1) * TT)
                px = banks[(t_ * 2) % 4]
                py = banks[(t_ * 2 + 1) % 4]
                for k in range(KC):
                    mm(px[:], wxy[:, k, 0:128], h_all[:, k, cs], k == 0, k == KC - 1, [wxy, h_all.s(t_)], [px])
                for k in range(KC):
                    mm(py[:], wxy[:, k, 128:256], h_all[:, k, cs], k == 0, k == KC - 1, [wxy, h_all.s(t_)], [py])
                evac_padded(axp, axp.tr, px, t_, "act")
                t1 = tmp_r.next()
                actf(t1[:], py[:], AF.Square, [py], [t1])
                ts("dve", t1[:], t1[:], 0.044715, 1.0, ALU.mult, ALU.add, [t1], [t1])
                tt("dve", t1[:], t1[:], py[:], ALU.mult, [t1, py], [t1])
                actf(t1[:], t1[:], AF.Sigmoid, [t1], [t1], scale=1.5957691216057308)
                tt("dve", gy[:, cs], t1[:], py[:], ALU.mult, [t1, py], [gy.s(t_)])
            conv4(lambda si: xa[:, SEQS[si][0]:SEQS[si][0] + SEQS[si][1]], axp, axp.tr,
                  lambda j: pr[:, j * 4 + cch:j * 4 + cch + 1], pr[:, 16 + cch:17 + cch], [pr], xa.tr)
            cp("act", xab[:], xa[:], [xa], [xab])
            for dr in range(2):
                ccol = pr[:, 36 + dr * 4 + cch:37 + dr * 4 + cch]
                c2col = pr[:, 44 + dr * 4 + cch:45 + dr * 4 + cch]
                for t_ in range(NTT):
                    cs = slice(t_ * TT, (t_ + 1) * TT)
                    pr_ = banks[4 + (t_ * 2) % 4]
                    pi_ = banks[4 + (t_ * 2 + 1) % 4]
                    mm(pr_[:], wbd[:, dr * 2 + 0, :], xab[:, cs], True, True, [wbd, xab], [pr_])
                    mm(pi_[:], wbd[:, dr * 2 + 1, :], xab[:, cs], True, True, [wbd, xab], [pi_])
                    rr = tmp_r.next()
                    actf(rr[:], pr_[:], AF.Sigmoid, [pr_, pr], [rr], bias=pr[:, 20 + dr * 4 + cch:21 + dr * 4 + cch])
                    ii = tmp_r.next()
                    actf(ii[:], pi_[:], AF.Sigmoid, [pi_, pr], [ii], bias=pr[:, 28 + dr * 4 + cch:29 + dr * 4 + cch])
                    actf(aa[:, cs], rr[:], AF.Exp, [rr, pr], [aa.s(t_)], scale=ccol)
                    actf(rr[:], rr[:], AF.Exp, [rr, pr], [rr], scale=c2col)
                    ts("dve", rr[:], rr[:], -1.0, 1.0, ALU.mult, ALU.add, [rr], [rr])
                    actf(rr[:], rr[:], AF.Sqrt, [rr], [rr])
                    tt("dve", ii[:], ii[:], xa[:, cs], ALU.mult, [ii, xa], [ii])
                    tt("dve", uu[:, cs], ii[:], rr[:], ALU.mult, [ii, rr], [uu.s(t_)])
                aatr = [aa.s(t_) for t_ in range(NTT)]
                uutr = [uu.s(t_) for t_ in range(NTT)]
                dst = hsum if dr == 0 else hdir
                for si, (t0, T, isctx, sidx) in enumerate(SEQS):
                    sl = slice(t0, t0 + T)
                    init = 0.0 if isctx else pr[:, 52 + dr * 4 + cch:53 + dr * 4 + cch]
                    if dr == 0:
                        P.dve(lambda e, sl=sl, init=init, dst=dst: e.tensor_tensor_scan(
                            out=dst[:, sl], data0=aa[:, sl], data1=uu[:, sl], initial=init, op0=ALU.mult, op1=ALU.add),
                            r=aatr + uutr + [pr], w=[dst])
                        fcol = t0 + T - 1
                    else:
                        P.dve(lambda e, t0=t0, T=T, init=init, dst=dst: e.tensor_tensor_scan(
                            out=dst[:, t0:t0 + T][:, ::-1], data0=aa[:, t0:t0 + T][:, ::-1], data1=uu[:, t0:t0 + T][:, ::-1],
                            initial=init, op0=ALU.mult, op1=ALU.add), r=aatr + uutr + [pr], w=[dst])
                        fcol = t0
                    if isctx:
                        P.dma("sp", nlru_d[sidx, l, dr, cch * 128:(cch + 1) * 128].rearrange("(p o) -> p o", o=1),
                              dst[:, fcol:fcol + 1], r=[dst], w=[nlru_d.s((sidx, l, dr, cch))])
            tt("dve", hsum[:], hsum[:], hdir[:], ALU.add, [hsum, hdir], [hsum])
            ob = ob_r.next()
            tt("dve", ob[:], hsum[:], gy[:], ALU.mult, [hsum, gy] + [gy.s(t_) for t_ in range(NTT)], [ob])
            P.dma("sp", oT[0, cch * 128:(cch + 1) * 128, :], ob[:], r=[ob], w=[oT.s((0, t_)) for t_ in range(NTT)])
        A.release(m)

    def attn_dense(qT, kT, V, nq_heads, kv_of, qcols, kchunks, out_n, out_head0, qtr, ktr, vtr):
        q0, QT = qcols
        m = A.mark()
        pt_r = A.ring("at_pt", [QT], BF16, 5)
        dr_r = A.ring("at_dr", [QT], F32, 2)
        rc_r = A.ring("at_rc", [QT], F32, 2, parts=64)
        ob_r = A.ring("at_ob", [QT], BF16, 2, parts=64)
        cnt = 0
        for h0 in range(0, nq_heads, 4):
            hs = list(range(h0, min(h0 + 4, nq_heads)))
            for wi_ in range(cfg.get("nwarm", 12)):
                mm(banks[4 + wi_ % 4][:, 0:512], ones_bf[:], h_all[:, 0, 0:512], True, True, [ones_bf], [banks[4 + wi_ % 4]])
            items = [(ki, kc, r_, h) for ki, kc in enumerate(kchunks) for r_, h in enumerate(hs)]
            pts = {}
            LA = 3
            for i in range(len(items) + LA):
                if i < len(items):
                    ki, kc, r_, h = items[i]
                    g = kv_of(h)
                    sb_ = banks[4 + cnt % 4]
                    cnt += 1
                    mm(sb_[:, 0:QT], kT[0:64, g, kc * 128:(kc + 1) * 128], qT[0:64, h, q0:q0 + QT], True, True,
                       [ktr, qtr], [sb_])
                    pt = pt_r.next()
                    actf(pt[:], sb_[:, 0:QT], AF.Exp, [sb_], [pt])
                    pts[i] = pt
                if i >= LA:
                    ki, kc, r_, h = items[i - LA]
                    g = kv_of(h)
                    pt = pts.pop(i - LA)
                    mm(banks[r_][0:65, 0:QT], V[:, kc, g, :], pt[:], ki == 0, ki == len(kchunks) - 1, [vtr, pt], [banks[r_]])
            for r_, h in enumerate(hs):
                po = banks[r_]
                dr = dr_r.next()
                cp("act", dr[64:65, :], po[64:65, 0:QT], [po], [dr])
                db_ = banks[4 + r_ % 2]
                mm(db_[0:64, 0:QT], ones_f[64:65, 0:64], dr[64:65, :], True, True, [ones_f, dr], [db_])
                rc = rc_r.next()
                P.dve(lambda e, rc=rc, QT=QT, db_=db_: e.reciprocal(out=rc[:], in_=db_[0:64, 0:QT]), r=[db_], w=[rc])
                ob = ob_r.next()
                tt("dve", ob[:], po[0:64, 0:QT], rc[:], ALU.mult, [po, rc], [ob])
                row0 = (out_head0 + h) * 64
                P.dma("sp", oT[out_n, row0:row0 + 64, q0:q0 + QT], ob[:], r=[ob], w=otr(out_n, q0, q0 + QT))
        A.release(m)

    def transpose_heads(dst, dst_tr, src_ap_fn, nheads, col0, r, scale=None):
        for h0 in range(0, nheads, 4):
            n = min(4, nheads - h0)
            bk = banks[6 + (h0 // 4) % 2]
            for i in range(n):
                tp(bk[0:64, i * 128:(i + 1) * 128], src_ap_fn(h0 + i), ident[:], r + [ident], [bk])
            src = bk[0:64, 0:n * 128].rearrange("p (a b) -> p a b", a=n, b=128)
            if scale is None:
                cp("act", dst[0:64, h0:h0 + n, col0:col0 + 128], src, [bk], [dst_tr])
            else:
                P.act(lambda e, src=src, h0=h0, n=n: e.mul(out=dst[0:64, h0:h0 + n, col0:col0 + 128], in_=src, mul=scale),
                      r=[bk], w=[dst_tr])

    def branch_B(l):
        m = A.mark()
        wb = A.alloc("b_w", [KC, 768], BF16)
        P.dma("pool", wb[:], w_in_ap(l, OFF["bq"], OFF["bq"] + 768), r=[w_in], w=[wb])
        gq = A.alloc("b_gq", [64], F32)
        gk = A.alloc("b_gk", [64], F32)
        P.dma("sp", gq[:], gqa_qn[l].partition_broadcast(128), r=[gqa_qn], w=[gq])
        P.dma("sp", gk[:], gqa_kn[l].partition_broadcast(128), r=[gqa_kn], w=[gk])
        qT = A.alloc("b_qT", [8, NTOK], BF16, parts=64)
        kT = A.alloc("b_kT", [2, NTOK + 512], BF16, parts=64)
        V = A.alloc("b_V", [24, 2, 65], BF16)
        memset("dve", V[:], 1.0, [V])
        m2 = A.mark()
        sq_r = A.ring("b_sq", [640], F32, 2)
        ss_r = A.ring("b_ss", [10], F32, 2)
        qk_r = A.ring("b_qk", [10, 64], F32, 2)
        rt_r = A.ring("b_rt", [10, 64], F32, 2)
        vv_r = A.ring("b_vv", [128], F32, 2)
        cs_r = A.ring("b_cs", [2, 64], F32, 2)
        def b_chunk(tc):
            tsl = slice(tc * 128, (tc + 1) * 128)
            t_ = tc // 4
            p0 = banks[(tc * 2) % 4]
            p1 = banks[(tc * 2 + 1) % 4]
            for k in range(KC):
                mm(p0[:], h_all[:, k, tsl], wb[:, k, 0:512], k == 0, k == KC - 1, [wb, h_all.s(t_)], [p0])
            for k in range(KC):
                mm(p1[:, 0:256], h_all[:, k, tsl], wb[:, k, 512:768], k == 0, k == KC - 1, [wb, h_all.s(t_)], [p1])
            yield
            sq = sq_r.next()
            actf(sq[:, 0:512], p0[:], AF.Square, [p0], [sq])
            actf(sq[:, 512:640], p1[:, 0:128], AF.Square, [p1], [sq])
            yield
            ss = ss_r.next()
            P.dve(lambda e, ss=ss, sq=sq: e.tensor_reduce(out=ss[:], in_=sq[:].rearrange("p (h d) -> p h d", d=64),
                                                          axis=AX.X, op=ALU.add), r=[sq], w=[ss])
            ts("dve", ss[:], ss[:], 1.0 / 64, EPS, ALU.mult, ALU.add, [ss], [ss])
            actf(ss[:], ss[:], AF.Sqrt, [ss], [ss])
            yield
            P.dve(lambda e, ss=ss: e.reciprocal(out=ss[:], in_=ss[:]), r=[ss], w=[ss])
            qk = qk_r.next()
            tt("dve", qk[:, 0:8, :], p0[:].rearrange("p (h d) -> p h d", d=64),
               ss[:, 0:8].unsqueeze(2).broadcast_to([128, 8, 64]), ALU.mult, [p0, ss], [qk])
            tt("dve", qk[:, 8:10, :], p1[:, 0:128].rearrange("p (h d) -> p h d", d=64),
               ss[:, 8:10].unsqueeze(2).broadcast_to([128, 2, 64]), ALU.mult, [p1, ss], [qk])
            tt("dve", qk[:, 0:8, :], qk[:, 0:8, :], gq[:].unsqueeze(1).broadcast_to([128, 8, 64]), ALU.mult, [qk, gq], [qk])
            tt("dve", qk[:, 8:10, :], qk[:, 8:10, :], gk[:].unsqueeze(1).broadcast_to([128, 2, 64]), ALU.mult, [qk, gk], [qk])
            vv = vv_r.next()
            cp("act", vv[:], p1[:, 128:256], [p1], [vv])
            cp("dve", V[:, tc, :, 0:64], vv[:].rearrange("p (g d) -> p g d", d=64), [vv], [V])
            yield
            if tc < 4:
                sidx = tc // 2
                r0 = (tc % 2) * 128
                P.dma("sp", nak_d[sidx, l, r0:r0 + 128].rearrange("t g d -> t (g d)"), qk[:, 8:10, :].rearrange("p g d -> p (g d)"),
                      r=[qk], w=[nak_d.s((tc, l))])
                P.dma("sp", nav_d[sidx, l, r0:r0 + 128].rearrange("t g d -> t (g d)"), vv[:], r=[vv], w=[nav_d.s((tc, l))])
                src = qk
            else:
                cst = cs_r.next()
                P.dma("sp", cst[:], rope_cs[(tc - 4) * 128:(tc - 3) * 128], r=[rope_cs], w=[cst])
                rt = rt_r.next()
                sw = qk[:].rearrange("p h (a b c) -> p h a b c", a=2, b=2, c=16)[:, :, :, ::-1, :]
                tt("dve", rt[:].rearrange("p h (a b c) -> p h a b c", a=2, b=2, c=16), sw,
                   cst[:, 1, :].rearrange("p (a b c) -> p a b c", a=2, b=2, c=16).unsqueeze(1).broadcast_to([128, 10, 2, 2, 16]),
                   ALU.mult, [qk, cst], [rt])
                tt("dve", qk[:], qk[:], cst[:, 0, :].unsqueeze(1).broadcast_to([128, 10, 64]), ALU.mult, [qk, cst], [qk])
                tt("dve", rt[:], rt[:], qk[:], ALU.add, [rt, qk], [rt])
                src = rt
            yield
            transpose_heads(qT, qT.tr, lambda h, src=src: src[:, h, :], 8, tc * 128, [src], scale=0.125)
            transpose_heads(kT, kT.tr, lambda h, src=src: src[:, 8 + h, :], 2, tc * 128, [src])
        for tc0 in range(0, 20, 2):
            run_rr([b_chunk(tc0), b_chunk(tc0 + 1)])
        ck_r = A.ring("b_ck", [2, 64], F32, 2)
        for i in range(4):
            ck = ck_r.next()
            P.dma("sp", ck[:], cak_d[l, i * 128:(i + 1) * 128], r=[cak_d], w=[ck])
            transpose_heads(kT, kT.tr, lambda h, ck=ck: ck[:, h, :], 2, NTOK + i * 128, [ck])
            cv = ck_r.next()
            P.dma("sp", cv[:], cav_d[l, i * 128:(i + 1) * 128], r=[cav_d], w=[cv])
            cp("dve", V[:, 20 + i, :, 0:64], cv[:], [cv], [V])
        A.release(m2)
        for (t0, T, isctx, sidx) in SEQS:
            if isctx:
                attn_dense(qT, kT, V, 8, lambda h: h // 4, (t0, T), [t0 // 128, t0 // 128 + 1], 1, 0, qT.tr, kT.tr, V.tr)
            else:
                for q0 in range(t0, t0 + T, 512):
                    attn_dense(qT, kT, V, 8, lambda h: h // 4, (q0, 512), list(range(4, 24)), 1, 0, qT.tr, kT.tr, V.tr)
        A.release(m)

    def branch_C(l):
        m = A.mark()
        nmask = A.alloc("c_mask", [64], F32)
        P.dma("sp", nmask[:], namask[:], r=[namask], w=[nmask])
        for hg in range(2):
            mh = A.mark()
            wc = A.alloc("c_w", [KC, 768], BF16)
            for i, nm in enumerate(("cq", "ck", "cv")):
                c0 = OFF[nm] + hg * 256
                P.dma("pool", wc[:, :, i * 256:(i + 1) * 256], w_in_ap(l, c0, c0 + 256), r=[w_in], w=[wc])
            qT = A.alloc("c_qT", [4, NTOK], BF16, parts=64)
            kT = A.alloc("c_kT", [4, NTOK + 512], BF16, parts=64)
            V = A.alloc("c_V", [24, 4, 65], BF16)
            memset("dve", V[:], 1.0, [V])
            bias = A.alloc("c_bias", [14, 4, 64], F32)
            for a_ in range(14):
                P.dma("sp", bias[:, a_, :, :], rpbT[l, a_, :, hg * 4:(hg + 1) * 4, :], r=[rpbT], w=[bias])
            tt("dve", bias[:].rearrange("p a h q -> p (a h) q"), bias[:].rearrange("p a h q -> p (a h) q"),
               nmask[:].unsqueeze(1).broadcast_to([128, 56, 64]), ALU.add, [bias, nmask], [bias])
            m2 = A.mark()
            qk_r = A.ring("c_qk", [768], F32, 2)
            def c_chunk(tc):
                tsl = slice(tc * 128, (tc + 1) * 128)
                t_ = tc // 4
                p0 = banks[(tc * 2) % 4]
                p1 = banks[(tc * 2 + 1) % 4]
                for k in range(KC):
                    mm(p0[:], h_all[:, k, tsl], wc[:, k, 0:512], k == 0, k == KC - 1, [wc, h_all.s(t_)], [p0])
                for k in range(KC):
                    mm(p1[:, 0:256], h_all[:, k, tsl], wc[:, k, 512:768], k == 0, k == KC - 1, [wc, h_all.s(t_)], [p1])
                yield
                qk = qk_r.next()
                cp("act", qk[:, 0:512], p0[:], [p0], [qk])
                cp("dve", qk[:, 512:768], p1[:, 0:256], [p1], [qk])
                cp("dve", V[:, tc, :, 0:64], qk[:, 512:768].rearrange("p (g d) -> p g d", d=64), [qk], [V])
                yield
                if tc < 4:
                    sidx = tc // 2
                    r0 = (tc % 2) * 128
                    P.dma("sp", nnk_d[sidx, l, r0:r0 + 128, hg * 4:(hg + 1) * 4, :].rearrange("t g d -> t (g d)"), qk[:, 256:512],
                          r=[qk], w=[nnk_d.s((tc, l, hg))])
                    P.dma("sp", nnv_d[sidx, l, r0:r0 + 128, hg * 4:(hg + 1) * 4, :].rearrange("t g d -> t (g d)"), qk[:, 512:768],
                          r=[qk], w=[nnv_d.s((tc, l, hg))])
                transpose_heads(qT, qT.tr, lambda h, qk=qk: qk[:, h * 64:(h + 1) * 64], 4, tc * 128, [qk], scale=0.125)
                transpose_heads(kT, kT.tr, lambda h, qk=qk: qk[:, 256 + h * 64:256 + (h + 1) * 64], 4, tc * 128, [qk])
            for tc0 in range(0, 20, 2):
                run_rr([c_chunk(tc0), c_chunk(tc0 + 1)])
            ck_r = A.ring("c_ck", [4, 64], F32, 2)
            for i in range(4):
                ck = ck_r.next()
                P.dma("sp", ck[:], cnk_d[l, i * 128:(i + 1) * 128, hg * 4:(hg + 1) * 4, :], r=[cnk_d], w=[ck])
                transpose_heads(kT, kT.tr, lambda h, ck=ck: ck[:, h, :], 4, NTOK + i * 128, [ck])
                cv = ck_r.next()
                P.dma("sp", cv[:], cnv_d[l, i * 128:(i + 1) * 128, hg * 4:(hg + 1) * 4, :], r=[cnv_d], w=[cv])
                cp("dve", V[:, 20 + i, :, 0:64], cv[:], [cv], [V])
            A.release(m2)
            for (t0, T, isctx, sidx) in SEQS:
                if isctx:
                    attn_dense(qT, kT, V, 4, lambda h: h, (t0, T), [t0 // 128, t0 // 128 + 1], 2, hg * 4, qT.tr, kT.tr, V.tr)
            m3 = A.mark()
            sc_r = A.ring("n_sc", [256], F32, 4)
            pt_r = A.ring("n_pt", [256], BF16, 5)
            dr_r = A.ring("n_dr", [256], F32, 2)
            rc_r = A.ring("n_rc", [256], F32, 2, parts=64)
            ob_r = A.ring("n_ob", [4, 512], BF16, 2, parts=64)
            ob = None
            cnt = 0
            work = []
            for r in range(32):
                rs = min(max(r - 4, 0), 24)
                units = []
                e = (rs // 2) * 2
                while e <= rs + 7:
                    units.append((4 + e // 2, e >= rs, e + 1 <= rs + 7, e - r + 7))
                    e += 2
                for i in range(4):
                    units.append((20 + i, True, True, None))
                for ui, u in enumerate(units):
                    work.append((r, ui, len(units), u))
            LA = cfg.get("na_la", 3)
            pts = {}
            for wi in range(len(work) + LA):
                if wi < len(work):
                    r, ui, nu, (kc, lo, hi, d0) = work[wi]
                    q0 = 512 + r * 64
                    if ui == 0 and r % cfg.get("na_warm_every", 4) == 0:
                        pe_warm(cfg.get("nwarm_na", 0))
                    sb_ = banks[2 + cnt % 4]
                    cnt += 1
                    for h in range(4):
                        mm(sb_[:, h * 64:(h + 1) * 64], kT[0:64, h, kc * 128:(kc + 1) * 128], qT[0:64, h, q0:q0 + 64], True, True,
                           [kT, qT], [sb_])
                    pt = pt_r.next()
                    if d0 is not None:
                        sc = sc_r.next()
                        tt("dve", sc[:], sb_[:, 0:256], bias[:, d0, :, :].rearrange("p h q -> p (h q)"), ALU.add, [sb_, bias], [sc])
                        actf(pt[:], sc[:], AF.Exp, [sc], [pt])
                    else:
                        actf(pt[:], sb_[:, 0:256], AF.Exp, [sb_], [pt])
                    if not lo:
                        memset("dve", pt[0:64, :], 0.0, [pt])
                    if not hi:
                        memset("dve", pt[64:128, :], 0.0, [pt])
                    pts[wi] = pt
                if wi >= LA:
                    r, ui, nu, (kc, lo, hi, d0) = work[wi - LA]
                    pt = pts.pop(wi - LA)
                    po = banks[r % 2]
                    for h in range(4):
                        mm(po[0:65, h * 64:(h + 1) * 64], V[:, kc, h, :], pt[:, h * 64:(h + 1) * 64], ui == 0 and h == 0,
                           ui == nu - 1, [V, pt], [po], skip_group_check=True)
                    if ui == nu - 1:
                        dr = dr_r.next()
                        cp("act", dr[64:65, :], po[64:65, 0:256], [po], [dr])
                        mm(banks[6][0:64, 0:256], ones_f[64:65, 0:64], dr[64:65, :], True, True, [ones_f, dr], [banks[6]])
                        rc = rc_r.next()
                        P.dve(lambda e, rc=rc: e.reciprocal(out=rc[:], in_=banks[6][0:64, 0:256]), r=[banks[6]], w=[rc])
                        if r % 8 == 0:
                            ob = ob_r.next()
                        tt("dve", ob[:, :, (r % 8) * 64:(r % 8 + 1) * 64], po[0:64, 0:256].rearrange("p (h q) -> p h q", q=64),
                           rc[:].rearrange("p (h q) -> p h q", q=64), ALU.mult, [po, rc], [ob])
                        if r % 8 == 7:
                            c0 = 512 + (r - 7) * 64
                            P.dma("sp", oT[2, hg * 256:(hg + 1) * 256, c0:c0 + 512].rearrange("(h d) t -> d h t", d=64), ob[:],
                                  r=[ob], w=otr(2, c0, c0 + 512))
            A.release(m3)
            A.release(mh)
        A.release(m)

    def pair_copy(dst, dst_tr, src, r, parts=128, engs=("act", "dve")):
        for (t0, T, _, _) in SEQS:
            n = T // 64
            d = dst[0:parts, 2 * t0:2 * t0 + 2 * T].rearrange("p (n a c) -> p n a c", n=n, a=2, c=64)
            s_ = src[0:parts, t0:t0 + T].rearrange("p (n c) -> p n c", c=64)
            cp(engs[0], d[:, :, 0, :], s_, r, [dst_tr])
            cp(engs[1], d[:, :, 1, :], s_[:, ::-1, :], r, [dst_tr])

    STEP0 = [0, 4, 8]

    def branch_D(l):
        m = A.mark()
        msk = A.alloc("d_msk", [6, 128], F32)
        P.dma("sp", msk[:], dnm.t.rearrange("a p q -> p a q"), r=[dnm], w=[msk])
        CM, NEGCM, POSSMT, BMk, HMf, HMb = (msk[:, i, :] for i in range(6))
        pr = A.alloc("d_par", [52], F32)
        load_rowsT(pr[:, 0:48], pr, dn_cw[l], dn_cw, 48, banks[7])
        load_rowsT(pr[:, 48:49], pr, dn_ng[l:l + 1, :], dn_ng, 1, banks[7])
        BG = A.alloc("d_BG", [40, 10], F32)
        memset("dve", BG[:], 0.0, [BG])
        m0_ = A.mark()
        p8 = A.alloc("d_p8", [4], F32)
        load_rowsT(p8[0:8, 0:1], p8, dn_alog[l:l + 1, :], dn_alog, 1, banks[7], C=8)
        load_rowsT(p8[0:8, 1:2], p8, dn_dtb[l:l + 1, :], dn_dtb, 1, banks[7], C=8)
        actf(p8[0:8, 2:3], p8[0:8, 0:1], AF.Exp, [p8], [p8])
        ts("dve", p8[0:8, 2:3], p8[0:8, 2:3], -1.0, None, ALU.mult, None, [p8], [p8])
        wd = A.alloc("d_wbg", [KC, 16], BF16)
        P.dma("pool", wd[:], w_in_ap(l, OFF["db"], OFF["db"] + 16), r=[w_in], w=[wd])
        bet = A.alloc("d_bet", [NTOK], F32)
        gg = A.alloc("d_gg", [NTOK], F32)
        for t_ in range(NTT):
            cs = slice(t_ * TT, (t_ + 1) * TT)
            pb = banks[(t_ * 2) % 4]
            pa = banks[(t_ * 2 + 1) % 4]
            for k in range(KC):
                mm(pb[0:8, :], wd[:, k, 0:8], h_all[:, k, cs], k == 0, k == KC - 1, [wd, h_all.s(t_)], [pb])
            for k in range(KC):
                mm(pa[0:8, :], wd[:, k, 8:16], h_all[:, k, cs], k == 0, k == KC - 1, [wd, h_all.s(t_)], [pa])
            actf(bet[0:8, cs], pb[0:8, :], AF.Sigmoid, [pb], [bet])
            actf(gg[0:8, cs], pa[0:8, :], AF.Exp, [pa, p8], [gg], bias=p8[0:8, 1:2])
            actf(gg[0:8, cs], gg[0:8, cs], AF.Ln, [gg], [gg], bias=1.0)
            ts("dve", gg[0:8, cs], gg[0:8, cs], p8[0:8, 2:3], None, ALU.mult, None, [gg, p8], [gg])
        betP = A.alloc("d_betP", [2 * NTOK], F32)
        ggP = A.alloc("d_ggP", [2 * NTOK], F32)
        pair_copy(betP, betP.tr, bet, [bet], parts=8)
        pair_copy(ggP, ggP.tr, gg, [gg], parts=8)
        for s8 in range(5):
            bk = banks[4 + s8 % 2]
            for i in range(8):
                sg = s8 * 8 + i
                tp(bk[:, i * 16:i * 16 + 8], betP[0:8, sg * 128:(sg + 1) * 128], ident[0:8, 0:8], [betP, ident], [bk])
                tp(bk[:, i * 16 + 8:i * 16 + 16], ggP[0:8, sg * 128:(sg + 1) * 128], ident[0:8, 0:8], [ggP, ident], [bk])
            v = bk[:, 0:128].rearrange("p (s c) -> p s c", c=16)
            d = BG[:, s8 * 8:(s8 + 1) * 8, 0:8]
            cp("act", d[0:64, :, 0:4], v[0:64, :, 0:4], [bk], [BG])
            cp("dve", d[64:128, :, 0:4], v[64:128, :, 4:8], [bk], [BG])
            cp("act", d[0:64, :, 4:8], v[0:64, :, 8:12], [bk], [BG])
            cp("dve", d[64:128, :, 4:8], v[64:128, :, 12:16], [bk], [BG])
        A.release(m0_)
        if cfg.get("d_stop") == 0:
            A.release(m)
            return
        for hd in range(cfg.get("d_heads", 4)):
            mh = A.mark()
            qP = A.alloc("d_qP", [2 * NTOK], F32)
            kP = A.alloc("d_kP", [2 * NTOK], F32)
            vP = A.alloc("d_vP", [2 * NTOK], F32)
            OTp = A.alloc("d_OTp", [2 * NTOK], F32)
            raw = A.alloc("d_raw", [PADW], F32)
            zero_pads(raw, raw.tr)
            cv = A.alloc("d_cv", [NTOK], F32)
            tq_r = A.ring("d_tq", [TT], F32, 2)
            m_w = A.mark()
            w4 = A.alloc("d_w4", [KC, 4, 128], BF16)
            for i, nm in enumerate(("dq", "dk", "dv", "dz")):
                c0 = OFF[nm] + hd * 128
                P.dma("pool", w4[:, :, i, :], w_in_ap(l, c0, c0 + 128), r=[w_in], w=[w4])
            for i, dstP in enumerate((qP, kP, vP)):
                for t_ in range(NTT):
                    cs = slice(t_ * TT, (t_ + 1) * TT)
                    px = banks[t_ % 4]
                    for k in range(KC):
                        mm(px[:], w4[:, k, i, :], h_all[:, k, cs], k == 0, k == KC - 1, [w4, h_all.s(t_)], [px])
                    evac_padded(raw, raw.tr, px, t_, "act")
                c12 = i * 4 + hd
                conv4(lambda si: cv[:, SEQS[si][0]:SEQS[si][0] + SEQS[si][1]], raw, raw.tr,
                      lambda j, c12=c12: pr[:, j * 12 + c12:j * 12 + c12 + 1], None, [pr], cv.tr)
                actf(cv[:], cv[:], AF.Silu, [cv], [cv])
                if i < 2:
                    for t_ in range(NTT):
                        cs = slice(t_ * TT, (t_ + 1) * TT)
                        tq = tq_r.next()
                        actf(tq[:], cv[:, cs], AF.Square, [cv], [tq])
                        bk = banks[4 + t_ % 2]
                        mm(bk[:], ones_f[:], tq[:], True, True, [ones_f, tq], [bk])
                        ts("dve", tq[:], bk[:], EPS, None, ALU.add, None, [bk], [tq])
                        actf(tq[:], tq[:], AF.Sqrt, [tq], [tq])
                        P.dve(lambda e, tq=tq: e.reciprocal(out=tq[:], in_=tq[:]), r=[tq], w=[tq])
                        if i == 0:
                            stt("dve", cv[:, cs], cv[:, cs], 128.0 ** -0.5, tq[:], ALU.mult, ALU.mult, [cv, tq], [cv])
                        else:
                            tt("dve", cv[:, cs], cv[:, cs], tq[:], ALU.mult, [cv, tq], [cv])
                pair_copy(dstP, dstP.tr, cv, [cv])
            for t_ in range(NTT):
                cs = slice(t_ * TT, (t_ + 1) * TT)
                px = banks[t_ % 4]
                for k in range(KC):
                    mm(px[:], w4[:, k, 3, :], h_all[:, k, cs], k == 0, k == KC - 1, [w4, h_all.s(t_)], [px])
                actf(cv[:, cs], px[:], AF.Silu, [px], [cv])
            zs = cv
            A.release(m_w)
            if cfg.get("d_stop") == 1:
                A.release(mh)
                continue
            G = cfg.get("d_G", 4)
            Sf = A.alloc("d_Sf", [128], F32)
            Sb = A.alloc("d_Sb", [128], F32)
            slots = []
            for gslot in range(G):
                sl = {"tmp": A.ring(f"d_tmp{gslot}_", [128], F32, 4)}
                for nm_, n_ in (("N", 1), ("MT", 2), ("M", 2), ("TT", 2), ("u", 1), ("At", 1), ("vb", 1), ("kbg", 1),
                                ("wTf", 1), ("wTb", 1), ("qgf", 1), ("qgb", 1), ("kdf", 1), ("kdb", 1)):
                    sl[nm_] = A.ring(f"d_{nm_}{gslot}_", [128], F32, n_)
                for nm_ in ("wTf", "wTb", "qgf", "qgb", "kdf", "kdb"):
                    for tl in sl[nm_].tiles:
                        memset("pool", tl[:], 0.0, [tl])
                sl["gcs"] = A.ring(f"d_gcs{gslot}_", [8], F32, 1)
                sl["nb"] = A.ring(f"d_nb{gslot}_", [2], F32, 1)
                slots.append(sl)
            vn_r = A.ring("d_vnew", [128], F32, 2)
            bc = [0]

            def nb_():
                b = banks[bc[0] % 8]
                bc[0] += 1
                return b

            def pre_gen(sl, ctx, si, s, ci):
                bX, bY = banks[2 * ci], banks[2 * ci + 1]
                t0, T, isctx, sidx = SEQS[si]
                sg = STEP0[si] + s
                pc = slice(2 * t0 + s * 128, 2 * t0 + (s + 1) * 128)
                kS, qS, vS = kP[:, pc], qP[:, pc], vP[:, pc]
                bcol = BG[:, sg, hd:hd + 1]
                gcol = BG[:, sg, 4 + hd:5 + hd]
                gcol2 = BG[:, sg, 4 + hd:6 + hd]
                tmp = sl["tmp"]
                Gb = tmp.next()
                ts("pool", Gb[:], ones_f[:], gcol, None, ALU.mult, None, [ones_f, BG], [Gb])
                nb = sl["nb"].next()
                ts("pool", nb[:, 0:1], bcol, -1.0, None, ALU.mult, None, [BG], [nb])
                mm(bX[:, 0:128], kS, kS, True, True, [kP], [bX])
                mm(bX[:, 128:256], kS, qS, True, True, [kP, qP], [bX])
                tp(bX[:, 256:384], kS, ident[:], [kP, ident], [bX])
                tp(bX[:, 384:512], vS, ident[:], [vP, ident], [bX])
                yield
                mm(bY[:, 0:128], Gb[:], CM, True, True, [Gb, msk], [bY])
                mm(bY[:, 128:130], CM, gcol2, True, True, [msk, BG], [bY])
                mm(bY[:, 130:132], HMf, gcol2, True, True, [msk, BG], [bY])
                mm(bY[:, 132:134], HMb, gcol2, True, True, [msk, BG], [bY])
                mm(bY[:, 134:136], BMk, gcol2, True, True, [msk, BG], [bY])
                vb = sl["vb"].next()
                ts("dve", vb[:], bX[:, 384:512], bcol, None, ALU.mult, None, [bX, BG], [vb])
                yield
                gcs = sl["gcs"].next()
                cp("dve", gcs[:, 0:4], bY[:, 128:136].rearrange("p (a b) -> p a b", b=2)[:, :, 0], [bY], [gcs])
                t1 = tmp.next()
                stt("dve", t1[:], bY[:, 0:128], gcs[:, 0:1], NEGCM, ALU.subtract, ALU.add, [bY, gcs, msk], [t1])
                t2 = tmp.next()
                stt("dve", t2[:], bY[:, 0:128], gcs[:, 0:1], POSSMT, ALU.subtract, ALU.add, [bY, gcs, msk], [t2])
                ER = tmp.next()
                actf(ER[:], bY[:, 0:128], AF.Exp, [bY], [ER])
                yield
                actf(gcs[:, 4:7], gcs[:, 0:3], AF.Exp, [gcs], [gcs])
                actf(gcs[:, 7:8], gcs[:, 0:1], AF.Exp, [gcs], [gcs], bias=gcs[:, 3:4], scale=-1.0)
                actf(t1[:], t1[:], AF.Exp, [t1], [t1])
                actf(t2[:], t2[:], AF.Exp, [t2], [t2], scale=-1.0)
                Dt, Ds = t1, t2
                qgf = sl["qgf"].next()
                qgb = sl["qgb"].next()
                tt("pool", qgf[:, 0:64], qP[:, pc][:, 0:64], ER[:, 0:64], ALU.mult, [qP, ER], [qgf])
                tt("pool", qgb[:, 64:128], qP[:, pc][:, 64:128], ER[:, 64:128], ALU.mult, [qP, ER], [qgb])
                yield
                tt("pool", nb[:, 1:2], bcol, gcs[:, 4:5], ALU.mult, [BG, gcs], [nb])
                Nm = sl["N"].next()
                stt("dve", Nm[:], bX[:, 0:128], nb[:, 0:1], Ds[:], ALU.mult, ALU.mult, [bX, nb, Ds], [Nm])
                At = sl["At"].next()
                tt("dve", At[:], bX[:, 128:256], Dt[:], ALU.mult, [bX, Dt], [At])
                kdf = sl["kdf"].next()
                kdb = sl["kdb"].next()
                ts("dve", kdf[0:64, :], bX[0:64, 256:384], gcs[0:64, 7:8], None, ALU.mult, None, [bX, gcs], [kdf])
                ts("dve", kdb[64:128, :], bX[64:128, 256:384], gcs[64:128, 7:8], None, ALU.mult, None, [bX, gcs], [kdb])
                yield
                tp(bY[:, 0:128], Nm[:], ident[:], [Nm, ident], [bY])
                kbg = sl["kbg"].next()
                ts("dve", kbg[:], bX[:, 256:384], nb[:, 1:2], None, ALU.mult, None, [bX, nb], [kbg])
                yield
                MT = sl["MT"].next()
                cp("act", MT[:], bY[:, 0:128], [bY], [MT])
                TTt = sl["TT"].next()
                tt("dve", TTt[:], bY[:, 0:128], ident[:], ALU.add, [bY, ident], [TTt])
                yield
                M_ = Nm
                IMp = None
                for kk in range(1, 6):
                    mm(bX[:, 0:128], MT[:], M_[:], True, True, [MT, M_], [bX])
                    if kk < 5:
                        mm(bY[:, 0:128], M_[:], MT[:], True, True, [MT, M_], [bY])
                    if IMp is not None:
                        mm(bX[:, 128:256], IMp[:], TTt[:], True, True, [IMp, TTt], [bX])
                    yield
                    IM = tmp.next()
                    tt("dve", IM[:], bX[:, 0:128], ident[:], ALU.add, [bX, ident], [IM])
                    if kk < 5:
                        Mn = sl["M"].next()
                        cp("act", Mn[:], bX[:, 0:128], [bX], [Mn])
                        MTn = sl["MT"].next()
                        cp("act", MTn[:], bY[:, 0:128], [bY], [MTn])
                    if IMp is not None:
                        TTn = sl["TT"].next()
                        cp("act", TTn[:], bX[:, 128:256], [bX], [TTn])
                        TTt = TTn
                    IMp = IM
                    if kk < 5:
                        M_, MT = Mn, MTn
                    yield
                mm(bX[:, 128:256], IMp[:], TTt[:], True, True, [IMp, TTt], [bX])
                yield
                TTn = sl["TT"].next()
                cp("act", TTn[:], bX[:, 128:256], [bX], [TTn])
                TTt = TTn
                yield
                mm(bX[:, 0:128], TTt[:], vb[:], True, True, [TTt, vb], [bX])
                mm(bY[:, 0:128], kbg[:], TTt[:], True, True, [kbg, TTt], [bY])
                yield
                usb = sl["u"].next()
                cp("act", usb[:], bX[:, 0:128], [bX], [usb])
                wTf = sl["wTf"].next()
                wTb = sl["wTb"].next()
                cp("dve", wTf[:, 0:64], bY[:, 0:64], [bY], [wTf])
                cp("act", wTb[:, 64:128], bY[:, 64:128], [bY], [wTb])
                ctx.update(usb=usb, wTf=wTf, wTb=wTb, qgf=qgf, qgb=qgb, At=At, kdf=kdf, kdb=kdb, gcs=gcs, pc=pc)

            def rec(ctx):
                usb, wTf, wTb, qgf, qgb, At, kdf, kdb, gcs, pc = (ctx[k_] for k_ in
                                                                   ("usb", "wTf", "wTb", "qgf", "qgb", "At", "kdf", "kdb", "gcs", "pc"))
                bV = nb_()
                mm(bV[:, 0:128], wTf[:], Sf[:], True, False, [wTf, Sf], [bV])
                mm(bV[:, 0:128], wTb[:], Sb[:], False, True, [wTb, Sb], [bV])
                vnew = vn_r.next()
                tt("dve", vnew[:], usb[:], bV[:, 0:128], ALU.subtract, [usb, bV], [vnew])
                bO = nb_()
                mm(bO[:, 0:128], Sf[:], qgf[:], True, False, [Sf, qgf], [bO])
                mm(bO[:, 0:128], Sb[:], qgb[:], False, False, [Sb, qgb], [bO])
                mm(bO[:, 0:128], vnew[:], At[:], False, True, [vnew, At], [bO])
                bS = nb_()
                mm(bS[:, 0:128], kdf[:], vnew[:], True, True, [kdf, vnew], [bS])
                mm(bS[:, 128:256], kdb[:], vnew[:], True, True, [kdb, vnew], [bS])
                cp("act", OTp[:, pc], bO[:, 0:128], [bO], [OTp])
                stt("dve", Sf[:], Sf[:], gcs[:, 5:6], bS[:, 0:128], ALU.mult, ALU.add, [Sf, gcs, bS], [Sf])
                stt("dve", Sb[:], Sb[:], gcs[:, 6:7], bS[:, 128:256], ALU.mult, ALU.add, [Sb, gcs, bS], [Sb])

            for si, (t0, T, isctx, sidx) in enumerate(SEQS):
                n = T // 64
                if isctx:
                    memset("dve", Sf[:], 0.0, [Sf])
                    memset("dve", Sb[:], 0.0, [Sb])
                else:
                    P.dma("sp", Sf[:], sdn_d[l, 0, hd], r=[sdn_d], w=[Sf])
                    P.dma("sp", Sb[:], sdn_d[l, 1, hd], r=[sdn_d], w=[Sb])
                nst = min(n, cfg.get("d_nsteps", 99))
                for g0 in range(0, nst, G):
                    pe_warm(cfg.get("nwarm_d", 0))
                    ss = list(range(g0, min(g0 + G, nst)))
                    ctxs = [dict() for _ in ss]
                    alive = [pre_gen(slots[i], ctxs[i], si, s_, i) for i, s_ in enumerate(ss)]
                    while alive:
                        for g_ in list(alive):
                            try:
                                next(g_)
                            except StopIteration:
                                alive.remove(g_)
                    for c_ in ctxs:
                        rec(c_)
                if isctx:
                    P.dma("sp", ndn_d[sidx, l, 0, hd], Sf[:], r=[Sf], w=[ndn_d.s((sidx, l, 0, hd))])
                    P.dma("sp", ndn_d[sidx, l, 1, hd], Sb[:], r=[Sb], w=[ndn_d.s((sidx, l, 1, hd))])
            osum = raw
            for (t0, T, _, _) in SEQS:
                n = T // 64
                v = OTp[:, 2 * t0:2 * t0 + 2 * T].rearrange("p (n a c) -> p n a c", n=n, a=2, c=64)
                tt("dve", osum[:, t0:t0 + T].rearrange("p (n c) -> p n c", c=64), v[:, :, 0, :], v[:, ::-1, 1, :], ALU.add,
                   [OTp], [raw])
            ob_r = A.ring("d_ob", [TT], BF16, 2)
            for t_ in range(NTT):
                cs = slice(t_ * TT, (t_ + 1) * TT)
                tq = tq_r.next()
                actf(tq[:], osum[:, cs], AF.Square, [raw], [tq])
                bk = banks[4 + t_ % 2]
                mm(bk[:], ones_f[:], tq[:], True, True, [ones_f, tq], [bk])
                ts("dve", tq[:], bk[:], 1.0 / 128, EPS, ALU.mult, ALU.add, [bk], [tq])
                actf(tq[:], tq[:], AF.Sqrt, [tq], [tq])
                P.dve(lambda e, tq=tq: e.reciprocal(out=tq[:], in_=tq[:]), r=[tq], w=[tq])
                stt("dve", tq[:], osum[:, cs], pr[:, 48:49], tq[:], ALU.mult, ALU.mult, [raw, pr, tq], [tq])
                ob = ob_r.next()
                tt("dve", ob[:], tq[:], zs[:, cs], ALU.mult, [tq, cv], [ob])
                P.dma("sp", oT[3, hd * 128:(hd + 1) * 128, cs], ob[:], r=[ob], w=[oT.s((3, t_))])
            A.release(mh)
        A.release(m)

    def mixer(l, next_norm=None):
        for n, (nm, fn) in enumerate((("A", branch_A), ("B", branch_B), ("C", branch_C), ("D", branch_D))):
            if nm in branches:
                fn(l)
            else:
                zero_branch(n)
        merge_stage(l, next_norm)

    for l in range(nl):
        last = (l == nl - 1)
        if do_ffn:
            if l == 0:
                norm_stage(l, 0)
            ffn_stage(l, 0, 2, next_norm=((l, 1) if do_mixer else (l, 2)))
        elif do_mixer:
            norm_stage(l, 1)
        if do_mixer:
            mixer(l, next_norm=((l, 2) if do_ffn else None))
        if do_ffn:
            ffn_stage(l, 1, 8, next_norm=(None if last else (l + 1, 0)))
    final_stage()
    nc = P.finish()
    return nc, P, A


def _consts():
    half = 32
    inv = np.power(10000.0, -np.arange(0, half, 2, dtype=np.float32) / half).astype(np.float32)
    pos = np.arange(2048)
    ang_r = (pos // 64).astype(np.float32)[:, None] * inv[None, :]
    ang_c = (pos % 64).astype(np.float32)[:, None] * inv[None, :]
    cs = np.zeros((2048, 2, 64), np.float32)
    for (o, ang) in ((0, ang_r), (32, ang_c)):
        c = np.cos(ang).astype(np.float32)
        s = np.sin(ang).astype(np.float32)
        cs[:, 0, o:o + 16] = c
        cs[:, 0, o + 16:o + 32] = c
        cs[:, 1, o:o + 16] = -s
        cs[:, 1, o + 16:o + 32] = s
    kc = np.arange(64)[:, None]
    qc = np.arange(64)[None, :]
    cstart = np.clip(qc - 8, 0, 48)
    inwin = (kc >= cstart) & (kc < cstart + 16)
    mask = np.where(inwin, 0.0, NEG).astype(np.float32)
    mask = np.concatenate([mask, mask], axis=0)
    return cs, mask


def _dnmasks():
    p = np.arange(128)
    k = p[:, None]
    i = p[None, :]
    same = (k // 64) == (i // 64)
    fwd = (k < 64)
    cm = same & np.where(fwd, k <= i, k >= i)
    smt = same & np.where(fwd, i < k, i > k)
    out = np.zeros((6, 128, 128), np.float32)
    out[0] = cm
    out[1] = np.where(cm, 0.0, NEG)
    out[2] = np.where(smt, 0.0, -NEG)
    out[3] = same
    out[4] = np.broadcast_to(fwd, (128, 128))
    out[5] = np.broadcast_to(~fwd, (128, 128))
    return out


def _rpb_layout(na_rpb):
    L = na_rpb.shape[0]
    kc = np.arange(64)[:, None]
    qc = np.arange(64)[None, :]
    cidx = np.clip(kc - qc + 15, 0, 30)
    out = np.empty((L, 14, 2, 64, 8, 64), np.float32)
    for d0 in range(14):
        for j in range(2):
            row = d0 + j
            blk = na_rpb[:, :, row, :][:, :, cidx]
            out[:, d0, j] = np.transpose(blk, (0, 2, 1, 3))
    return np.ascontiguousarray(out.reshape(L, 14, 128, 8, 64))


def make_in_maps(inp, ncores=8, ld=DEPTH):
    f = lambda a: np.ascontiguousarray(np.asarray(a, dtype=np.float32))
    cs, mask = _consts()
    shared = {
        "w_mod": f(inp["w_mod"]), "b_mod": f(inp["b_mod"]).reshape(DEPTH, 72, 128),
        "norm_g": f(inp["norm_g"]).reshape(DEPTH, 24, 128),
        "w_ffn_gate": f(inp["w_ffn_gate"]), "w_ffn_up": f(inp["w_ffn_up"]), "w_ffn_down": f(inp["w_ffn_down"]),
        "w_in": f(inp["w_in"]),
        "lru_conv_w": f(inp["lru_conv_w"]).reshape(DEPTH, 16, 128), "lru_conv_b": f(inp["lru_conv_b"]).reshape(DEPTH, 4, 128),
        "lru_w_r": f(inp["lru_w_r"]), "lru_b_r": f(inp["lru_b_r"]).reshape(DEPTH, 8, 128),
        "lru_w_i": f(inp["lru_w_i"]), "lru_b_i": f(inp["lru_b_i"]).reshape(DEPTH, 8, 128),
        "lru_lambda": f(inp["lru_lambda"]).reshape(DEPTH, 8, 128),
        "gqa_q_norm": f(inp["gqa_q_norm"]), "gqa_k_norm": f(inp["gqa_k_norm"]),
        "rpbT": _rpb_layout(f(inp["na_rpb"])), "namask": mask,
        "dn_conv_w": f(inp["dn_conv_w"]).reshape(DEPTH, 48, 128),
        "dn_a_log": f(inp["dn_a_log"]).reshape(DEPTH, 8), "dn_dt_bias": f(inp["dn_dt_bias"]).reshape(DEPTH, 8),
        "dn_norm_g": f(inp["dn_norm_g"]),
        "w_branch": f(inp["w_branch"]), "w_out": f(inp["w_out"]),
        "final_norm_g": f(inp["final_norm_g"]).reshape(8, 128), "rope_cs": cs, "dnmasks": _dnmasks(),
    }
    maps = []
    for c in range(ncores):
        s = c % 4
        m = dict(shared)
        m["xp"] = f(inp["x_prompt"][2 * c:2 * c + 2]).reshape(512, D)
        m["xs"] = f(inp["x_sample"][s])
        m["c2"] = np.stack([f(inp["c_ctx"]), f(inp["c"][s])], axis=0)
        m["cak"] = f(inp["cache_attn_k"][s])
        m["cav"] = f(inp["cache_attn_v"][s])
        m["cnk"] = f(inp["cache_na_k"][s])
        m["cnv"] = f(inp["cache_na_v"][s])
        m["slru"] = f(inp["state_lru"][s])
        m["sdn"] = f(inp["state_delta"][s])
        maps.append(m)
    if ld != DEPTH:
        for m in maps:
            for k in list(m.keys()):
                a = m[k]
                if k in ('cak','cav','cnk','cnv','slru','sdn') or (a.shape[0] == DEPTH and k not in ('xp','xs','c2','namask','rope_cs','final_norm_g','dnmasks')):
                    m[k] = np.ascontiguousarray(a[:ld])
    return maps


_NC_CACHE = {}


def kernel(**inputs):
    if "nc" not in _NC_CACHE:
        _NC_CACHE["nc"] = build({})[0]
    nc = _NC_CACHE["nc"]
    maps = make_in_maps(inputs)
    res = run_bass_kernel_spmd(nc, maps, core_ids=list(range(8)))
    R = res.results
    y_p = np.concatenate([R[c]["y_p"].reshape(2, 256, D) for c in range(8)], axis=0)
    y_s = np.stack([R[c]["y_s"] for c in range(4)], axis=0)
    cat = lambda k: np.concatenate([R[c][k] for c in range(8)], axis=0)
    return (y_p.astype(np.float32), y_s.astype(np.float32), cat("new_ak"), cat("new_av"), cat("new_nk"),
            cat("new_nv"), cat("new_lru"), cat("new_dn"))
```
